# Optimizing a Trainium2 kernel written in Bass

```python
import jax, jax.numpy as jnp
from jax import lax
import numpy as np

D_MODEL = 1024
BATCH = 16
SEQ = 256
DEPTH = 4
DEC_BATCH = 8
DEC_SEQ = 4096
PAST_LEN = 256

GRID_W = 64
N_MIXERS = 2
N_RWKV = (DEPTH + 1) // 2
N_NA = DEPTH // 2
RWKV_HEAD = 64
RWKV_HEADS = D_MODEL // RWKV_HEAD
DECAY_LORA = 64
AAA_LORA = 64
GATE_LORA = 128
NA_HEAD = 64
NA_HEADS = D_MODEL // NA_HEAD
WIN_ROWS = 8
WIN_COLS = 16
Q_COL_BLOCK = 16
K_COL_BAND = 32
N_COL_BLOCKS = GRID_W // Q_COL_BLOCK
D_FF = 4 * D_MODEL
NORM_EPS = 1e-6
GN_EPS = 64e-5
ATTN_SCALE = NA_HEAD ** -0.5
NEG_BIG = -1e30

kernel_name = 'hybrid_rwkv7_natten_diffusion_step'


def rms_norm(x, g):
    x32 = x.astype(jnp.float32)
    y = x32 * lax.rsqrt(jnp.mean(x32 * x32, axis=-1, keepdims=True) + NORM_EPS)
    return (y * g.astype(jnp.float32)).astype(x.dtype)


def ada_mod(cond, w, b):
    m = jax.nn.silu(cond) @ w + b
    return jnp.split(m, 6, axis=-1)


def modulate(x, shift, scale):
    return x * (1 + scale[:, None, :]) + shift[:, None, :]


def sq_relu_mlp(h, w1, w2):
    return jnp.square(jax.nn.relu(h @ w1)) @ w2


def centred_token_shift(x):
    zero = jnp.zeros_like(x[:, :1])
    prev = jnp.concatenate([zero, x[:, :-1]], axis=1)
    nxt = jnp.concatenate([x[:, 1:], zero], axis=1)
    return 0.5 * (prev + nxt)


def delta_rule_scan(r, w, k, v, kk, a, s0, reverse):
    def step(S, inp):
        r_t, w_t, k_t, v_t, kk_t, a_t = inp
        sa = jnp.einsum('bhvk,bhk->bhv', S, -kk_t)
        S_new = (S * w_t[:, :, None, :] + sa[..., None] * (kk_t * a_t)[:, :, None, :]
                 + v_t[..., None] * k_t[:, :, None, :]).astype(S.dtype)
        y = jnp.einsum('bhvk,bhk->bhv', S_new, r_t)
        return S_new, y
    xs = tuple(jnp.moveaxis(t, 1, 0) for t in (r, w, k, v, kk, a))
    S_fin, ys = lax.scan(step, s0, xs, reverse=reverse)
    return jnp.moveaxis(ys, 0, 1), S_fin


def rwkv7_mixer(h, s0, mu, w_rkv, w_o, w0, w1, w2, a0, a1, a2, g1, g2, k_k, k_a, r_k, ln_w, ln_b):
    B, T, D = h.shape
    H, K = RWKV_HEADS, RWKV_HEAD
    delta = centred_token_shift(h) - h
    xs = h[None] + delta[None] * mu[:, None, None, :]
    rkv = jnp.einsum('ibtd,ide->ibte', xs[:3], w_rkv)
    r, k, v = rkv[0], rkv[1], rkv[2]
    xw, xa, xg = xs[3], xs[4], xs[5]
    g = jax.nn.sigmoid(xg @ g1) @ g2
    heads = lambda t: t.reshape(B, T, H, K)
    kk = heads(k * k_k).astype(jnp.float32)
    kk = (kk * lax.rsqrt(jnp.sum(kk * kk, axis=-1, keepdims=True) + 1e-12)).astype(h.dtype)
    r_h, v_h = heads(r), heads(v)
    ys, finals, bonuses = [], [], []
    for d in range(2):
        z = w0[d] + jnp.tanh(xw @ w1[d]) @ w2[d]
        decay = jnp.exp(-jnp.exp(-jax.nn.softplus(-z) - 0.5))
        a = jax.nn.sigmoid(a0[d] + (xa @ a1[d]) @ a2[d])
        k_d = heads(k * (1 + (a - 1) * k_a))
        y_d, s_d = delta_rule_scan(r_h, heads(decay), k_d, v_h, kk, heads(a), s0[:, d], reverse=(d == 1))
        ys.append(y_d)
        finals.append(s_d)
        bonuses.append(jnp.sum(r_h * k_d * r_k, axis=-1, keepdims=True) * v_h)
    y = (ys[0] + ys[1]).astype(jnp.float32)
    mean = jnp.mean(y, axis=-1, keepdims=True)
    var = jnp.mean(jnp.square(y - mean), axis=-1, keepdims=True)
    yn = ((y - mean) * lax.rsqrt(var + GN_EPS)).reshape(B, T, D) * ln_w + ln_b
    o = (yn + (bonuses[0] + bonuses[1]).reshape(B, T, D)) * g
    return o.astype(h.dtype) @ w_o, jnp.stack(finals, axis=1)


def na_qkv(h, w_qkv, q_g, k_g):
    B, T, _ = h.shape
    qkv = (h @ w_qkv).reshape(B, T, 3, NA_HEADS, NA_HEAD)
    q = rms_norm(qkv[:, :, 0], q_g).transpose(0, 2, 1, 3)
    k = rms_norm(qkv[:, :, 1], k_g).transpose(0, 2, 1, 3)
    v = qkv[:, :, 2].transpose(0, 2, 1, 3)
    return q, k, v


def na_context(h, w_qkv, w_o, q_g, k_g):
    B, T, D = h.shape
    q, k, v = na_qkv(h, w_qkv, q_g, k_g)
    s = jnp.einsum('bhqd,bhkd->bhqk', q, k).astype(jnp.float32) * ATTN_SCALE
    p = jax.nn.softmax(s, axis=-1).astype(v.dtype)
    o = jnp.einsum('bhqk,bhkd->bhqd', p, v)
    return o.transpose(0, 2, 1, 3).reshape(B, T, D) @ w_o, k, v


def neighbourhood_tables():
    cb = np.arange(N_COL_BLOCKS)
    band_start = np.clip(cb * Q_COL_BLOCK - WIN_COLS // 2, 0, GRID_W - K_COL_BAND)
    band_idx = band_start[:, None] + np.arange(K_COL_BAND)[None, :]
    q_col = cb[:, None] * Q_COL_BLOCK + np.arange(Q_COL_BLOCK)[None, :]
    win_start = np.clip(q_col - WIN_COLS // 2, 0, GRID_W - WIN_COLS)
    key_col = band_idx[:, None, :]
    valid = (key_col >= win_start[..., None]) & (key_col < win_start[..., None] + WIN_COLS)
    col_off = np.clip(key_col - q_col[..., None] + WIN_COLS - 1, 0, 2 * WIN_COLS - 2)
    return band_idx.astype(np.int32), valid, col_off.astype(np.int32)


def na_latent(h, k_ctx, v_ctx, w_qkv, w_o, q_g, k_g, rpb):
    B, T, D = h.shape
    rows = T // GRID_W
    wr = min(WIN_ROWS, rows)
    q, k, v = na_qkv(h, w_qkv, q_g, k_g)
    grid = lambda t: t.reshape(B, NA_HEADS, rows, GRID_W, NA_HEAD)
    qg, kg, vg = grid(q), grid(k), grid(v)
    band_np, valid_np, col_off_np = neighbourhood_tables()
    band_idx = jnp.asarray(band_np)
    valid = jnp.asarray(valid_np)[:, :, None, :]
    col_off = jnp.asarray(col_off_np)
    n_win = wr * K_COL_BAND

    def row_block(r):
        rs = jnp.clip(r - WIN_ROWS // 2, 0, rows - wr)
        q_r = lax.dynamic_index_in_dim(qg, r, axis=2, keepdims=False).reshape(
            B, NA_HEADS, N_COL_BLOCKS, Q_COL_BLOCK, NA_HEAD)
        k_r = jnp.take(lax.dynamic_slice_in_dim(kg, rs, wr, axis=2), band_idx, axis=3)
        v_r = jnp.take(lax.dynamic_slice_in_dim(vg, rs, wr, axis=2), band_idx, axis=3)
        row_off = rs + jnp.arange(wr) - r + WIN_ROWS - 1
        bias = jnp.take(jnp.take(rpb, row_off, axis=1), col_off, axis=2)
        bias = bias.transpose(0, 2, 3, 1, 4).astype(jnp.float32)
        s_win = jnp.einsum('bhnqd,bhrnkd->bhnqrk', q_r, k_r).astype(jnp.float32) * ATTN_SCALE + bias
        s_win = jnp.where(valid, s_win, NEG_BIG)
        s_ctx = jnp.einsum('bhnqd,bhld->bhnql', q_r, k_ctx).astype(jnp.float32) * ATTN_SCALE
        s = jnp.concatenate([s_win.reshape(B, NA_HEADS, N_COL_BLOCKS, Q_COL_BLOCK, n_win), s_ctx], axis=-1)
        p = jax.nn.softmax(s, axis=-1).astype(v.dtype)
        p_win = p[..., :n_win].reshape(B, NA_HEADS, N_COL_BLOCKS, Q_COL_BLOCK, wr, K_COL_BAND)
        p_ctx = p[..., n_win:]
        o = (jnp.einsum('bhnqrk,bhrnkd->bhnqd', p_win, v_r)
             + jnp.einsum('bhnql,bhld->bhnqd', p_ctx, v_ctx))
        return o.reshape(B, NA_HEADS, GRID_W, NA_HEAD)

    o = lax.map(row_block, jnp.arange(rows))
    o = o.transpose(1, 0, 3, 2, 4).reshape(B, T, D)
    return o @ w_o


def setup_inputs(seed: int = 0) -> dict:
    key = jax.random.key(seed)
    ks = iter(jax.random.split(key, 40))
    nrm = lambda shape, scale: jax.random.normal(next(ks), shape, jnp.float32) * scale
    D, H, K = D_MODEL, RWKV_HEADS, RWKV_HEAD
    return {
        'x_prompt': nrm((BATCH, SEQ, D), 1.0),
        'x_sample': nrm((DEC_BATCH, DEC_SEQ, D), 1.0),
        'state_rwkv': nrm((DEC_BATCH, N_RWKV, 2, H, K, K), 0.5),
        'cache_na_k': nrm((DEC_BATCH, N_NA, NA_HEADS, PAST_LEN, NA_HEAD), 1.0),
        'cache_na_v': nrm((DEC_BATCH, N_NA, NA_HEADS, PAST_LEN, NA_HEAD), 1.0),
        'c': nrm((DEC_BATCH, D), 1.0),
        'c_ctx': nrm((D,), 1.0),
        'norm_g': 1.0 + nrm((DEPTH, 2, D), 0.02),
        'ada_w': nrm((DEPTH, D, 6 * D), 0.3 * D ** -0.5),
        'ada_b': nrm((DEPTH, 6 * D), 0.02),
        'mlp_w1': nrm((DEPTH, D, D_FF), D ** -0.5),
        'mlp_w2': nrm((DEPTH, D_FF, D), D_FF ** -0.5),
        'rwkv_mu': jax.random.uniform(next(ks), (N_RWKV, 6, D), jnp.float32),
        'rwkv_w_rkv': nrm((N_RWKV, 3, D, D), D ** -0.5),
        'rwkv_w_o': nrm((N_RWKV, D, D), D ** -0.5),
        'rwkv_w0': nrm((N_RWKV, 2, D), 0.5),
        'rwkv_w1': nrm((N_RWKV, 2, D, DECAY_LORA), D ** -0.5),
        'rwkv_w2': nrm((N_RWKV, 2, DECAY_LORA, D), 0.3 * DECAY_LORA ** -0.5),
        'rwkv_a0': nrm((N_RWKV, 2, D), 0.1),
        'rwkv_a1': nrm((N_RWKV, 2, D, AAA_LORA), D ** -0.5),
        'rwkv_a2': nrm((N_RWKV, 2, AAA_LORA, D), 0.3 * AAA_LORA ** -0.5),
        'rwkv_g1': nrm((N_RWKV, D, GATE_LORA), D ** -0.5),
        'rwkv_g2': nrm((N_RWKV, GATE_LORA, D), GATE_LORA ** -0.5),
        'rwkv_k_k': 0.85 + nrm((N_RWKV, D), 0.02),
        'rwkv_k_a': 1.0 + nrm((N_RWKV, D), 0.02),
        'rwkv_r_k': nrm((N_RWKV, H, K), 0.1),
        'rwkv_ln_w': 1.0 + nrm((N_RWKV, D), 0.02),
        'rwkv_ln_b': nrm((N_RWKV, D), 0.02),
        'na_w_qkv': nrm((N_NA, D, 3 * D), D ** -0.5),
        'na_w_o': nrm((N_NA, D, D), D ** -0.5),
        'na_q_g': 1.0 + nrm((N_NA, NA_HEAD), 0.02),
        'na_k_g': 1.0 + nrm((N_NA, NA_HEAD), 0.02),
        'na_rpb': nrm((N_NA, NA_HEADS, 2 * WIN_ROWS - 1, 2 * WIN_COLS - 1), 0.1),
    }


def reference(x_prompt, x_sample, state_rwkv, cache_na_k, cache_na_v, c, c_ctx,
              norm_g, ada_w, ada_b, mlp_w1, mlp_w2,
              rwkv_mu, rwkv_w_rkv, rwkv_w_o, rwkv_w0, rwkv_w1, rwkv_w2, rwkv_a0, rwkv_a1, rwkv_a2,
              rwkv_g1, rwkv_g2, rwkv_k_k, rwkv_k_a, rwkv_r_k, rwkv_ln_w, rwkv_ln_b,
              na_w_qkv, na_w_o, na_q_g, na_k_g, na_rpb):
    rwkv_params = (rwkv_mu, rwkv_w_rkv, rwkv_w_o, rwkv_w0, rwkv_w1, rwkv_w2, rwkv_a0, rwkv_a1, rwkv_a2,
                   rwkv_g1, rwkv_g2, rwkv_k_k, rwkv_k_a, rwkv_r_k, rwkv_ln_w, rwkv_ln_b)

    xp = x_prompt
    bp = x_prompt.shape[0]
    new_states, new_k, new_v = [], [], []
    for l in range(DEPTH):
        i = l // N_MIXERS
        sh1, sc1, gt1, sh2, sc2, gt2 = ada_mod(c_ctx[None, :], ada_w[l], ada_b[l])
        h = modulate(rms_norm(xp, norm_g[l, 0]), sh1, sc1)
        if l % N_MIXERS == 0:
            s0 = jnp.zeros((bp, 2, RWKV_HEADS, RWKV_HEAD, RWKV_HEAD), xp.dtype)
            o, s_fin = rwkv7_mixer(h, s0, *[p[i] for p in rwkv_params])
            new_states.append(s_fin)
        else:
            o, k_c, v_c = na_context(h, na_w_qkv[i], na_w_o[i], na_q_g[i], na_k_g[i])
            new_k.append(k_c)
            new_v.append(v_c)
        xp = xp + gt1[:, None, :] * o
        h = modulate(rms_norm(xp, norm_g[l, 1]), sh2, sc2)
        xp = xp + gt2[:, None, :] * sq_relu_mlp(h, mlp_w1[l], mlp_w2[l])
    y_prompt = xp

    xs = x_sample
    for l in range(DEPTH):
        i = l // N_MIXERS
        sh1, sc1, gt1, sh2, sc2, gt2 = ada_mod(c, ada_w[l], ada_b[l])
        h = modulate(rms_norm(xs, norm_g[l, 0]), sh1, sc1)
        if l % N_MIXERS == 0:
            o, _ = rwkv7_mixer(h, state_rwkv[:, i], *[p[i] for p in rwkv_params])
        else:
            o = na_latent(h, cache_na_k[:, i], cache_na_v[:, i], na_w_qkv[i], na_w_o[i],
                          na_q_g[i], na_k_g[i], na_rpb[i])
        xs = xs + gt1[:, None, :] * o
        h = modulate(rms_norm(xs, norm_g[l, 1]), sh2, sc2)
        xs = xs + gt2[:, None, :] * sq_relu_mlp(h, mlp_w1[l], mlp_w2[l])
    y_sample = xs

    new_state_rwkv = jnp.stack(new_states, axis=1)
    new_cache_na_k = jnp.stack(new_k, axis=1)
    new_cache_na_v = jnp.stack(new_v, axis=1)
    return (y_prompt, y_sample, new_state_rwkv, new_cache_na_k, new_cache_na_v)
```

```python
from contextlib import ExitStack
import numpy as np
import ml_dtypes
import concourse.bass as bass
import concourse.mybir as mybir
from concourse.bass_utils import run_bass_kernel_spmd

F32 = mybir.dt.float32
BF16 = mybir.dt.bfloat16
AF = mybir.ActivationFunctionType
ALU = mybir.AluOpType
AX = mybir.AxisListType

D = 1024
DFF = 4096
NCH = 8
H = 16
HK = 64
NEG = -30000.0


class Buf:
    __slots__ = ("t", "w", "r", "name", "banks")

    def __init__(self, t, name="", banks=()):
        self.t = t
        self.w = {}
        self.r = {}
        self.name = name
        self.banks = banks

    def __getitem__(self, k):
        return self.t[k]


class Eng:
    def __init__(self, name, sem, is_dma_only=False):
        self.name = name
        self.sem = sem
        self.count = 0
        self.prog = []
        self.seen = {}
        self.slots = []
        self.slot_i = 0


class K:
    def __init__(self, nc, es):
        self.nc = nc
        self.es = es
        self.E = {}
        for n in ("pe", "act", "dve", "pool", "sp"):
            self.E[n] = Eng(n, es.enter_context(nc.semaphore("s_" + n)))
        for n, cnt in (("sp", 12), ("pool", 8), ("act", 4)):
            e = self.E[n]
            for i in range(cnt):
                e.slots.append([es.enter_context(nc.semaphore("d_%s%d" % (n, i))), 0])
        self.dram = {}
        self.ninst = 0

    def sb(self, st, name, shape, dt):
        self.uid = getattr(self, "uid", 0) + 1
        name = "sb%d_%s" % (self.uid, name)
        return Buf(st.enter_context(self.nc.sbuf_tensor(name, list(shape), dt)), name)

    def ps(self, st, name, shape, dt=F32):
        self.uid = getattr(self, "uid", 0) + 1
        name = "ps%d_%s" % (self.uid, name)
        return Buf(st.enter_context(self.nc.psum_tensor(name, list(shape), dt)), name, banks=(Buf(None, name + "_bank"),))

    def dbuf(self, key):
        b = self.dram.get(key)
        if b is None:
            b = self.dram[key] = Buf(None, str(key))
        return b

    def _wait(self, eng, tok):
        sem, val, owner = tok
        if owner is eng and eng.name == "pe":
            return
        sid = id(sem)
        if eng.seen.get(sid, 0) >= val:
            return
        eng.seen[sid] = val
        eng.prog.append(lambda e, sem=sem, val=val: e.wait_ge(sem, val))

    def _deps(self, eng, reads, writes):
        for b in reads:
            for tok in b.w.values():
                self._wait(eng, tok)
        for b in writes:
            for tok in b.w.values():
                self._wait(eng, tok)
            for tok in b.r.values():
                self._wait(eng, tok)

    def _mark(self, tok, reads, writes):
        sid = id(tok[0])
        for b in reads:
            b.r[sid] = tok
        for b in writes:
            b.w = {sid: tok}
            b.r = {}

    def op(self, en, fn, reads=(), writes=(), inc=True):
        eng = self.E[en]
        bk = [x for b in list(reads) + list(writes) for x in b.banks]
        if bk:
            writes = list(writes) + bk
        self._deps(eng, reads, writes)
        self.ninst += 1
        if inc:
            eng.count += 1
            sem = eng.sem
            eng.prog.append(lambda e, fn=fn, sem=sem: fn(e).then_inc(sem, 1))
            self._mark((sem, eng.count, eng), reads, writes)
        else:
            eng.prog.append(lambda e, fn=fn: fn(e))
            self._mark((eng.sem, eng.count + 1, eng), reads, writes)

    def dma(self, en, fn, reads=(), writes=()):
        eng = self.E[en]
        self._deps(eng, reads, writes)
        slot = eng.slots[eng.slot_i]
        eng.slot_i = (eng.slot_i + 1) % len(eng.slots)
        sem = slot[0]
        if slot[1] > 0:
            self._wait(eng, (sem, 16 * slot[1], None))
        slot[1] += 1
        val = 16 * slot[1]
        self.ninst += 1
        eng.prog.append(lambda e, fn=fn, sem=sem: fn(e).then_inc(sem, 16))
        self._mark((sem, val, None), reads, writes)

    def capture(self, f):
        rec = []
        o_op, o_dma = self.op, self.dma
        self.op = lambda *a, **kw: rec.append((o_op, a, kw))
        self.dma = lambda *a, **kw: rec.append((o_dma, a, kw))
        try:
            f()
        finally:
            del self.op
            del self.dma
        return rec

    def play_interleaved(self, recs):
        n = max(len(r) for r in recs)
        for i_ in range(n):
            for r in recs:
                if i_ < len(r):
                    fn, a, kw = r[i_]
                    fn(*a, **kw)

    def barrier(self):
        toks = []
        for e in self.E.values():
            if e.count:
                toks.append((e.sem, e.count, e))
            for s in e.slots:
                if s[1]:
                    toks.append((s[0], 16 * s[1], None))
        for e in self.E.values():
            for t in toks:
                self._wait(e, t)

    def finish(self):
        self.barrier()

    def emit(self):
        nc = self.nc
        with nc.Block() as block:
            @block.sync
            def _(e):
                for f in self.E["sp"].prog:
                    f(e)

            @block.gpsimd
            def _(e):
                for f in self.E["pool"].prog:
                    f(e)

            @block.scalar
            def _(e):
                for f in self.E["act"].prog:
                    f(e)

            @block.vector
            def _(e):
                for f in self.E["dve"].prog:
                    f(e)

            @block.tensor
            def _(e):
                for f in self.E["pe"].prog:
                    f(e)


class Cfg:
    def __init__(self, tp=256, n_p=2, ts=4096, depth=4, dbg=None, phases=('rwkv', 'scan', 'na', 'mlp')):
        self.tp, self.n_p, self.ts, self.depth = tp, n_p, ts, depth
        self.phases = phases
        self.scan_stop = 99
        self.na_stop = 99
        self.sub = 99
        self.seqs = [(i * tp, tp, 0) for i in range(n_p)] + [(n_p * tp, ts, 1)]
        self.rows = n_p * tp + ts
        self.dbg = dbg or []
        self.n_rw = (depth + 1) // 2
        self.n_na = depth // 2


WEIGHTS = [
    ("ada_w", "L", D, 6 * D), ("mlp_w1", "L", D, DFF), ("mlp_w2", "L", DFF, D),
    ("rwkv_w_rkv", "R3", D, D), ("rwkv_w_o", "R", D, D),
    ("rwkv_w1", "R2", D, 64), ("rwkv_w2", "R2", 64, D),
    ("rwkv_a1", "R2", D, 64), ("rwkv_a2", "R2", 64, D),
    ("rwkv_g1", "R", D, 128), ("rwkv_g2", "R", 128, D),
    ("na_w_qkv", "N", D, 3 * D), ("na_w_o", "N", D, D),
]


def build(cfg):
    nc = bass.Bass("TRN2", target_bir_lowering=False)
    es = ExitStack()
    k = K(nc, es)
    L = cfg.depth
    NR, NN = cfg.n_rw, cfg.n_na
    ROWS = cfg.rows
    NP, TP, TS = cfg.n_p, cfg.tp, cfg.ts

    def din(name, shape, dt=F32):
        return nc.dram_tensor(name, list(shape), dt, kind="ExternalInput").ap()

    def dout(name, shape, dt=F32):
        return nc.dram_tensor(name, list(shape), dt, kind="ExternalOutput").ap()

    def dscr(name, shape, dt=F32):
        if name in cfg.dbg:
            return nc.dram_tensor(name, list(shape), dt, kind="ExternalOutput").ap()
        return nc.dram_tensor(name, list(shape), dt, kind="Internal").ap()

    x_in = din("x_in", [ROWS, D])
    conds = din("conds", [2, D])
    norm_g = din("norm_g", [L, 2, D])
    ada_b = din("ada_b", [L, 6 * D])
    nlay = {"L": L, "R": NR, "R2": NR * 2, "R3": NR * 3, "N": NN}
    wf = {}
    wb = {}
    for name, kind, r, c in WEIGHTS:
        n = nlay[kind]
        if n == 0:
            continue
        wf[name] = din(name, [n * r, c])
        wb[name] = dscr(name + "_b", [n * r, c], BF16)
    ident_in = din("ident", [128, 128])
    zeros_in = din("zeros", [1, D])
    tric_in = din("tric", [2, 128, 128])
    allc_in = din("allc", [128, 128])
    mask4_in = din("mask4", [2, 128, 512])
    maskl_in = din("maskl", [2, 128, 128])
    y_out = dout("y_out", [ROWS, D])
    if NR:
        rw_mu = din("rwkv_mu", [NR, 6, D])
        rw_vec = {n: din(n, [NR, 2, D]) for n in ("rwkv_w0", "rwkv_a0")}
        for n in ("rwkv_k_k", "rwkv_k_a", "rwkv_r_k", "rwkv_ln_w", "rwkv_ln_b"):
            rw_vec[n] = din(n, [NR, D])
        st_in = din("state_in", [NR, 2, H, 64, 64])
        st_out = dout("state_out", [NP, NR, 2, H, 64, 64])
        SC = {n: dscr("S_" + n, [ROWS, D]) for n in ("Hh", "LW0", "LW1", "YF", "YB")}
        for n in ("R", "V", "KK", "K0", "K1", "B0", "B1", "G"):
            SC[n] = dscr("S_" + n, [ROWS, D], BF16)
        SC["COEF"] = dscr("S_COEF", [ROWS, H])

    if NN:
        ecol_in = din("ecol", [31, 4096])
        colmask_in = din("colmask", [1, 4096])
        j15_in = din("j15", [15, 15])
        rpb_in = din("na_rpb", [NN, H, 15, 31])
        qg_in = din("na_q_g", [NN, 64])
        kg_in = din("na_k_g", [NN, 64])
        ck_in = din("cache_k", [NN, H, 256, 64])
        cv_in = din("cache_v", [NN, H, 256, 64])
        nk_out = dout("nk_out", [NP, NN, H, TP, 64])
        nv_out = dout("nv_out", [NP, NN, H, TP, 64])
        QT = dscr("S_QT", [H, 64, ROWS], BF16)
        KT = dscr("S_KT", [H, 64, ROWS], BF16)
        VV = dscr("S_VV", [ROWS, D], BF16)
        OO = dscr("S_OO", [ROWS, D], BF16)
        T1R = dscr("S_T1R", [H, 15, 4096])
    X = dscr("X", [ROWS, D])
    MODS = dscr("MODS", [L, 2, 6 * D])

    with es:
        cst = ExitStack()
        es.enter_context(cst)
        ident_f = k.sb(cst, "ident_f", [128, 128], F32)
        ident_b = k.sb(cst, "ident_b", [128, 128], BF16)
        k.dma("sp", lambda e: e.dma_start(out=ident_f[:], in_=ident_in[:, :]), writes=[ident_f])
        k.op("dve", lambda e: e.tensor_copy(out=ident_b[:], in_=ident_f[:]), reads=[ident_f], writes=[ident_b])
        tric = [k.sb(cst, "tric%d" % d, [128, 128], F32) for d in range(2)]
        allc = k.sb(cst, "allc", [128, 128], F32)
        mask4 = [k.sb(cst, "mask4_%d" % d, [128, 512], F32) for d in range(2)]
        maskl = [k.sb(cst, "maskl_%d" % d, [128, 128], F32) for d in range(2)]
        for d in range(2):
            k.dma("sp", lambda e, d=d: e.dma_start(out=tric[d][:], in_=tric_in[d]), writes=[tric[d]])
            k.dma("sp", lambda e, d=d: e.dma_start(out=mask4[d][:], in_=mask4_in[d]), writes=[mask4[d]])
            k.dma("sp", lambda e, d=d: e.dma_start(out=maskl[d][:], in_=maskl_in[d]), writes=[maskl[d]])
        k.dma("sp", lambda e: e.dma_start(out=allc[:], in_=allc_in[:, :]), writes=[allc])

        def cast_list(names):
            out = []
            for name, kind, r, c in WEIGHTS:
                if name not in wf or name not in names:
                    continue
                tot = nlay[kind] * r
                step = max(1, (1 << 19) // c)
                for r0 in range(0, tot, step):
                    r1 = min(tot, r0 + step)
                    out.append(lambda name=name, r0=r0, r1=r1: k.dma(
                        "pool", lambda e: e.dma_start(out=wb[name][r0:r1, :], in_=wf[name][r0:r1, :]), writes=[k.dbuf((name, "b", r0))]))
            return out
        early = [n for n, _, _, _ in WEIGHTS if n == "ada_w" or n.startswith("rwkv")]
        late = [n for n, _, _, _ in WEIGHTS if n not in early]
        if not NR or "rwkv" not in cfg.phases:
            early, late = early + late, []
        for f in cast_list(early):
            f()
        pending_casts = cast_list(late)

        def trickle(n=1):
            for _ in range(n):
                if pending_casts:
                    pending_casts.pop(0)()

        def drain_casts():
            while pending_casts:
                pending_casts.pop(0)()
        k.barrier()

        def wread(name):
            return [b for key, b in k.dram.items() if key[0] == name]

        with ExitStack() as st:
            cT = k.sb(st, "cT", [128, 2, NCH], F32)
            sg = k.sb(st, "sg", [128, 2, NCH], F32)
            cTb = k.sb(st, "cTb", [128, NCH, 2], BF16)
            k.dma("sp", lambda e: e.dma_start(out=cT[:], in_=conds.rearrange("j (c p) -> p j c", p=128), allow_slow_non_contiguous=True), writes=[cT])
            k.op("act", lambda e: e.activation(out=sg[:], in_=cT[:], func=AF.Sigmoid), reads=[cT], writes=[sg])
            k.op("dve", lambda e: e.tensor_tensor(out=cTb[:].rearrange("p c j -> p j c"), in0=cT[:], in1=sg[:], op=ALU.mult), reads=[cT, sg], writes=[cTb])
            aw = [k.sb(st, "aw%d" % i, [128, NCH, 512], BF16) for i in range(3)]
            adb = k.sb(st, "adb", [2, 6 * D], F32)
            mo = k.sb(st, "mo", [2, 6 * D], F32)
            pa = [k.ps(st, "pa%d" % i, [2, 512]) for i in range(2)]
            it = 0
            for l in range(L):
                k.dma("sp", lambda e, l=l: e.dma_start(out=adb[:], in_=ada_b[l:l + 1, :].broadcast_to([2, 6 * D])), writes=[adb])
                for nb in range(12):
                    a = aw[it % 3]
                    p = pa[it % 2]
                    it += 1
                    k.dma("sp", lambda e, a=a, l=l, nb=nb: e.dma_start(
                        out=a[:], in_=wb["ada_w"][l * D:(l + 1) * D, nb * 512:(nb + 1) * 512].rearrange("(c p) f -> p c f", p=128)),
                        reads=wread("ada_w"), writes=[a])
                    for kc in range(NCH):
                        k.op("pe", lambda e, p=p, a=a, kc=kc: e.matmul(p[:], lhsT=cTb[:, kc, :], rhs=a[:, kc, :], start=(kc == 0), stop=(kc == NCH - 1)),
                             reads=[cTb, a], writes=[p], inc=(kc == NCH - 1))
                    k.op("dve", lambda e, p=p, nb=nb: e.tensor_tensor(out=mo[:, nb * 512:(nb + 1) * 512], in0=p[:], in1=adb[:, nb * 512:(nb + 1) * 512], op=ALU.add),
                         reads=[p, adb], writes=[mo])
                k.dma("sp", lambda e, l=l: e.dma_start(out=MODS[l], in_=mo[:]), reads=[mo], writes=[k.dbuf(("MODS", l))])
        k.barrier()

        with ExitStack() as st:
            xb_ = [k.sb(st, "xcp%d" % i, [128, D], F32) for i in range(3)]
            for t in range(ROWS // 128):
                b = xb_[t % 3]
                k.dma("sp", lambda e, b=b, t=t: e.dma_start(out=b[:], in_=x_in[t * 128:(t + 1) * 128, :]), writes=[b])
                k.dma("sp", lambda e, b=b, t=t: e.dma_start(out=X[t * 128:(t + 1) * 128, :], in_=b[:]), reads=[b], writes=[k.dbuf(("X", t))])
        k.barrier()

        def load_mod(st, l, which, gidx=None):
            out = {}
            for cnd in range(2):
                for idx in which:
                    b = k.sb(st, "mod_%d_%d" % (cnd, idx), [128, D], F32)
                    k.dma("sp", lambda e, b=b, cnd=cnd, idx=idx: e.dma_start(
                        out=b[:], in_=MODS[l, cnd:cnd + 1, idx * D:(idx + 1) * D].broadcast_to([128, D])),
                        reads=[k.dbuf(("MODS", l))], writes=[b])
                    out[(cnd, idx)] = b
            return out

        def make_gain(st, l, sub, mods, sc_idx):
            g = k.sb(st, "ng", [128, D], F32)
            k.dma("sp", lambda e: e.dma_start(out=g[:], in_=norm_g[l, sub:sub + 1, :].broadcast_to([128, D])), writes=[g])
            for cnd in range(2):
                b = mods[(cnd, sc_idx)]
                k.op("dve", lambda e, b=b: e.scalar_tensor_tensor(out=b[:], in0=b[:], scalar=1.0, in1=g[:], op0=ALU.add, op1=ALU.mult),
                     reads=[b, g], writes=[b])

        def tile_cond(t):
            return 0 if t * 128 < NP * TP else 1

        def norm_mod(xt, hb, G, S, ss, junk):
            k.op("act", lambda e: e.activation(out=hb[:], in_=xt[:], func=AF.Square, accum_out=ss[:]), reads=[xt], writes=[hb, ss])
            k.op("act", lambda e: e.activation(out=ss[:], in_=ss[:], func=AF.Ln, scale=1.0 / D, bias=1e-6), reads=[ss], writes=[ss])
            k.op("act", lambda e: e.activation(out=ss[:], in_=ss[:], func=AF.Exp, scale=-0.5), reads=[ss], writes=[ss])
            k.op("dve", lambda e: e.scalar_tensor_tensor(out=hb[:], in0=xt[:], scalar=ss[:, 0:1], in1=G[:], op0=ALU.mult, op1=ALU.mult),
                 reads=[xt, ss, G], writes=[hb])
            k.op("pool", lambda e: e.tensor_tensor(out=hb[:], in0=hb[:], in1=S[:], op=ALU.add), reads=[hb, S], writes=[hb])

        def mlp_phase(l):
            drain_casts()
            with ExitStack() as st:
                mods = load_mod(st, l, [3, 4, 5])
                make_gain(st, l, 1, mods, 4)
                xts = [k.sb(st, "mx%d" % i, [128, D], F32) for i in range(8)]
                hbs = [k.sb(st, "mh%d" % i, [128, D], BF16) for i in range(2)]
                junk = k.sb(st, "mjunk", [128, D], F32)
                junk2 = k.sb(st, "mjunk2", [128, 512], F32)
                sss = [k.sb(st, "mss%d" % i, [128, 1], F32) for i in range(2)]
                hT = [k.sb(st, "mhT%d" % i, [128, NCH, 512], BF16) for i in range(2)]
                hidb = [k.sb(st, "mhid%d" % i, [128, 4, 512], BF16) for i in range(8)]
                rl = [k.sb(st, "mrl%d" % i, [128, 512], F32) for i in range(2)]
                w1s = [k.sb(st, "mw1_%d" % i, [128, NCH, 512], BF16) for i in range(4)]
                w2s = [k.sb(st, "mw2_%d" % i, [128, 4, 512], BF16) for i in range(4)]
                ptr = k.ps(st, "mptr", [128, NCH, 128], BF16)
                phid = [k.ps(st, "mphid%d" % i, [128, 512]) for i in range(2)]
                pacc = [k.ps(st, "mpacc%d" % i, [128, 512]) for i in range(4)]
                ngrp = ROWS // 512
                c1 = [0]
                c2 = [0]
                ch = [0]
                w1r = wread("mlp_w1")
                w2r = wread("mlp_w2")

                hb4 = [k.sb(st, "mhb%d" % i, [128, D], BF16) for i in range(4)]

                def front_norm(g):
                    cnd = tile_cond(g * 4)
                    G, S = mods[(cnd, 4)], mods[(cnd, 3)]
                    for s in range(4):
                        t = g * 4 + s
                        xt = xts[(g % 2) * 4 + s]
                        k.dma("sp", lambda e, xt=xt, t=t: e.dma_start(out=xt[:], in_=X[t * 128:(t + 1) * 128, :]),
                              reads=[k.dbuf(("X", t))], writes=[xt])
                        norm_mod(xt, hb4[s], G, S, sss[s % 2], junk)

                def front_T(g):
                    hTg = hT[g % 2]
                    for s in range(4):
                        hb = hb4[s]
                        for kc in range(NCH):
                            k.op("pe", lambda e, hb=hb, kc=kc: e.transpose(out=ptr[:, kc, :], in_=hb[:, kc * 128:(kc + 1) * 128], identity=ident_b[:]),
                                 reads=[hb, ident_b], writes=[ptr], inc=(kc == NCH - 1))
                        k.op("act", lambda e, hTg=hTg, s=s: e.copy(out=hTg[:, :, s * 128:(s + 1) * 128], in_=ptr[:]), reads=[ptr], writes=[hTg])

                def out_block(g, half, fb, w2):
                    for s in range(4):
                        for fi in range(4):
                            fc = fb * 4 + fi
                            k.op("pe", lambda e, s=s, fi=fi, fc=fc, w2=w2: e.matmul(
                                pacc[s][:], lhsT=hidb[fb][:, fi, s * 128:(s + 1) * 128], rhs=w2[:, fi, :], start=(fc == 0), stop=(fc == 31)),
                                reads=[hidb[fb], w2], writes=[pacc[s]], inc=(fi == 3))

                def load_w2(half, fb):
                    w2 = w2s[c2[0] % 4]
                    c2[0] += 1
                    k.dma("sp", lambda e, w2=w2, fb=fb, half=half: e.dma_start(
                        out=w2[:], in_=wb["mlp_w2"][l * DFF + fb * 512:l * DFF + (fb + 1) * 512, half * 512:(half + 1) * 512].rearrange("(c p) f -> p c f", p=128)),
                        reads=w2r, writes=[w2])
                    return w2

                front_norm(0)
                front_T(0)
                pend_store = []
                for g in range(ngrp):
                    cnd = tile_cond(g * 4)
                    GT = mods[(cnd, 5)]
                    hTg = hT[g % 2]
                    xg = [xts[(g % 2) * 4 + s] for s in range(4)]
                    for half in range(2):
                        prev = None
                        for fb in range(8):
                            if half == 0:
                                w1 = w1s[c1[0] % 4]
                                c1[0] += 1
                                k.dma("sp", lambda e, w1=w1, fb=fb: e.dma_start(
                                    out=w1[:], in_=wb["mlp_w1"][l * D:(l + 1) * D, fb * 512:(fb + 1) * 512].rearrange("(c p) f -> p c f", p=128)),
                                    reads=w1r, writes=[w1])
                                w2 = load_w2(half, fb)
                                if fb == 2:
                                    while pend_store:
                                        pend_store.pop(0)()
                                for fi in range(4):
                                    fc = fb * 4 + fi
                                    ph = phid[ch[0] % 2]
                                    r_ = rl[ch[0] % 2]
                                    ch[0] += 1
                                    for kc in range(NCH):
                                        k.op("pe", lambda e, ph=ph, w1=w1, kc=kc, fi=fi, hTg=hTg: e.matmul(
                                            ph[:], lhsT=w1[:, kc, fi * 128:(fi + 1) * 128], rhs=hTg[:, kc, :], start=(kc == 0), stop=(kc == NCH - 1)),
                                            reads=[w1, hTg], writes=[ph], inc=(kc == NCH - 1))
                                    k.op("act", lambda e, ph=ph, r_=r_: e.activation(out=r_[:], in_=ph[:], func=AF.Relu), reads=[ph], writes=[r_])
                                    k.op("pool", lambda e, r_=r_, fb=fb, fi=fi: e.tensor_tensor(out=hidb[fb][:, fi, :], in0=r_[:], in1=r_[:], op=ALU.mult), reads=[r_], writes=[hidb[fb]])
                                if prev is not None:
                                    out_block(g, half, prev[0], prev[1])
                                prev = (fb, w2)
                            else:
                                w2 = load_w2(half, fb)
                                out_block(g, half, fb, w2)
                                if fb == 0 and g + 1 < ngrp:
                                    front_norm(g + 1)
                                if fb == 5 and g + 1 < ngrp:
                                    front_T(g + 1)
                        if half == 0:
                            out_block(g, half, prev[0], prev[1])
                        for s in range(4):
                            xt = xg[s]
                            k.op("dve", lambda e, s=s, half=half, GT=GT: e.tensor_tensor(out=junk2[:], in0=pacc[s][:], in1=GT[:, half * 512:(half + 1) * 512], op=ALU.mult),
                                 reads=[pacc[s], GT], writes=[junk2])
                            k.op("dve", lambda e, xt=xt, half=half: e.tensor_tensor(out=xt[:, half * 512:(half + 1) * 512], in0=xt[:, half * 512:(half + 1) * 512], in1=junk2[:], op=ALU.add),
                                 reads=[xt, junk2], writes=[xt])
                    def store(g=g, xg=xg):
                        for s in range(4):
                            t = g * 4 + s
                            k.dma("sp", lambda e, xt=xg[s], t=t: e.dma_start(out=X[t * 128:(t + 1) * 128, :], in_=xt[:]),
                                  reads=[xg[s]], writes=[k.dbuf(("X", t))])
                    pend_store.append(store)
                while pend_store:
                    pend_store.pop(0)()
            k.barrier()

        def seq_tiles():
            out = []
            for si, (r0, T, cnd) in enumerate(cfg.seqs):
                n = T // 128
                for j in range(n):
                    out.append((r0 // 128 + j, si, j, n))
            return out

        def bcast_load(st, name, src_row_ap):
            b = k.sb(st, name, [128, D], F32)
            k.dma("sp", lambda e: e.dma_start(out=b[:], in_=src_row_ap.broadcast_to([128, D])), writes=[b])
            return b

        def scan_pass(l, d):
            i = l // 2
            rows = lambda t: slice(t * 128, (t + 1) * 128)
            hsl = lambda h: slice(h * 64, (h + 1) * 64)
            ydst = "YF" if d == 0 else "YB"
            with ExitStack() as st:
                LfB = [{n: k.sb(st, "sl%d_%s" % (j, n), [128, D], BF16) for n in ("r", "v", "kk", "kd", "bd")} for j in range(2)]
                LfW = [k.sb(st, "slw%d" % j, [128, D], F32) for j in range(2)]
                srcn = {"r": "R", "v": "V", "kk": "KK", "kd": "K%d" % d, "bd": "B%d" % d}
                cumS = k.sb(st, "cumS", [128, D], F32)
                EX = [k.sb(st, "EX%d" % j, [128, D], F32) for j in range(2)]
                tmpf = k.sb(st, "tmpf", [128, D], F32)
                yout = k.sb(st, "yout", [128, D], F32)
                RT, BT, KT = [k.sb(st, n, [128, D], BF16) for n in ("RT", "BT", "KT")]
                XA = k.sb(st, "XA", [128, H, 64], BF16)
                ART = k.sb(st, "ART", [64, H, 128], BF16)
                RTT = [k.sb(st, "RTT%d" % j, [64, H, 128], BF16) for j in range(2)]
                BTT = k.sb(st, "BTT", [64, H, 128], BF16)
                KTT = k.sb(st, "KTT", [64, H, 128], BF16)
                BP = [k.sb(st, "BP%d" % j, [128, D], BF16) for j in range(2)]
                KP = [k.sb(st, "KP%d" % j, [128, D], BF16) for j in range(2)]
                vb = [k.sb(st, "vb%d" % j, [128, D], BF16) for j in range(2)]
                PC = [k.sb(st, "PC%d" % j, [64, H], F32) for j in range(2)]
                MM = [[k.sb(st, "MM%d_%d" % (j, h), [128, 512], BF16) for h in range(H)] for j in range(2)]
                Z = [[k.sb(st, "Z%d_%d" % (h, j), [128, 384], BF16) for j in range(2)] for h in range(H)]
                WTg = [k.sb(st, "WTg%d" % j, [64, 8, 128], BF16) for j in range(2)]
                U = [k.sb(st, "U%d" % h, [128, 64], BF16) for h in range(H)]
                A = [k.sb(st, "A%d" % h, [64, 64], F32) for h in range(H)]
                Ab = [k.sb(st, "Ab%d" % h, [64, 64], BF16) for h in range(H)]
                Sio = k.sb(st, "Sio", [64, H, 64], F32)
                T0 = st.enter_context(nc.psum_tensor("scT0_%d_%d" % (l, d), [128, D], F32))
                T1 = st.enter_context(nc.psum_tensor("scT1_%d_%d" % (l, d), [128, D], F32))
                T2 = st.enter_context(nc.psum_tensor("scT2_%d_%d" % (l, d), [128, D], F32))
                T3 = st.enter_context(nc.psum_tensor("scT3_%d_%d" % (l, d), [128, D], F32))
                bks = [Buf(None, "bank%d" % j) for j in range(8)]
                psy = Buf(T0, "psy", banks=(bks[0], bks[1]))
                pPC = Buf(T0[0:64, 0:16], "pPC", banks=(bks[0],))
                PP = [Buf(T1[:, 0:512], "PP0", banks=(bks[2],)), Buf(T1[:, 512:1024], "PP1", banks=(bks[3],))]
                PQ = [Buf(T2[:, 0:512], "PQ0", banks=(bks[4],)), Buf(T2[:, 512:1024], "PQ1", banks=(bks[5],))]
                PRf = [Buf(T3[:, 0:512], "PRf0", banks=(bks[6],)), Buf(T3[:, 512:1024], "PRf1", banks=(bks[7],))]
                PR = [Buf(T3[:, 512 * j:512 * (j + 1)].bitcast(BF16).rearrange("p (a b) -> p a b", a=8), "PR%d" % j, banks=(bks[6 + j],)) for j in range(2)]
                PT0 = [Buf(T0[:, 512 * j:512 * (j + 1)].bitcast(BF16).rearrange("p (a b) -> p a b", a=8), "PT0_%d" % j, banks=(bks[j],)) for j in range(2)]
                ring4 = [PP[0], PP[1], PQ[0], PQ[1], PRf[0], PRf[1]]
                cnt = {"x": 0, "c": 0}
                ec = float(np.exp(-0.5))

                def load(t, sl):
                    for nm, b in LfB[sl].items():
                        k.dma("sp", lambda e, b=b, nm=nm, t=t: e.dma_start(out=b[:], in_=SC[srcn[nm]][rows(t), :]), reads=[k.dbuf((srcn[nm], t))], writes=[b])
                    k.dma("sp", lambda e, t=t, sl=sl: e.dma_start(out=LfW[sl][:], in_=SC["LW%d" % d][rows(t), :]), reads=[k.dbuf(("LW%d" % d, t))], writes=[LfW[sl]])

                def front12(sl, par):
                    L_, lw = LfB[sl], LfW[sl]
                    th = []
                    ad = th.append
                    for half in range(2):
                        ad(lambda half=half: k.op("pe", lambda e: e.matmul(T0[:, half * 512:(half + 1) * 512], lhsT=tric[d][:], rhs=lw[:, half * 512:(half + 1) * 512], start=True, stop=True),
                                                  reads=[tric[d], lw], writes=[psy]))
                    ad(lambda: k.op("dve", lambda e: e.tensor_copy(out=cumS[:], in_=T0[:]), reads=[psy], writes=[cumS]))
                    for half in range(2):
                        ad(lambda half=half: k.op("pe", lambda e: e.matmul(T0[:, half * 512:(half + 1) * 512], lhsT=allc[:], rhs=lw[:, half * 512:(half + 1) * 512], start=True, stop=True),
                                                  reads=[allc, lw], writes=[psy]))
                    ad(lambda: k.op("dve", lambda e: e.tensor_tensor(out=tmpf[:], in0=T0[:], in1=cumS[:], op=ALU.subtract), reads=[psy, cumS], writes=[tmpf]))
                    ad(lambda: k.op("act", lambda e: e.activation(out=EX[1][:], in_=tmpf[:], func=AF.Exp), reads=[tmpf], writes=[EX[1]]))
                    ad(lambda: k.op("dve", lambda e: e.tensor_tensor(out=BP[par][:], in0=L_["bd"][:], in1=EX[1][:], op=ALU.mult), reads=[L_["bd"], EX[1]], writes=[BP[par]]))
                    ad(lambda: k.op("pool", lambda e: e.tensor_tensor(out=KP[par][:], in0=L_["kd"][:], in1=EX[1][:], op=ALU.mult), reads=[L_["kd"], EX[1]], writes=[KP[par]]))

                    def pcs():
                        for h in range(H):
                            k.op("pe", lambda e, h=h: e.matmul(pPC[:, h:h + 1], lhsT=lw[:, hsl(h)], rhs=allc[:, 0:1], start=True, stop=True), reads=[lw, allc], writes=[pPC], inc=(h == H - 1))
                        k.op("act", lambda e: e.activation(out=PC[par][:], in_=pPC[:], func=AF.Exp), reads=[pPC], writes=[PC[par]])
                    ad(pcs)
                    ad(lambda: k.op("act", lambda e: e.activation(out=EX[0][:], in_=cumS[:], func=AF.Exp), reads=[cumS], writes=[EX[0]]))
                    ad(lambda: k.op("dve", lambda e: e.tensor_tensor(out=RT[:], in0=L_["r"][:], in1=EX[0][:], op=ALU.mult), reads=[L_["r"], EX[0]], writes=[RT]))
                    ad(lambda: k.op("act", lambda e: e.activation(out=EX[1][:], in_=cumS[:], func=AF.Exp, scale=-1.0), reads=[cumS], writes=[EX[1]]))
                    ad(lambda: k.op("dve", lambda e: e.tensor_tensor(out=BT[:], in0=L_["bd"][:], in1=EX[1][:], op=ALU.mult), reads=[L_["bd"], EX[1]], writes=[BT]))
                    ad(lambda: k.op("pool", lambda e: e.tensor_tensor(out=KT[:], in0=L_["kd"][:], in1=EX[1][:], op=ALU.mult), reads=[L_["kd"], EX[1]], writes=[KT]))
                    ad(lambda: k.op("dve", lambda e: e.scalar_tensor_tensor(out=tmpf[:], in0=lw[:], scalar=ec, in1=cumS[:], op0=ALU.mult, op1=ALU.add), reads=[lw, cumS], writes=[tmpf]))
                    ad(lambda: k.op("act", lambda e: e.activation(out=EX[0][:], in_=tmpf[:], func=AF.Exp), reads=[tmpf], writes=[EX[0]]))
                    ad(lambda: k.op("dve", lambda e: e.scalar_tensor_tensor(out=XA[:].rearrange("p h c -> p (h c)"), in0=L_["kk"][:], scalar=-1.0, in1=EX[0][:], op0=ALU.mult, op1=ALU.mult),
                                    reads=[L_["kk"], EX[0]], writes=[XA]))
                    ad(lambda: k.op("pool", lambda e: e.tensor_copy(out=vb[par][:], in_=L_["v"][:]), reads=[L_["v"]], writes=[vb[par]]))
                    for (srcb, srcf, dst) in ((RT, lambda h: RT[:, hsl(h)], RTT[par]), (BT, lambda h: BT[:, hsl(h)], BTT), (KT, lambda h: KT[:, hsl(h)], KTT), (XA, lambda h: XA[:, h, :], ART)):
                        for g8 in range(2):
                            def tr(srcb=srcb, srcf=srcf, dst=dst, g8=g8):
                                pr = PT0[cnt["x"] % 2]
                                cnt["x"] += 1
                                for hh in range(8):
                                    h = g8 * 8 + hh
                                    k.op("pe", lambda e, hh=hh, h=h: e.transpose(out=pr[0:64, hh, :], in_=srcf(h), identity=ident_b[:]), reads=[srcb, ident_b], writes=[pr], inc=(hh == 7))
                                k.op("act", lambda e: e.copy(out=dst[:, g8 * 8:(g8 + 1) * 8, :], in_=pr[0:64, :, :]), reads=[pr], writes=[dst])
                            ad(tr)
                    return th

                def AB1(par, h):
                    p = PQ[h % 2]
                    k.op("pe", lambda e: e.matmul(p[:, 0:128], lhsT=BTT[:, h, :], rhs=ART[:, h, :], start=True, stop=True), reads=[BTT, ART], writes=[p], inc=False)
                    k.op("pe", lambda e: e.matmul(p[:, 128:256], lhsT=KTT[:, h, :], rhs=ART[:, h, :], start=True, stop=True), reads=[KTT, ART], writes=[p], inc=False)
                    k.op("pe", lambda e: e.matmul(p[:, 256:384], lhsT=BTT[:, h, :], rhs=RTT[par][:, h, :], start=True, stop=True), reads=[BTT, RTT[par]], writes=[p], inc=False)
                    k.op("pe", lambda e: e.matmul(p[:, 384:512], lhsT=KTT[:, h, :], rhs=RTT[par][:, h, :], start=True, stop=True), reads=[KTT, RTT[par]], writes=[p])
                    k.op("dve", lambda e: e.tensor_tensor(out=MM[par][h][:], in0=p[:], in1=mask4[d][:], op=ALU.mult), reads=[p, mask4[d]], writes=[MM[par][h]])

                def AB2(par, h):
                    q = PRf[h % 2]
                    k.op("pe", lambda e: e.matmul(q[:, 0:128], lhsT=ART[:, h, :], rhs=BTT[:, h, :], start=True, stop=True), reads=[ART, BTT], writes=[q])
                    k.op("dve", lambda e: e.tensor_tensor(out=Z[h][0][:, 128:256], in0=q[:, 0:128], in1=maskl[d][:], op=ALU.mult), reads=[q, maskl[d]], writes=[Z[h][0]])
                    k.op("pool", lambda e: e.tensor_copy(out=Z[h][0][:, 0:128], in_=MM[par][h][:, 0:128]), reads=[MM[par][h]], writes=[Z[h][0]])

                def AB3(par, h):
                    q = PRf[h % 2]
                    k.op("pe", lambda e: e.matmul(q[:, 128:192], lhsT=MM[par][h][:, 128:256], rhs=vb[par][:, hsl(h)], start=True, stop=True), reads=[MM[par][h], vb[par]], writes=[q])
                    k.op("act", lambda e: e.copy(out=Z[h][0][:, 320:384], in_=q[:, 128:192]), reads=[q], writes=[Z[h][0]])
                    k.op("pool", lambda e: e.tensor_copy(out=Z[h][0][:, 256:320], in_=XA[:, h, :]), reads=[XA], writes=[Z[h][0]])

                def stageAB(par, h):
                    AB1(par, h)
                    AB2(par, h)
                    AB3(par, h)

                def stageC_step(jj, th=None):
                    a_, b_ = jj % 2, (jj + 1) % 2
                    for h in range(H):
                        if th and (jj * H + h) % 3 == 2:
                            th.pop(0)()
                        q = ring4[cnt["c"] % len(ring4)]
                        cnt["c"] += 1
                        za, zb = Z[h][a_], Z[h][b_]
                        if jj < 5:
                            k.op("pe", lambda e, q=q, za=za: e.matmul(q[:, 128:384], lhsT=za[:, 0:128], rhs=za[:, 128:384], start=True, stop=True), reads=[za], writes=[q], inc=False)
                            k.op("pe", lambda e, q=q, za=za: e.matmul(q[:, 0:128], lhsT=za[:, 128:256], rhs=za[:, 0:128], start=True, stop=True), reads=[za], writes=[q])
                        elif jj == 5:
                            k.op("pe", lambda e, q=q, za=za: e.matmul(q[:, 256:384], lhsT=za[:, 0:128], rhs=za[:, 256:384], start=True, stop=True), reads=[za], writes=[q], inc=False)
                            k.op("pe", lambda e, q=q, za=za: e.matmul(q[:, 0:128], lhsT=za[:, 128:256], rhs=za[:, 0:128], start=True, stop=True), reads=[za], writes=[q])
                        else:
                            k.op("pe", lambda e, q=q, za=za: e.matmul(q[:, 256:384], lhsT=za[:, 0:128], rhs=za[:, 256:384], start=True, stop=True), reads=[za], writes=[q])
                        k.op("dve", lambda e, q=q, za=za, zb=zb: e.tensor_tensor(out=zb[:, 256:384], in0=q[:, 256:384], in1=za[:, 256:384], op=ALU.add), reads=[q, za], writes=[zb])
                        if jj < 5:
                            k.op("act", lambda e, q=q, zb=zb: e.copy(out=zb[:, 0:256], in_=q[:, 0:256]), reads=[q], writes=[zb])
                        elif jj == 5:
                            k.op("act", lambda e, q=q, zb=zb: e.copy(out=zb[:, 0:128], in_=q[:, 0:128]), reads=[q], writes=[zb])

                def E1(par, h):
                    q = PP[h % 2]
                    k.op("pe", lambda e: e.matmul(q[:, 0:64], lhsT=ident_b[:], rhs=Z[h][1][:, 320:384], start=True, stop=False), reads=[ident_b, Z[h][1]], writes=[q], inc=False)
                    k.op("pe", lambda e: e.matmul(q[:, 0:64], lhsT=WTg[h // 8][:, h % 8, :], rhs=Ab[h][:], start=False, stop=True), reads=[WTg[h // 8], Ab[h]], writes=[q])
                    k.op("act", lambda e: e.copy(out=U[h][:], in_=q[:, 0:64]), reads=[q], writes=[U[h]])

                def E2(par, h):
                    k.op("pe", lambda e: e.matmul(T0[:, hsl(h)], lhsT=RTT[par][:, h, :], rhs=Ab[h][:], start=True, stop=False), reads=[RTT[par], Ab[h]], writes=[psy], inc=False)
                    k.op("pe", lambda e: e.matmul(T0[:, hsl(h)], lhsT=MM[par][h][:, 256:384], rhs=U[h][:], start=False, stop=False), reads=[MM[par][h], U[h]], writes=[psy], inc=False)
                    k.op("pe", lambda e: e.matmul(T0[:, hsl(h)], lhsT=MM[par][h][:, 384:512], rhs=vb[par][:, hsl(h)], start=False, stop=True), reads=[MM[par][h], vb[par]], writes=[psy])
                    pa = PP[h % 2]
                    k.op("pe", lambda e: e.matmul(pa[0:64, 64:128], lhsT=BP[par][:, hsl(h)], rhs=U[h][:], start=True, stop=False), reads=[BP[par], U[h]], writes=[pa], inc=False)
                    k.op("pe", lambda e: e.matmul(pa[0:64, 64:128], lhsT=KP[par][:, hsl(h)], rhs=vb[par][:, hsl(h)], start=False, stop=True), reads=[KP[par], vb[par]], writes=[pa])
                    k.op("dve", lambda e: e.scalar_tensor_tensor(out=A[h][:], in0=A[h][:], scalar=PC[par][:, h:h + 1], in1=pa[0:64, 64:128], op0=ALU.mult, op1=ALU.add), reads=[A[h], PC[par], pa], writes=[A[h]])
                    k.op("act", lambda e: e.copy(out=Ab[h][:], in_=A[h][:]), reads=[A[h]], writes=[Ab[h]])

                def stageD():
                    for g8 in range(2):
                        pr = PR[cnt["x"] % 2]
                        cnt["x"] += 1
                        for hh in range(8):
                            h = g8 * 8 + hh
                            k.op("pe", lambda e, hh=hh, h=h, pr=pr: e.transpose(out=pr[0:64, hh, :], in_=Z[h][1][:, 256:320], identity=ident_b[:]), reads=[Z[h][1], ident_b], writes=[pr], inc=(hh == 7))
                        k.op("act", lambda e, g8=g8, pr=pr: e.copy(out=WTg[g8][:], in_=pr[0:64, :, :]), reads=[pr], writes=[WTg[g8]])

                sched = []
                for si, (r0s, T, cnd) in enumerate(cfg.seqs):
                    n = T // 128
                    order = list(range(n)) if d == 0 else list(range(n - 1, -1, -1))
                    for oi, j in enumerate(order):
                        sched.append((r0s // 128 + j, si, oi == 0, oi == n - 1, cnd))
                load(sched[0][0], 0)
                if len(sched) > 1:
                    load(sched[1][0], 1)
                for f in front12(0, 0):
                    f()
                for h in range(H):
                    stageAB(0, h)
                for ci, (t, si, first, last, cnd) in enumerate(sched):
                    trickle(2)
                    par = ci % 2
                    nxt = sched[ci + 1] if ci + 1 < len(sched) else None
                    if first:
                        if cnd == 0:
                            for h in range(H):
                                k.op("pool", lambda e, h=h: e.memset(A[h][:], 0.0), writes=[A[h]])
                                k.op("pool", lambda e, h=h: e.memset(Ab[h][:], 0.0), writes=[Ab[h]])
                        else:
                            k.dma("sp", lambda e: e.dma_start(out=Sio[:], in_=st_in[i, d].rearrange("h v k -> v h k")), writes=[Sio])
                            for h in range(H):
                                p = PR[h % 2]
                                pf = Buf(p.t.bitcast(F32), "prf", banks=p.banks) if False else None
                                q = PQ[h % 2]
                                k.op("pe", lambda e, h=h, q=q: e.transpose(out=q[0:64, 256:320], in_=Sio[:, h, :], identity=ident_f[0:64, 0:64]), reads=[Sio, ident_f], writes=[q])
                                k.op("dve", lambda e, h=h, q=q: e.tensor_copy(out=A[h][:], in_=q[0:64, 256:320]), reads=[q], writes=[A[h]])
                                k.op("act", lambda e, h=h: e.copy(out=Ab[h][:], in_=A[h][:]), reads=[A[h]], writes=[Ab[h]])
                    th = []
                    if nxt is not None:
                        th = front12((ci + 1) % 2, (ci + 1) % 2)
                    for jj in range(7):
                        stageC_step(jj, th)
                    while th:
                        th.pop(0)()
                    if ci + 2 < len(sched):
                        load(sched[ci + 2][0], ci % 2)
                    stageD()
                    np_ = (ci + 1) % 2
                    E1(par, 0)
                    if nxt is not None:
                        AB1(np_, 0)
                    for h in range(H):
                        if h + 1 < H:
                            E1(par, h + 1)
                        if nxt is not None:
                            AB2(np_, h)
                        E2(par, h)
                        if nxt is not None:
                            if h + 1 < H:
                                AB1(np_, h + 1)
                            AB3(np_, h)
                    k.op("act", lambda e: e.copy(out=yout[:], in_=T0[:]), reads=[psy], writes=[yout])
                    k.dma("sp", lambda e, t=t: e.dma_start(out=SC[ydst][rows(t), :], in_=yout[:]), reads=[yout], writes=[k.dbuf((ydst, t))])
                    if last and cnd == 0:
                        for h in range(H):
                            q = PQ[h % 2]
                            k.op("pe", lambda e, h=h, q=q: e.transpose(out=q[0:64, 256:320], in_=A[h][:], identity=ident_f[0:64, 0:64]), reads=[A[h], ident_f], writes=[q])
                            k.op("dve", lambda e, h=h, q=q: e.tensor_copy(out=Sio[:, h, :], in_=q[0:64, 256:320]), reads=[q], writes=[Sio])
                        k.dma("sp", lambda e, si=si: e.dma_start(out=st_out[si, i, d].rearrange("h v k -> v h k"), in_=Sio[:]), reads=[Sio], writes=[k.dbuf(("st_out", si, i, d))])
            k.barrier()

        def rwkv_epilogue(l):
            i = l // 2
            rows = lambda t: slice(t * 128, (t + 1) * 128)
            with ExitStack() as st:
                wo = k.sb(st, "wo", [128, NCH, D], BF16)
                k.dma("sp", lambda e: e.dma_start(out=wo[:], in_=wb["rwkv_w_o"][i * D:(i + 1) * D, :].rearrange("(c p) f -> p c f", p=128)), reads=wread("rwkv_w_o"), writes=[wo])
                mods = load_mod(st, l, [2])
                lnw = bcast_load(st, "lnw", rw_vec["rwkv_ln_w"][i:i + 1, :])
                lnb = bcast_load(st, "lnb", rw_vec["rwkv_ln_b"][i:i + 1, :])
                NB_ = 4
                yf = [k.sb(st, "e_yf%d" % j, [128, D], F32) for j in range(NB_)]
                yb = [k.sb(st, "e_yb%d" % j, [128, D], F32) for j in range(NB_)]
                gb = [k.sb(st, "e_g%d" % j, [128, D], BF16) for j in range(NB_)]
                xb = [k.sb(st, "e_x%d" % j, [128, D], F32) for j in range(NB_)]
                vv = [k.sb(st, "e_v%d" % j, [128, D], BF16) for j in range(NB_)]
                cen = [k.sb(st, "e_c%d" % j, [128, D], F32) for j in range(NB_)]
                sq = [k.sb(st, "e_s%d" % j, [128, D], F32) for j in range(NB_)]
                ob16 = [k.sb(st, "e_ob%d" % j, [128, D], BF16) for j in range(NB_)]
                oT = [k.sb(st, "e_oT%d" % j, [128, NCH, 128], BF16) for j in range(NB_)]
                cf = [k.sb(st, "e_cf%d" % j, [128, H], F32) for j in range(NB_)]
                st1 = [k.sb(st, "e_st%d" % j, [128, H], F32) for j in range(NB_)]
                ptr = [k.ps(st, "e_ptr%d" % j, [128, NCH, 128], BF16) for j in range(2)]
                po = [k.ps(st, "e_po%d" % j, [128, D]) for j in range(2)]
                v3 = lambda b: b[:].rearrange("p (h c) -> p h c", h=H)
                bc = lambda b: b[:].unsqueeze(2).to_broadcast([128, H, 64])
                def epiA(t):
                    j = t % NB_
                    yf_, yb_, g_, x_, v_, c_, s_, o_, oT_, cf_, s1, pt, pp = yf[j], yb[j], gb[j], xb[j], vv[j], cen[j], sq[j], ob16[j], oT[j], cf[j], st1[j], ptr[t % 2], po[t % 2]
                    for (dst, nm) in ((yf_, "YF"), (yb_, "YB"), (g_, "G"), (v_, "V")):
                        k.dma("sp", lambda e, dst=dst, nm=nm, t=t: e.dma_start(out=dst[:], in_=SC[nm][rows(t), :]), reads=[k.dbuf((nm, t))], writes=[dst])
                    k.dma("sp", lambda e, x_=x_, t=t: e.dma_start(out=x_[:], in_=X[rows(t), :]), reads=[k.dbuf(("X", t))], writes=[x_])
                    k.dma("sp", lambda e, cf_=cf_, t=t: e.dma_start(out=cf_[:], in_=SC["COEF"][rows(t), :]), reads=[k.dbuf(("COEF", t))], writes=[cf_])
                    k.op("dve", lambda e, yf_=yf_, yb_=yb_: e.tensor_tensor(out=yf_[:], in0=yf_[:], in1=yb_[:], op=ALU.add), reads=[yf_, yb_], writes=[yf_])
                    k.op("dve", lambda e, yf_=yf_, s1=s1: e.tensor_reduce(out=s1[:], in_=v3(yf_), axis=AX.X, op=ALU.add), reads=[yf_], writes=[s1])
                    k.op("dve", lambda e, s1=s1: e.tensor_scalar(out=s1[:], in0=s1[:], scalar1=-1.0 / 64, scalar2=0.0, op0=ALU.mult, op1=ALU.add), reads=[s1], writes=[s1])
                    k.op("dve", lambda e, c_=c_, yf_=yf_, s1=s1: e.tensor_tensor(out=v3(c_), in0=v3(yf_), in1=bc(s1), op=ALU.add), reads=[yf_, s1], writes=[c_])
                    k.op("pool", lambda e, c_=c_, s_=s_: e.tensor_tensor(out=s_[:], in0=c_[:], in1=c_[:], op=ALU.mult), reads=[c_], writes=[s_])
                    k.op("dve", lambda e, s_=s_, s1=s1: e.tensor_reduce(out=s1[:], in_=v3(s_), axis=AX.X, op=ALU.add), reads=[s_], writes=[s1])
                    k.op("act", lambda e, s1=s1: e.activation(out=s1[:], in_=s1[:], func=AF.Ln, scale=1.0 / 64, bias=64e-5), reads=[s1], writes=[s1])
                    k.op("act", lambda e, s1=s1: e.activation(out=s1[:], in_=s1[:], func=AF.Exp, scale=-0.5), reads=[s1], writes=[s1])
                    k.op("dve", lambda e, c_=c_, s1=s1: e.tensor_tensor(out=v3(c_), in0=v3(c_), in1=bc(s1), op=ALU.mult), reads=[c_, s1], writes=[c_])
                    k.op("pool", lambda e, c_=c_: e.tensor_tensor(out=c_[:], in0=c_[:], in1=lnw[:], op=ALU.mult), reads=[c_, lnw], writes=[c_])
                    k.op("pool", lambda e, c_=c_: e.tensor_tensor(out=c_[:], in0=c_[:], in1=lnb[:], op=ALU.add), reads=[c_, lnb], writes=[c_])
                    k.op("dve", lambda e, s_=s_, v_=v_, cf_=cf_: e.tensor_tensor(out=v3(s_), in0=v3(v_), in1=bc(cf_), op=ALU.mult), reads=[v_, cf_], writes=[s_])
                    k.op("dve", lambda e, c_=c_, s_=s_: e.tensor_tensor(out=c_[:], in0=c_[:], in1=s_[:], op=ALU.add), reads=[c_, s_], writes=[c_])
                    k.op("dve", lambda e, o_=o_, c_=c_, g_=g_: e.tensor_tensor(out=o_[:], in0=c_[:], in1=g_[:], op=ALU.mult), reads=[c_, g_], writes=[o_])

                def epiB(t):
                    cnd = tile_cond(t)
                    GT = mods[(cnd, 2)]
                    j = t % NB_
                    yf_, yb_, g_, x_, v_, c_, s_, o_, oT_, cf_, s1, pt, pp = yf[j], yb[j], gb[j], xb[j], vv[j], cen[j], sq[j], ob16[j], oT[j], cf[j], st1[j], ptr[t % 2], po[t % 2]
                    for kc in range(NCH):
                        k.op("pe", lambda e, kc=kc, pt=pt, o_=o_: e.transpose(out=pt[:, kc, :], in_=o_[:, kc * 128:(kc + 1) * 128], identity=ident_b[:]), reads=[o_, ident_b], writes=[pt], inc=(kc == NCH - 1))
                    k.op("act", lambda e, pt=pt, oT_=oT_: e.copy(out=oT_[:], in_=pt[:]), reads=[pt], writes=[oT_])
                    for half in range(2):
                        for kc in range(NCH):
                            k.op("pe", lambda e, half=half, kc=kc, pp=pp, oT_=oT_: e.matmul(pp[:, half * 512:(half + 1) * 512], lhsT=oT_[:, kc, :], rhs=wo[:, kc, half * 512:(half + 1) * 512], start=(kc == 0), stop=(kc == NCH - 1)),
                                 reads=[oT_, wo], writes=[pp], inc=(kc == NCH - 1))
                    k.op("dve", lambda e, pp=pp, yb_=yb_, GT=GT: e.tensor_tensor(out=yb_[:], in0=pp[:], in1=GT[:], op=ALU.mult), reads=[pp, GT], writes=[yb_])
                    k.op("pool", lambda e, x_=x_, yb_=yb_: e.tensor_tensor(out=x_[:], in0=x_[:], in1=yb_[:], op=ALU.add), reads=[x_, yb_], writes=[x_])
                    k.dma("sp", lambda e, x_=x_, t=t: e.dma_start(out=X[rows(t), :], in_=x_[:]), reads=[x_], writes=[k.dbuf(("X", t))])

                NTe = ROWS // 128
                pairs = [list(range(t, min(t + 2, NTe))) for t in range(0, NTe, 2)]

                def emitA(pr):
                    k.play_interleaved([k.capture(lambda t=t: epiA(t)) for t in pr])
                emitA(pairs[0])
                for pi, pr in enumerate(pairs):
                    if pi + 1 < len(pairs):
                        emitA(pairs[pi + 1])
                    for t in pr:
                        epiB(t)
            k.barrier()

        def rwkv_layer(l):
            i = l // 2
            rows = lambda t: slice(t * 128, (t + 1) * 128)
            with ExitStack() as st:
                mods = load_mod(st, l, [0, 1])
                make_gain(st, l, 0, mods, 1)
                xts = [k.sb(st, "r1x%d" % j, [128, D], F32) for j in range(3)]
                hs = [k.sb(st, "r1h%d" % j, [128, D], F32) for j in range(3)]
                junk = k.sb(st, "r1junk", [128, D], F32)
                sss = [k.sb(st, "r1ss%d" % j, [128, 1], F32) for j in range(3)]
                NT1 = ROWS // 128

                def r1load(t):
                    xt = xts[t % 3]
                    k.dma("sp", lambda e: e.dma_start(out=xt[:], in_=X[rows(t), :]), reads=[k.dbuf(("X", t))], writes=[xt])
                r1load(0)
                if NT1 > 1:
                    r1load(1)
                for t in range(NT1):
                    cnd = tile_cond(t)
                    xt, hb, ss = xts[t % 3], hs[t % 3], sss[t % 3]
                    norm_mod(xt, hb, mods[(cnd, 1)], mods[(cnd, 0)], ss, junk)
                    if t + 2 < NT1:
                        r1load(t + 2)
                    k.dma("sp", lambda e, hb=hb, t=t: e.dma_start(out=SC["Hh"][rows(t), :], in_=hb[:]), reads=[hb], writes=[k.dbuf(("Hh", t))])
            k.barrier()

            with ExitStack() as st:
                wr = [k.sb(st, "wrkv%d" % j, [128, NCH, D], BF16) for j in range(3)]
                for j in range(3):
                    k.dma("sp", lambda e, j=j: e.dma_start(out=wr[j][:], in_=wb["rwkv_w_rkv"][(i * 3 + j) * D:(i * 3 + j + 1) * D, :].rearrange("(c p) f -> p c f", p=128)),
                          reads=wread("rwkv_w_rkv"), writes=[wr[j]])
                w1 = [k.sb(st, "w1_%d" % d, [128, NCH, 64], BF16) for d in range(2)]
                a1 = [k.sb(st, "a1_%d" % d, [128, NCH, 64], BF16) for d in range(2)]
                w2 = [k.sb(st, "w2_%d" % d, [64, D], BF16) for d in range(2)]
                a2 = [k.sb(st, "a2_%d" % d, [64, D], BF16) for d in range(2)]
                g1 = k.sb(st, "g1", [128, NCH, 128], BF16)
                g2 = k.sb(st, "g2", [128, D], BF16)
                for d in range(2):
                    o = i * 2 + d
                    k.dma("sp", lambda e, d=d, o=o: e.dma_start(out=w1[d][:], in_=wb["rwkv_w1"][o * D:(o + 1) * D, :].rearrange("(c p) f -> p c f", p=128)), reads=wread("rwkv_w1"), writes=[w1[d]])
                    k.dma("sp", lambda e, d=d, o=o: e.dma_start(out=a1[d][:], in_=wb["rwkv_a1"][o * D:(o + 1) * D, :].rearrange("(c p) f -> p c f", p=128)), reads=wread("rwkv_a1"), writes=[a1[d]])
                    k.dma("sp", lambda e, d=d, o=o: e.dma_start(out=w2[d][:], in_=wb["rwkv_w2"][o * 64:(o + 1) * 64, :]), reads=wread("rwkv_w2"), writes=[w2[d]])
                    k.dma("sp", lambda e, d=d, o=o: e.dma_start(out=a2[d][:], in_=wb["rwkv_a2"][o * 64:(o + 1) * 64, :]), reads=wread("rwkv_a2"), writes=[a2[d]])
                k.dma("sp", lambda e: e.dma_start(out=g1[:], in_=wb["rwkv_g1"][i * D:(i + 1) * D, :].rearrange("(c p) f -> p c f", p=128)), reads=wread("rwkv_g1"), writes=[g1])
                k.dma("sp", lambda e: e.dma_start(out=g2[:], in_=wb["rwkv_g2"][i * 128:(i + 1) * 128, :]), reads=wread("rwkv_g2"), writes=[g2])
                mu = k.sb(st, "mu", [128, 6, NCH], F32)
                k.dma("sp", lambda e: e.dma_start(out=mu[:], in_=rw_mu[i].rearrange("m (c p) -> p m c", p=128), allow_slow_non_contiguous=True), writes=[mu])
                w0b = [bcast_load(st, "w0b%d" % d, rw_vec["rwkv_w0"][i, d:d + 1, :]) for d in range(2)]
                a0b = [bcast_load(st, "a0b%d" % d, rw_vec["rwkv_a0"][i, d:d + 1, :]) for d in range(2)]
                kkb = bcast_load(st, "kkb", rw_vec["rwkv_k_k"][i:i + 1, :])
                kab = bcast_load(st, "kab", rw_vec["rwkv_k_a"][i:i + 1, :])
                rkb = bcast_load(st, "rkb", rw_vec["rwkv_r_k"][i:i + 1, :])
                h_, hp, hn, dl = [k.sb(st, n, [128, D], F32) for n in ("h_", "hp", "hn", "dl")]
                hb16, db16 = [k.sb(st, n, [128, D], BF16) for n in ("hb16", "db16")]
                hT, dT = [k.sb(st, n, [128, NCH, 128], BF16) for n in ("hT", "dT")]
                xsT = [k.sb(st, "xsT%d" % m, [128, NCH, 128], BF16) for m in range(6)]
                th = [k.sb(st, "th%d" % d, [64, 128], BF16) for d in range(2)]
                la = [k.sb(st, "la%d" % d, [64, 128], BF16) for d in range(2)]
                sgT = k.sb(st, "sgT", [128, 128], BF16)
                F = {n: k.sb(st, "f_" + n, [128, D], F32) for n in ("rf", "kf", "vf", "kk", "kka", "rr", "zt0", "zt1", "at0", "at1", "t10", "t11", "ksum")}
                Fb = {n: k.sb(st, "fb_" + n, [128, D], BF16) for n in ("kd0", "kd1", "bd0", "bd1", "gf")}
                ssq = k.sb(st, "ssq", [128, H], F32)
                coef = k.sb(st, "coef", [128, H], F32)
                ptr = [k.ps(st, "r2ptr%d" % j, [128, NCH, 128], BF16) for j in range(2)]
                pbig = [k.ps(st, "r2pb%d" % j, [128, D]) for j in range(2)]
                psm = [k.ps(st, "r2ps%d" % j, [128, 128]) for j in range(2)]
                cb = [0]
                cs = [0]
                v3 = lambda b: b[:].rearrange("p (h c) -> p h c", h=H)
                tiles = seq_tiles()

                def A1a(t, si, j, n):
                    r0 = t * 128
                    k.dma("sp", lambda e: e.dma_start(out=h_[:], in_=SC["Hh"][rows(t), :]), reads=[k.dbuf(("Hh", t))], writes=[h_])
                    if j == 0:
                        k.dma("sp", lambda e: e.dma_start(out=hp[0:1, :], in_=zeros_in[0:1, :]), writes=[hp])
                        k.dma("sp", lambda e: e.dma_start(out=hp[1:128, :], in_=SC["Hh"][r0:r0 + 127, :]), reads=[k.dbuf(("Hh", t))], writes=[hp])
                    else:
                        k.dma("sp", lambda e: e.dma_start(out=hp[:], in_=SC["Hh"][r0 - 1:r0 + 127, :]), reads=[k.dbuf(("Hh", t)), k.dbuf(("Hh", t - 1))], writes=[hp])
                    if j == n - 1:
                        k.dma("sp", lambda e: e.dma_start(out=hn[127:128, :], in_=zeros_in[0:1, :]), writes=[hn])
                        k.dma("sp", lambda e: e.dma_start(out=hn[0:127, :], in_=SC["Hh"][r0 + 1:r0 + 128, :]), reads=[k.dbuf(("Hh", t))], writes=[hn])
                    else:
                        k.dma("sp", lambda e: e.dma_start(out=hn[:], in_=SC["Hh"][r0 + 1:r0 + 129, :]), reads=[k.dbuf(("Hh", t)), k.dbuf(("Hh", t + 1))], writes=[hn])
                    k.op("pool", lambda e: e.tensor_tensor(out=hp[:], in0=hp[:], in1=hn[:], op=ALU.add), reads=[hp, hn], writes=[hp])
                    k.op("dve", lambda e: e.scalar_tensor_tensor(out=dl[:], in0=hp[:], scalar=0.5, in1=h_[:], op0=ALU.mult, op1=ALU.subtract), reads=[hp, h_], writes=[dl])
                    k.op("act", lambda e: e.copy(out=hb16[:], in_=h_[:]), reads=[h_], writes=[hb16])
                    k.op("act", lambda e: e.copy(out=db16[:], in_=dl[:]), reads=[dl], writes=[db16])
                    for (src, dst) in ((hb16, hT), (db16, dT)):
                        p = ptr[cb[0] % 2]
                        cb[0] += 1
                        for kc in range(NCH):
                            k.op("pe", lambda e, p=p, src=src, kc=kc: e.transpose(out=p[:, kc, :], in_=src[:, kc * 128:(kc + 1) * 128], identity=ident_b[:]),
                                 reads=[src, ident_b], writes=[p], inc=(kc == NCH - 1))
                        k.op("act", lambda e, p=p, dst=dst: e.copy(out=dst[:], in_=p[:]), reads=[p], writes=[dst])

                def A1b():
                    for m in range(6):
                        for kc in range(NCH):
                            k.op("dve", lambda e, m=m, kc=kc: e.scalar_tensor_tensor(out=xsT[m][:, kc, :], in0=dT[:, kc, :], scalar=mu[:, m, kc:kc + 1], in1=hT[:, kc, :], op0=ALU.mult, op1=ALU.add),
                                 reads=[dT, hT, mu], writes=[xsT[m]])

                def A23(t):
                    def proj(m, w, dstf):
                        p = pbig[cb[0] % 2]
                        cb[0] += 1
                        for half in range(2):
                            for kc in range(NCH):
                                k.op("pe", lambda e, p=p, half=half, kc=kc: e.matmul(p[:, half * 512:(half + 1) * 512], lhsT=xsT[m][:, kc, :], rhs=w[:, kc, half * 512:(half + 1) * 512], start=(kc == 0), stop=(kc == NCH - 1)),
                                     reads=[xsT[m], w], writes=[p], inc=(kc == NCH - 1))
                        k.op("act", lambda e, p=p: e.copy(out=dstf[:], in_=p[:]), reads=[p], writes=[dstf])
                    proj(0, wr[0], F["rf"])
                    proj(1, wr[1], F["kf"])
                    proj(2, wr[2], F["vf"])

                    def lora1(wl, m, dst, func, npart):
                        p = psm[cs[0] % 2]
                        cs[0] += 1
                        for kc in range(NCH):
                            k.op("pe", lambda e, p=p, kc=kc: e.matmul(p[0:npart, :], lhsT=wl[:, kc, :], rhs=xsT[m][:, kc, :], start=(kc == 0), stop=(kc == NCH - 1)),
                                 reads=[wl, xsT[m]], writes=[p], inc=(kc == NCH - 1))
                        k.op("act", lambda e, p=p: e.activation(out=dst[:], in_=p[0:npart, :], func=func), reads=[p], writes=[dst])
                    for d in range(2):
                        lora1(w1[d], 3, th[d], AF.Tanh, 64)
                        lora1(a1[d], 4, la[d], AF.Copy, 64)
                    lora1(g1, 5, sgT, AF.Sigmoid, 128)

                def B(t):
                    k.dma("pool", lambda e: e.dma_start(out=SC["R"][rows(t), :], in_=F["rf"][:]), reads=[F["rf"]], writes=[k.dbuf(("R", t))])
                    k.dma("pool", lambda e: e.dma_start(out=SC["V"][rows(t), :], in_=F["vf"][:]), reads=[F["vf"]], writes=[k.dbuf(("V", t))])

                    def lora2(lhs, w, p):
                        for half in range(2):
                            k.op("pe", lambda e, half=half: e.matmul(p[:, half * 512:(half + 1) * 512], lhsT=lhs[:], rhs=w[:, half * 512:(half + 1) * 512], start=True, stop=True),
                                 reads=[lhs, w], writes=[p])
                    sqb = F["t10"]
                    k.op("dve", lambda e: e.tensor_tensor(out=F["kk"][:], in0=F["kf"][:], in1=kkb[:], op=ALU.mult), reads=[F["kf"], kkb], writes=[F["kk"]])
                    k.op("pool", lambda e: e.tensor_tensor(out=sqb[:], in0=F["kk"][:], in1=F["kk"][:], op=ALU.mult), reads=[F["kk"]], writes=[sqb])
                    k.op("dve", lambda e: e.tensor_reduce(out=ssq[:], in_=v3(sqb), axis=AX.X, op=ALU.add), reads=[sqb], writes=[ssq])
                    k.op("act", lambda e: e.activation(out=ssq[:], in_=ssq[:], func=AF.Ln, bias=1e-12), reads=[ssq], writes=[ssq])
                    k.op("act", lambda e: e.activation(out=ssq[:], in_=ssq[:], func=AF.Exp, scale=-0.5), reads=[ssq], writes=[ssq])
                    k.op("dve", lambda e: e.tensor_tensor(out=v3(F["kk"]), in0=v3(F["kk"]), in1=ssq[:].unsqueeze(2).to_broadcast([128, H, 64]), op=ALU.mult), reads=[F["kk"], ssq], writes=[F["kk"]])
                    k.dma("pool", lambda e: e.dma_start(out=SC["KK"][rows(t), :], in_=F["kk"][:]), reads=[F["kk"]], writes=[k.dbuf(("KK", t))])
                    k.op("pool", lambda e: e.tensor_tensor(out=F["kka"][:], in0=F["kf"][:], in1=kab[:], op=ALU.mult), reads=[F["kf"], kab], writes=[F["kka"]])
                    k.op("pool", lambda e: e.tensor_tensor(out=F["rr"][:], in0=F["rf"][:], in1=rkb[:], op=ALU.mult), reads=[F["rf"], rkb], writes=[F["rr"]])
                    for d in range(2):
                        zt, at, t1, kd, bd = F["zt%d" % d], F["at%d" % d], F["t1%d" % d], Fb["kd%d" % d], Fb["bd%d" % d]
                        p = pbig[cb[0] % 2]
                        cb[0] += 1
                        lora2(th[d], w2[d], p)
                        k.op("dve", lambda e, p=p, d=d, zt=zt: e.tensor_tensor(out=zt[:], in0=p[:], in1=w0b[d][:], op=ALU.add), reads=[p, w0b[d]], writes=[zt])
                        k.op("act", lambda e, zt=zt: e.activation(out=zt[:], in_=zt[:], func=AF.Sigmoid), reads=[zt], writes=[zt])
                        k.dma("sp", lambda e, d=d, zt=zt: e.dma_start(out=SC["LW%d" % d][rows(t), :], in_=zt[:]), reads=[zt], writes=[k.dbuf(("LW%d" % d, t))])
                        p = pbig[cb[0] % 2]
                        cb[0] += 1
                        lora2(la[d], a2[d], p)
                        k.op("dve", lambda e, p=p, d=d, at=at: e.tensor_tensor(out=at[:], in0=p[:], in1=a0b[d][:], op=ALU.add), reads=[p, a0b[d]], writes=[at])
                        k.op("act", lambda e, at=at: e.activation(out=at[:], in_=at[:], func=AF.Sigmoid), reads=[at], writes=[at])
                        k.op("dve", lambda e, at=at, t1=t1: e.scalar_tensor_tensor(out=t1[:], in0=at[:], scalar=-1.0, in1=F["kka"][:], op0=ALU.add, op1=ALU.mult), reads=[at, F["kka"]], writes=[t1])
                        k.op("pool", lambda e, t1=t1, kd=kd: e.tensor_tensor(out=kd[:], in0=t1[:], in1=F["kf"][:], op=ALU.add), reads=[t1, F["kf"]], writes=[kd])
                        k.op("pool", lambda e, at=at, bd=bd: e.tensor_tensor(out=bd[:], in0=F["kk"][:], in1=at[:], op=ALU.mult), reads=[F["kk"], at], writes=[bd])
                        k.dma("sp", lambda e, d=d, kd=kd: e.dma_start(out=SC["K%d" % d][rows(t), :], in_=kd[:]), reads=[kd], writes=[k.dbuf(("K%d" % d, t))])
                        k.dma("sp", lambda e, d=d, bd=bd: e.dma_start(out=SC["B%d" % d][rows(t), :], in_=bd[:]), reads=[bd], writes=[k.dbuf(("B%d" % d, t))])
                    k.op("pool", lambda e: e.tensor_tensor(out=F["ksum"][:], in0=Fb["kd0"][:], in1=Fb["kd1"][:], op=ALU.add), reads=[Fb["kd0"], Fb["kd1"]], writes=[F["ksum"]])
                    k.op("dve", lambda e: e.tensor_tensor(out=F["t10"][:], in0=F["rr"][:], in1=F["ksum"][:], op=ALU.mult), reads=[F["rr"], F["ksum"]], writes=[F["t10"]])
                    k.op("dve", lambda e: e.tensor_reduce(out=coef[:], in_=v3(F["t10"]), axis=AX.X, op=ALU.add), reads=[F["t10"]], writes=[coef])
                    k.dma("sp", lambda e: e.dma_start(out=SC["COEF"][rows(t), :], in_=coef[:]), reads=[coef], writes=[k.dbuf(("COEF", t))])
                    p = pbig[cb[0] % 2]
                    cb[0] += 1
                    lora2(sgT, g2, p)
                    k.op("act", lambda e, p=p: e.copy(out=Fb["gf"][:], in_=p[:]), reads=[p], writes=[Fb["gf"]])
                    k.dma("sp", lambda e: e.dma_start(out=SC["G"][rows(t), :], in_=Fb["gf"][:]), reads=[Fb["gf"]], writes=[k.dbuf(("G", t))])

                A1a(*tiles[0])
                A1b()
                for ti, (t, si, j, n) in enumerate(tiles):
                    trickle(1)
                    A23(t)
                    if ti + 1 < len(tiles):
                        A1a(*tiles[ti + 1])
                    B(t)
                    if ti + 1 < len(tiles):
                        A1b()
            k.barrier()
            if "scan" in cfg.phases:
                for d in range(2):
                    scan_pass(l, d)
                rwkv_epilogue(l)

        def na_layer(l):
            drain_casts()
            i = l // 2
            rows = lambda t: slice(t * 128, (t + 1) * 128)
            hsl = lambda h: slice(h * 64, (h + 1) * 64)
            NT = ROWS // 128
            S0 = NP * TP
            with ExitStack() as st:
                Ecol = k.sb(st, "Ecol", [31, 4096], F32)
                CM = k.sb(st, "CM", [15, 4096], F32)
                J15 = k.sb(st, "J15", [15, 15], F32)
                k.dma("sp", lambda e: e.dma_start(out=Ecol[:], in_=ecol_in[:, :]), writes=[Ecol])
                k.dma("sp", lambda e: e.dma_start(out=CM[:], in_=colmask_in[0:1, :].broadcast_to([15, 4096])), writes=[CM])
                k.dma("sp", lambda e: e.dma_start(out=J15[:], in_=j15_in[:, :]), writes=[J15])
                rp = [k.sb(st, "rp%d" % j, [15, 31], F32) for j in range(2)]
                rrT = [k.sb(st, "rrT%d" % j, [31, 15], BF16) for j in range(2)]
                Ecb = k.sb(st, "Ecb", [31, 4096], BF16)
                k.op("act", lambda e: e.copy(out=Ecb[:], in_=Ecol[:]), reads=[Ecol], writes=[Ecb])
                t1 = [k.sb(st, "t1r%d" % j, [15, 4096], F32) for j in range(2)]
                pr_ = [k.ps(st, "n0p%d" % j, [31, 15]) for j in range(2)]
                pt_ = [k.ps(st, "n0t%d" % j, [15, 512]) for j in range(4)]
                c = 0
                for h in range(H):
                    a, b_, t_ = rp[h % 2], rrT[h % 2], t1[h % 2]
                    k.dma("sp", lambda e, a=a, h=h: e.dma_start(out=a[:], in_=rpb_in[i, h]), writes=[a])
                    p = pr_[h % 2]
                    k.op("pe", lambda e, p=p, a=a: e.matmul(p[:], lhsT=a[:], rhs=J15[:], start=True, stop=True), reads=[a, J15], writes=[p])
                    k.op("act", lambda e, p=p, b_=b_: e.copy(out=b_[:], in_=p[:]), reads=[p], writes=[b_])
                    for cb in range(8):
                        q = pt_[c % 4]
                        c += 1
                        k.op("pe", lambda e, q=q, b_=b_, cb=cb: e.matmul(q[:], lhsT=b_[:], rhs=Ecb[:, cb * 512:(cb + 1) * 512], start=True, stop=True), reads=[b_, Ecb], writes=[q])
                        k.op("dve", lambda e, q=q, t_=t_, cb=cb: e.tensor_tensor(out=t_[:, cb * 512:(cb + 1) * 512], in0=q[:], in1=CM[:, cb * 512:(cb + 1) * 512], op=ALU.add), reads=[q, CM], writes=[t_])
                    k.dma("sp", lambda e, t_=t_, h=h: e.dma_start(out=T1R[h], in_=t_[:]), reads=[t_], writes=[k.dbuf(("T1R", h))])
            k.barrier()

            if cfg.na_stop < 1:
                return
            with ExitStack() as st:
                mods = load_mod(st, l, [0, 1])
                make_gain(st, l, 0, mods, 1)
                wq = k.sb(st, "wqkv", [128, NCH, 3 * D], BF16)
                for j in range(3):
                    k.dma("sp", lambda e, j=j: e.dma_start(out=wq[:, :, j * D:(j + 1) * D], in_=wb["na_w_qkv"][i * D:(i + 1) * D, j * D:(j + 1) * D].rearrange("(c p) f -> p c f", p=128)),
                          reads=wread("na_w_qkv"), writes=[wq])
                qg = k.sb(st, "qg", [128, 64], F32)
                kg = k.sb(st, "kg", [128, 64], F32)
                k.dma("sp", lambda e: e.dma_start(out=qg[:], in_=qg_in[i:i + 1, :].broadcast_to([128, 64])), writes=[qg])
                k.dma("sp", lambda e: e.dma_start(out=kg[:], in_=kg_in[i:i + 1, :].broadcast_to([128, 64])), writes=[kg])
                k.op("dve", lambda e: e.tensor_scalar(out=qg[:], in0=qg[:], scalar1=0.125, scalar2=0.0, op0=ALU.mult, op1=ALU.add), reads=[qg], writes=[qg])
                xt = [k.sb(st, "n1x%d" % j, [128, D], F32) for j in range(2)]
                hb = [k.sb(st, "n1h%d" % j, [128, D], BF16) for j in range(2)]
                junk = k.sb(st, "n1junk", [128, D], F32)
                ss = [k.sb(st, "n1ss%d" % j, [128, 1], F32) for j in range(2)]
                hT = [k.sb(st, "n1hT%d" % j, [128, NCH, 128], BF16) for j in range(2)]
                qf2 = [k.sb(st, "n1q%d" % j, [128, D], F32) for j in range(2)]
                kf2 = [k.sb(st, "n1k%d" % j, [128, D], F32) for j in range(2)]
                vf2 = [k.sb(st, "n1v%d" % j, [128, D], F32) for j in range(2)]
                sq = k.sb(st, "n1sq", [128, D], F32)
                qb, kb_, vb_ = [k.sb(st, n, [128, D], BF16) for n in ("n1qb", "n1kb", "n1vb")]
                qTs = k.sb(st, "n1qT", [64, H, 128], BF16)
                kTs = k.sb(st, "n1kT", [64, H, 128], BF16)
                st16 = [k.sb(st, "n1st%d" % j, [128, H], F32) for j in range(2)]
                ptr = k.ps(st, "n1ptr", [128, NCH, 128], BF16)
                pq = [k.ps(st, "n1pq%d" % j, [128, D]) for j in range(3)]
                ptr2 = k.ps(st, "n1ptr2", [64, 8, 128], BF16)
                v3 = lambda b: b[:].rearrange("p (h c) -> p h c", h=H)
                def n1L(t):
                    x_ = xt[t % 2]
                    k.dma("sp", lambda e, x_=x_, t=t: e.dma_start(out=x_[:], in_=X[rows(t), :]), reads=[k.dbuf(("X", t))], writes=[x_])

                def n1A(t):
                    cnd = tile_cond(t)
                    x_, h_, s_, hT_ = xt[t % 2], hb[t % 2], ss[t % 2], hT[t % 2]
                    qf, kf, vf = qf2[t % 2], kf2[t % 2], vf2[t % 2]
                    norm_mod(x_, h_, mods[(cnd, 1)], mods[(cnd, 0)], s_, junk)

                def n1Ape(t):
                    x_, h_, s_, hT_ = xt[t % 2], hb[t % 2], ss[t % 2], hT[t % 2]
                    qf, kf, vf = qf2[t % 2], kf2[t % 2], vf2[t % 2]
                    for kc in range(NCH):
                        k.op("pe", lambda e, h_=h_, kc=kc: e.transpose(out=ptr[:, kc, :], in_=h_[:, kc * 128:(kc + 1) * 128], identity=ident_b[:]), reads=[h_, ident_b], writes=[ptr], inc=(kc == NCH - 1))
                    k.op("act", lambda e, hT_=hT_: e.copy(out=hT_[:], in_=ptr[:]), reads=[ptr], writes=[hT_])
                    for j, dstf in enumerate((qf, kf, vf)):
                        p = pq[j]
                        for half in range(2):
                            for kc in range(NCH):
                                k.op("pe", lambda e, p=p, half=half, kc=kc, j=j, hT_=hT_: e.matmul(p[:, half * 512:(half + 1) * 512], lhsT=hT_[:, kc, :], rhs=wq[:, kc, j * D + half * 512:j * D + (half + 1) * 512], start=(kc == 0), stop=(kc == NCH - 1)),
                                     reads=[hT_, wq], writes=[p], inc=(kc == NCH - 1))

                def n1A3(t):
                    for j, dstf in enumerate((qf2[t % 2], kf2[t % 2], vf2[t % 2])):
                        p = pq[j]
                        k.op("act", lambda e, p=p, dstf=dstf: e.copy(out=dstf[:], in_=p[:]), reads=[p], writes=[dstf])

                def n1B(t):
                    cnd = tile_cond(t)
                    qf, kf, vf = qf2[t % 2], kf2[t % 2], vf2[t % 2]
                    for (src, gn, dstb, sti) in ((qf, qg, qb, 0), (kf, kg, kb_, 1)):
                        s16 = st16[sti]
                        k.op("pool", lambda e, src=src: e.tensor_tensor(out=sq[:], in0=src[:], in1=src[:], op=ALU.mult), reads=[src], writes=[sq])
                        k.op("dve", lambda e, s16=s16: e.tensor_reduce(out=s16[:], in_=v3(sq), axis=AX.X, op=ALU.add), reads=[sq], writes=[s16])
                        k.op("act", lambda e, s16=s16: e.activation(out=s16[:], in_=s16[:], func=AF.Ln, scale=1.0 / 64, bias=1e-6), reads=[s16], writes=[s16])
                        k.op("act", lambda e, s16=s16: e.activation(out=s16[:], in_=s16[:], func=AF.Exp, scale=-0.5), reads=[s16], writes=[s16])
                        k.op("dve", lambda e, src=src, s16=s16: e.tensor_tensor(out=v3(src), in0=v3(src), in1=s16[:].unsqueeze(2).to_broadcast([128, H, 64]), op=ALU.mult), reads=[src, s16], writes=[src])
                        k.op("dve", lambda e, src=src, gn=gn: e.tensor_tensor(out=v3(src), in0=v3(src), in1=gn[:].unsqueeze(1).to_broadcast([128, H, 64]), op=ALU.mult), reads=[src, gn], writes=[src])
                        k.op("act", lambda e, src=src, dstb=dstb: e.copy(out=dstb[:], in_=src[:]), reads=[src], writes=[dstb])
                    k.op("act", lambda e, vf=vf: e.copy(out=vb_[:], in_=vf[:]), reads=[vf], writes=[vb_])
                    k.dma("sp", lambda e, t=t: e.dma_start(out=VV[rows(t), :], in_=vb_[:]), reads=[vb_], writes=[k.dbuf(("VV", t))])

                def n1BT(t):
                    cnd = tile_cond(t)
                    qf, kf, vf = qf2[t % 2], kf2[t % 2], vf2[t % 2]
                    for (srcb, dsts, dname, DR) in ((qb, qTs, "QT", QT), (kb_, kTs, "KT", KT)):
                        for g8 in range(2):
                            for hh in range(8):
                                h = g8 * 8 + hh
                                k.op("pe", lambda e, hh=hh, h=h, srcb=srcb: e.transpose(out=ptr2[:, hh, :], in_=srcb[:, hsl(h)], identity=ident_b[:]), reads=[srcb, ident_b], writes=[ptr2], inc=(hh == 7))
                            k.op("act", lambda e, g8=g8, dsts=dsts: e.copy(out=dsts[:, g8 * 8:(g8 + 1) * 8, :], in_=ptr2[:]), reads=[ptr2], writes=[dsts])
                        k.dma("sp", lambda e, t=t, dsts=dsts, DR=DR: e.dma_start(out=DR[:, :, rows(t)].rearrange("h d t -> d h t"), in_=dsts[:]), reads=[dsts], writes=[k.dbuf((dname, t))])
                    if cnd == 0:
                        bi, t0 = (t * 128) // TP, (t * 128) % TP
                        k.dma("sp", lambda e, bi=bi, t0=t0, kf=kf: e.dma_start(out=nk_out[bi, i, :, t0:t0 + 128, :].rearrange("h t d -> t h d"), in_=v3(kf)), reads=[kf], writes=[k.dbuf(("nk", bi, i, t0))])
                        k.dma("sp", lambda e, bi=bi, t0=t0, vf=vf: e.dma_start(out=nv_out[bi, i, :, t0:t0 + 128, :].rearrange("h t d -> t h d"), in_=v3(vf)), reads=[vf], writes=[k.dbuf(("nv", bi, i, t0))])

                n1L(0)
                if NT > 1:
                    n1L(1)
                n1A(0)
                n1Ape(0)
                n1A3(0)
                for t in range(NT):
                    if t + 2 < NT:
                        n1L(t + 2)
                    if t + 1 < NT:
                        n1A(t + 1)
                        n1Ape(t + 1)
                    n1B(t)
                    n1BT(t)
                    if t + 1 < NT:
                        n1A3(t + 1)
            k.barrier()

            if cfg.na_stop < 2:
                return
            with ExitStack() as st:
                NB = TS // 128
                HS = []
                for j2 in range(2):
                    d_ = {}
                    d_["qTh"] = k.sb(st, "qTh%d" % j2, [64, TS], BF16)
                    d_["kTh"] = k.sb(st, "kTh%d" % j2, [64, TS], BF16)
                    d_["Vx"] = k.sb(st, "Vx%d" % j2, [128, NB, 65], BF16)
                    d_["qTp"] = k.sb(st, "qTp%d" % j2, [64, NP * TP], BF16)
                    d_["kTp"] = k.sb(st, "kTp%d" % j2, [64, NP * TP], BF16)
                    d_["Vxp"] = k.sb(st, "Vxp%d" % j2, [128, NP * TP // 128, 65], BF16)
                    d_["ck"] = k.sb(st, "ck%d" % j2, [128, 2, 64], F32)
                    d_["ckb"] = k.sb(st, "ckb%d" % j2, [128, 2, 64], BF16)
                    d_["cv"] = k.sb(st, "cv%d" % j2, [128, 2, 64], F32)
                    d_["KTc"] = k.sb(st, "KTc%d" % j2, [64, 256], BF16)
                    d_["Vcx"] = k.sb(st, "Vcx%d" % j2, [128, 2, 65], BF16)
                    d_["BB"] = [k.sb(st, "BB%d_%d" % (j2, v), [128, 16, 64], F32) for v in range(2)]
                    d_["Oh"] = k.sb(st, "Oh%d" % j2, [128, NB, 64], BF16)
                    d_["Ohp"] = k.sb(st, "Ohp%d" % j2, [128, NP * TP // 128, 64], BF16)
                    HS.append(d_)
                ND = 3
                PT = [k.sb(st, "PT%d" % j, [128, 1024], BF16) for j in range(ND)]
                tmpb = [k.sb(st, "tmpb%d" % j, [128, 640], F32) for j in range(ND)]
                rc = [k.sb(st, "rc%d" % j, [128, 1], F32) for j in range(ND)]
                SA = [k.ps(st, "nSA%d" % j, [128, 1024]) for j in range(ND)]
                PO = [k.ps(st, "nPO%d" % j, [128, 65]) for j in range(1)]
                ptc = k.ps(st, "nptc", [64, 2, 128], BF16)
                for d_ in HS:
                    for nm in ("Vx", "Vxp", "Vcx"):
                        k.op("pool", lambda e, b=d_[nm]: e.memset(b[:], 1.0), writes=[d_[nm]])
                uc = [0]

                pend = []

                def unit(qT_ap, kts, vxs, nbias, bias_ap, o_ap, rd, bbr=()):
                    bbr = list(bbr)
                    u = uc[0]
                    uc[0] += 1
                    S, P, tb, po, r_ = SA[u % ND], PT[u % ND], tmpb[u % ND], PO[0], rc[u % ND]
                    nb = len(kts)
                    for b in range(nb):
                        k.op("pe", lambda e, b=b: e.matmul(S[:, b * 128:(b + 1) * 128], lhsT=kts[b], rhs=qT_ap, start=True, stop=True), reads=rd[:-1], writes=[S], inc=(b == nb - 1))
                    if nbias:
                        k.op("dve", lambda e: e.tensor_tensor(out=tb[:, 0:nbias * 128], in0=S[:, 0:nbias * 128], in1=bias_ap, op=ALU.add), reads=[S] + bbr, writes=[tb])
                        k.op("act", lambda e: e.activation(out=P[:, 0:nbias * 128], in_=tb[:, 0:nbias * 128], func=AF.Exp), reads=[tb], writes=[P])
                    if nb > nbias:
                        k.op("act", lambda e: e.activation(out=P[:, nbias * 128:nb * 128], in_=S[:, nbias * 128:nb * 128], func=AF.Exp), reads=[S], writes=[P])

                    def partB():
                        for b in range(nb):
                            k.op("pe", lambda e, b=b: e.matmul(po[:], lhsT=P[:, b * 128:(b + 1) * 128], rhs=vxs[b], start=(b == 0), stop=(b == nb - 1)), reads=[P] + rd[:-1], writes=[po], inc=(b == nb - 1))
                        k.op("dve", lambda e: e.reciprocal(out=r_[:], in_=po[:, 64:65]), reads=[po], writes=[r_])
                        k.op("dve", lambda e: e.tensor_scalar(out=o_ap, in0=po[:, 0:64], scalar1=r_[:, 0:1], scalar2=0.0, op0=ALU.mult, op1=ALU.add), reads=[po, r_], writes=rd[-1:])
                    pend.append(partB)
                    if len(pend) > ND - 1:
                        pend.pop(0)()

                def flush():
                    while pend:
                        pend.pop(0)()

                def load_head(h):
                    d_ = HS[h % 2]
                    qTp, kTp, Vxp, qTh, kTh, Vx, ck, ckb, cv, KTc, Vcx, BB = (d_[n_] for n_ in ("qTp", "kTp", "Vxp", "qTh", "kTh", "Vx", "ck", "ckb", "cv", "KTc", "Vcx", "BB"))
                    k.dma("sp", lambda e: e.dma_start(out=qTp[:], in_=QT[h, :, 0:S0]), reads=[k.dbuf(("QT", t)) for t in range(S0 // 128)], writes=[qTp])
                    k.dma("sp", lambda e: e.dma_start(out=kTp[:], in_=KT[h, :, 0:S0]), reads=[k.dbuf(("KT", t)) for t in range(S0 // 128)], writes=[kTp])
                    k.dma("sp", lambda e: e.dma_start(out=Vxp[:, :, 0:64], in_=VV[0:S0, hsl(h)].rearrange("(b t) c -> t b c", t=128)), reads=[k.dbuf(("VV", t)) for t in range(S0 // 128)], writes=[Vxp])
                    k.dma("sp", lambda e: e.dma_start(out=qTh[:], in_=QT[h, :, S0:ROWS]), reads=[k.dbuf(("QT", t)) for t in range(S0 // 128, NT)], writes=[qTh])
                    k.dma("sp", lambda e: e.dma_start(out=kTh[:], in_=KT[h, :, S0:ROWS]), reads=[k.dbuf(("KT", t)) for t in range(S0 // 128, NT)], writes=[kTh])
                    k.dma("sp", lambda e: e.dma_start(out=Vx[:, :, 0:64], in_=VV[S0:ROWS, hsl(h)].rearrange("(b t) c -> t b c", t=128)), reads=[k.dbuf(("VV", t)) for t in range(S0 // 128, NT)], writes=[Vx])
                    k.dma("sp", lambda e: e.dma_start(out=ck[:], in_=ck_in[i, h].rearrange("(b t) d -> t b d", t=128)), writes=[ck])
                    k.dma("sp", lambda e: e.dma_start(out=cv[:], in_=cv_in[i, h].rearrange("(b t) d -> t b d", t=128)), writes=[cv])
                    k.op("act", lambda e: e.copy(out=ckb[:], in_=ck[:]), reads=[ck], writes=[ckb])
                    k.op("act", lambda e: e.copy(out=Vcx[:, :, 0:64], in_=cv[:]), reads=[cv], writes=[Vcx])
                    for b in range(2):
                        k.op("pe", lambda e, b=b: e.transpose(out=ptc[:, b, :], in_=ckb[:, b, :], identity=ident_b[:]), reads=[ckb, ident_b], writes=[ptc], inc=(b == 1))
                    k.op("act", lambda e: e.copy(out=KTc[:].rearrange("p (b t) -> p b t", b=2), in_=ptc[:]), reads=[ptc], writes=[KTc])
                    for v, (lo, hi) in enumerate(((4, 11), (0, 14))):
                        k.op("pool", lambda e, v=v: e.memset(BB[v][:], NEG), writes=[BB[v]])
                        for kr2 in range(2):
                            hi2 = min(hi, 14 - kr2)
                            k.dma("sp", lambda e, v=v, kr2=kr2, lo=lo, hi2=hi2: e.dma_start(
                                out=BB[v][kr2 * 64:(kr2 + 1) * 64, lo + 1 + kr2:hi2 + 2 + kr2, :],
                                in_=T1R[h, lo:hi2 + 1, :].rearrange("r (k q) -> k r q", k=64)), reads=[k.dbuf(("T1R", h))], writes=[BB[v]])

                def run_head(h):
                    d_ = HS[h % 2]
                    qTp, kTp, Vxp, qTh, kTh, Vx, KTc, Vcx, BB, Oh, Ohp = (d_[n_] for n_ in ("qTp", "kTp", "Vxp", "qTh", "kTh", "Vx", "KTc", "Vcx", "BB", "Oh", "Ohp"))
                    nbp = TP // 128
                    for sq_ in range(NP):
                        for qt in range(nbp):
                            tq = sq_ * nbp + qt
                            unit(qTp[:, tq * 128:(tq + 1) * 128],
                                 [kTp[:, (sq_ * nbp + b) * 128:(sq_ * nbp + b + 1) * 128] for b in range(nbp)],
                                 [Vxp[:, sq_ * nbp + b, :] for b in range(nbp)], 0, None, Ohp[:, tq, :], [qTp, kTp, Vxp, Ohp])
                    for p in range(NB):
                        if NB >= 5 and 2 <= p <= NB - 3:
                            kbs = [p + 2, p + 1, p, p - 1, p - 2]
                            v = 0
                        else:
                            base = 0 if p < 2 else NB - 4
                            kbs = [base + 3, base + 2, base + 1, base]
                            v = 1
                        j0 = 8 - 2 * (kbs[0] - p)
                        nbz = len(kbs)
                        bias_ap = BB[v][:, j0:j0 + 2 * nbz, :].rearrange("p j q -> p (j q)")
                        kts = [kTh[:, b * 128:(b + 1) * 128] for b in kbs] + [KTc[:, 0:128], KTc[:, 128:256]]
                        vxs = [Vx[:, b, :] for b in kbs] + [Vcx[:, 0, :], Vcx[:, 1, :]]
                        unit(qTh[:, p * 128:(p + 1) * 128], kts, vxs, nbz, bias_ap, Oh[:, p, :], [qTh, kTh, Vx, KTc, Vcx, Oh], bbr=BB)
                    flush()
                    k.dma("sp", lambda e: e.dma_start(out=OO[0:S0, hsl(h)].rearrange("(b t) c -> t b c", t=128), in_=Ohp[:]), reads=[Ohp], writes=[k.dbuf(("OO", "p", h))])
                    k.dma("sp", lambda e: e.dma_start(out=OO[S0:ROWS, hsl(h)].rearrange("(b t) c -> t b c", t=128), in_=Oh[:]), reads=[Oh], writes=[k.dbuf(("OO", "s", h))])

                load_head(0)
                for h in range(H):
                    if h + 1 < H:
                        load_head(h + 1)
                    run_head(h)
            k.barrier()

            if cfg.na_stop < 3:
                return
            with ExitStack() as st:
                mods = load_mod(st, l, [2])
                wo = k.sb(st, "nwo", [128, NCH, D], BF16)
                k.dma("sp", lambda e: e.dma_start(out=wo[:], in_=wb["na_w_o"][i * D:(i + 1) * D, :].rearrange("(c p) f -> p c f", p=128)), reads=wread("na_w_o"), writes=[wo])
                ob = [k.sb(st, "n3o%d" % j, [128, D], BF16) for j in range(2)]
                oT = [k.sb(st, "n3oT%d" % j, [128, NCH, 128], BF16) for j in range(2)]
                xb = [k.sb(st, "n3x%d" % j, [128, D], F32) for j in range(2)]
                yb = [k.sb(st, "n3y%d" % j, [128, D], F32) for j in range(2)]
                ptr3 = [k.ps(st, "n3ptr%d" % j, [128, NCH, 128], BF16) for j in range(2)]
                po = [k.ps(st, "n3po%d" % j, [128, D]) for j in range(2)]
                oread = [b for key, b in k.dram.items() if key[0] == "OO"]
                def n3A(t):
                    o_, x_ = ob[t % 2], xb[t % 2]
                    k.dma("sp", lambda e, o_=o_, t=t: e.dma_start(out=o_[:], in_=OO[rows(t), :]), reads=oread, writes=[o_])
                    k.dma("sp", lambda e, x_=x_, t=t: e.dma_start(out=x_[:], in_=X[rows(t), :]), reads=[k.dbuf(("X", t))], writes=[x_])

                def n3B(t):
                    cnd = tile_cond(t)
                    GT = mods[(cnd, 2)]
                    o_, oT_, x_, y_, pt, pp = ob[t % 2], oT[t % 2], xb[t % 2], yb[t % 2], ptr3[t % 2], po[t % 2]
                    for kc in range(NCH):
                        k.op("pe", lambda e, pt=pt, o_=o_, kc=kc: e.transpose(out=pt[:, kc, :], in_=o_[:, kc * 128:(kc + 1) * 128], identity=ident_b[:]), reads=[o_, ident_b], writes=[pt], inc=(kc == NCH - 1))
                    k.op("act", lambda e, pt=pt, oT_=oT_: e.copy(out=oT_[:], in_=pt[:]), reads=[pt], writes=[oT_])
                    for half in range(2):
                        for kc in range(NCH):
                            k.op("pe", lambda e, pp=pp, oT_=oT_, half=half, kc=kc: e.matmul(pp[:, half * 512:(half + 1) * 512], lhsT=oT_[:, kc, :], rhs=wo[:, kc, half * 512:(half + 1) * 512], start=(kc == 0), stop=(kc == NCH - 1)),
                                 reads=[oT_, wo], writes=[pp], inc=(kc == NCH - 1))
                    k.op("dve", lambda e, pp=pp, y_=y_, GT=GT: e.tensor_tensor(out=y_[:], in0=pp[:], in1=GT[:], op=ALU.mult), reads=[pp, GT], writes=[y_])
                    k.op("pool", lambda e, x_=x_, y_=y_: e.tensor_tensor(out=x_[:], in0=x_[:], in1=y_[:], op=ALU.add), reads=[x_, y_], writes=[x_])
                    k.dma("sp", lambda e, x_=x_, t=t: e.dma_start(out=X[rows(t), :], in_=x_[:]), reads=[x_], writes=[k.dbuf(("X", t))])

                n3A(0)
                for t in range(NT):
                    if t + 1 < NT:
                        n3A(t + 1)
                    n3B(t)
            k.barrier()

        for l in range(L):
            if l % 2 == 0 and "rwkv" in cfg.phases:
                rwkv_layer(l)
            if l % 2 == 1 and "na" in cfg.phases:
                na_layer(l)
            if "mlp" in cfg.phases:
                mlp_phase(l)

        with ExitStack() as st:
            xb_ = [k.sb(st, "ycp%d" % i, [128, D], F32) for i in range(3)]
            for t in range(ROWS // 128):
                b = xb_[t % 3]
                k.dma("sp", lambda e, b=b, t=t: e.dma_start(out=b[:], in_=X[t * 128:(t + 1) * 128, :]), reads=[k.dbuf(("X", t))], writes=[b])
                k.dma("sp", lambda e, b=b, t=t: e.dma_start(out=y_out[t * 128:(t + 1) * 128, :], in_=b[:]), reads=[b], writes=[k.dbuf(("Y", t))])
        k.finish()
        k.emit()
    return nc, k


def make_consts():
    c = -float(np.exp(-0.5))
    s_ = np.arange(128)[:, None]
    t_ = np.arange(128)[None, :]
    prec = [s_ < t_, s_ > t_]
    tric = np.stack([np.where(s_ <= t_, c, 0.0), np.where(s_ >= t_, c, 0.0)]).astype(np.float32)
    mask4 = np.stack([np.concatenate([p, p, p | (s_ == t_), p | (s_ == t_)], axis=1) for p in prec]).astype(np.float32)
    maskl = np.stack([p.T for p in prec]).astype(np.float32)
    kc = np.arange(64)[:, None]
    qc = np.arange(64)[None, :]
    ecol = np.stack([(kc - qc + 15 == co) for co in range(31)]).reshape(31, 4096).astype(np.float32)
    ws = np.clip(qc - 8, 0, 48)
    colmask = np.where((kc >= ws) & (kc < ws + 16), 0.0, NEG).reshape(1, 4096).astype(np.float32)
    j15 = np.eye(15, dtype=np.float32)[::-1].copy()
    return {"ident": np.eye(128, dtype=np.float32), "zeros": np.zeros((1, D), np.float32), "tric": tric,
            "allc": np.full((128, 128), c, np.float32), "mask4": mask4, "maskl": maskl,
            "ecol": ecol, "colmask": colmask, "j15": j15}


N_CORES = 8
_PHASES = ('rwkv', 'scan', 'na', 'mlp')


def kernel(x_prompt, x_sample, state_rwkv, cache_na_k, cache_na_v, c, c_ctx,
           norm_g, ada_w, ada_b, mlp_w1, mlp_w2,
           rwkv_mu, rwkv_w_rkv, rwkv_w_o, rwkv_w0, rwkv_w1, rwkv_w2, rwkv_a0, rwkv_a1, rwkv_a2,
           rwkv_g1, rwkv_g2, rwkv_k_k, rwkv_k_a, rwkv_r_k, rwkv_ln_w, rwkv_ln_b,
           na_w_qkv, na_w_o, na_q_g, na_k_g, na_rpb):
    f = lambda a: np.ascontiguousarray(np.asarray(a, dtype=np.float32))
    x_prompt, x_sample, state_rwkv, c, c_ctx = f(x_prompt), f(x_sample), f(state_rwkv), f(c), f(c_ctx)
    B, T, _ = x_prompt.shape
    BS, TS, _ = x_sample.shape
    L = norm_g.shape[0]
    n_p = B // N_CORES
    cfg = Cfg(tp=T, n_p=n_p, ts=TS, depth=L, phases=_PHASES)
    nc, k = build(cfg)
    NR, NN = cfg.n_rw, cfg.n_na
    shared = {
        "norm_g": f(norm_g), "ada_b": f(ada_b),
        "ada_w": f(ada_w).reshape(L * D, 6 * D), "mlp_w1": f(mlp_w1).reshape(L * D, DFF), "mlp_w2": f(mlp_w2).reshape(L * DFF, D),
        "rwkv_mu": f(rwkv_mu), "rwkv_w_rkv": f(rwkv_w_rkv).reshape(NR * 3 * D, D), "rwkv_w_o": f(rwkv_w_o).reshape(NR * D, D),
        "rwkv_w0": f(rwkv_w0), "rwkv_a0": f(rwkv_a0),
        "rwkv_w1": f(rwkv_w1).reshape(NR * 2 * D, 64), "rwkv_w2": f(rwkv_w2).reshape(NR * 2 * 64, D),
        "rwkv_a1": f(rwkv_a1).reshape(NR * 2 * D, 64), "rwkv_a2": f(rwkv_a2).reshape(NR * 2 * 64, D),
        "rwkv_g1": f(rwkv_g1).reshape(NR * D, 128), "rwkv_g2": f(rwkv_g2).reshape(NR * 128, D),
        "rwkv_k_k": f(rwkv_k_k), "rwkv_k_a": f(rwkv_k_a), "rwkv_r_k": f(rwkv_r_k).reshape(NR, D),
        "rwkv_ln_w": f(rwkv_ln_w), "rwkv_ln_b": f(rwkv_ln_b),
        "na_w_qkv": f(na_w_qkv).reshape(NN * D, 3 * D), "na_w_o": f(na_w_o).reshape(NN * D, D),
        "na_q_g": f(na_q_g), "na_k_g": f(na_k_g), "na_rpb": f(na_rpb),
    }
    cache_na_k, cache_na_v = f(cache_na_k), f(cache_na_v)
    shared.update(make_consts())
    in_maps = []
    for i in range(N_CORES):
        m = dict(shared)
        m["x_in"] = np.concatenate([x_prompt[i * n_p + j] for j in range(n_p)] + [x_sample[i]], axis=0)
        m["conds"] = np.stack([c_ctx, c[i]])
        m["state_in"] = state_rwkv[i]
        m["cache_k"] = cache_na_k[i]
        m["cache_v"] = cache_na_v[i]
        in_maps.append(m)
    res = run_bass_kernel_spmd(nc, in_maps, core_ids=list(range(N_CORES)))
    y_prompt = np.zeros((B, T, D), np.float32)
    y_sample = np.zeros((BS, TS, D), np.float32)
    new_state = np.zeros((B, NR, 2, H, 64, 64), np.float32)
    new_k = np.zeros((B, NN, H, T, 64), np.float32)
    new_v = np.zeros((B, NN, H, T, 64), np.float32)
    for i in range(N_CORES):
        r = res.results[i]
        y = r["y_out"]
        for j in range(n_p):
            y_prompt[i * n_p + j] = y[j * T:(j + 1) * T]
            new_state[i * n_p + j] = r["state_out"][j]
            new_k[i * n_p + j] = r["nk_out"][j]
            new_v[i * n_p + j] = r["nv_out"][j]
        y_sample[i] = y[n_p * T:]
    return (y_prompt, y_sample, new_state, new_k, new_v)
```

```python
from contextlib import ExitStack
import numpy as np
import ml_dtypes
import concourse.bass as bass
import concourse.mybir as mybir
from concourse.bass_utils import run_bass_kernel_spmd

F32 = mybir.dt.float32
BF16 = mybir.dt.bfloat16
AF = mybir.ActivationFunctionType
ALU = mybir.AluOpType
AX = mybir.AxisListType

D = 1024
DFF = 4096
NCH = 8
H = 16
HK = 64
NEG = -30000.0


class Buf:
    __slots__ = ("t", "w", "r", "name", "banks")

    def __init__(self, t, name="", banks=()):
        self.t = t
        self.w = {}
        self.r = {}
        self.name = name
        self.banks = banks

    def __getitem__(self, k):
        return self.t[k]


class Eng:
    def __init__(self, name, sem, is_dma_only=False):
        self.name = name
        self.sem = sem
        self.count = 0
        self.prog = []
        self.seen = {}
        self.slots = []
        self.slot_i = 0


class K:
    def __init__(self, nc, es):
        self.nc = nc
        self.es = es
        self.E = {}
        for n in ("pe", "act", "dve", "pool", "sp"):
            self.E[n] = Eng(n, es.enter_context(nc.semaphore("s_" + n)))
        for n, cnt in (("sp", 12), ("pool", 8), ("act", 4)):
            e = self.E[n]
            for i in range(cnt):
                e.slots.append([es.enter_context(nc.semaphore("d_%s%d" % (n, i))), 0])
        self.dram = {}
        self.ninst = 0

    def sb(self, st, name, shape, dt):
        self.uid = getattr(self, "uid", 0) + 1
        name = "sb%d_%s" % (self.uid, name)
        return Buf(st.enter_context(self.nc.sbuf_tensor(name, list(shape), dt)), name)

    def ps(self, st, name, shape, dt=F32):
        self.uid = getattr(self, "uid", 0) + 1
        name = "ps%d_%s" % (self.uid, name)
        return Buf(st.enter_context(self.nc.psum_tensor(name, list(shape), dt)), name, banks=(Buf(None, name + "_bank"),))

    def dbuf(self, key):
        b = self.dram.get(key)
        if b is None:
            b = self.dram[key] = Buf(None, str(key))
        return b

    def _wait(self, eng, tok):
        sem, val, owner = tok
        if owner is eng and eng.name == "pe":
            return
        sid = id(sem)
        if eng.seen.get(sid, 0) >= val:
            return
        eng.seen[sid] = val
        eng.prog.append(lambda e, sem=sem, val=val: e.wait_ge(sem, val))

    def _deps(self, eng, reads, writes):
        for b in reads:
            for tok in b.w.values():
                self._wait(eng, tok)
        for b in writes:
            for tok in b.w.values():
                self._wait(eng, tok)
            for tok in b.r.values():
                self._wait(eng, tok)

    def _mark(self, tok, reads, writes):
        sid = id(tok[0])
        for b in reads:
            b.r[sid] = tok
        for b in writes:
            b.w = {sid: tok}
            b.r = {}

    def op(self, en, fn, reads=(), writes=(), inc=True):
        eng = self.E[en]
        bk = [x for b in list(reads) + list(writes) for x in b.banks]
        if bk:
            writes = list(writes) + bk
        self._deps(eng, reads, writes)
        self.ninst += 1
        if inc:
            eng.count += 1
            sem = eng.sem
            eng.prog.append(lambda e, fn=fn, sem=sem: fn(e).then_inc(sem, 1))
            self._mark((sem, eng.count, eng), reads, writes)
        else:
            eng.prog.append(lambda e, fn=fn: fn(e))
            self._mark((eng.sem, eng.count + 1, eng), reads, writes)

    def dma(self, en, fn, reads=(), writes=()):
        eng = self.E[en]
        self._deps(eng, reads, writes)
        slot = eng.slots[eng.slot_i]
        eng.slot_i = (eng.slot_i + 1) % len(eng.slots)
        sem = slot[0]
        if slot[1] > 0:
            self._wait(eng, (sem, 16 * slot[1], None))
        slot[1] += 1
        val = 16 * slot[1]
        self.ninst += 1
        eng.prog.append(lambda e, fn=fn, sem=sem: fn(e).then_inc(sem, 16))
        self._mark((sem, val, None), reads, writes)

    def capture(self, f):
        rec = []
        o_op, o_dma = self.op, self.dma
        self.op = lambda *a, **kw: rec.append((o_op, a, kw))
        self.dma = lambda *a, **kw: rec.append((o_dma, a, kw))
        try:
            f()
        finally:
            del self.op
            del self.dma
        return rec

    def play_interleaved(self, recs):
        n = max(len(r) for r in recs)
        for i_ in range(n):
            for r in recs:
                if i_ < len(r):
                    fn, a, kw = r[i_]
                    fn(*a, **kw)

    def barrier(self):
        toks = []
        for e in self.E.values():
            if e.count:
                toks.append((e.sem, e.count, e))
            for s in e.slots:
                if s[1]:
                    toks.append((s[0], 16 * s[1], None))
        for e in self.E.values():
            for t in toks:
                self._wait(e, t)

    def finish(self):
        self.barrier()

    def emit(self):
        nc = self.nc
        with nc.Block() as block:
            @block.sync
            def _(e):
                for f in self.E["sp"].prog:
                    f(e)

            @block.gpsimd
            def _(e):
                for f in self.E["pool"].prog:
                    f(e)

            @block.scalar
            def _(e):
                for f in self.E["act"].prog:
                    f(e)

            @block.vector
            def _(e):
                for f in self.E["dve"].prog:
                    f(e)

            @block.tensor
            def _(e):
                for f in self.E["pe"].prog:
                    f(e)


class Cfg:
    def __init__(self, tp=256, n_p=2, ts=4096, depth=4, dbg=None, phases=('rwkv', 'scan', 'na', 'mlp')):
        self.tp, self.n_p, self.ts, self.depth = tp, n_p, ts, depth
        self.phases = phases
        self.scan_stop = 99
        self.na_stop = 99
        self.sub = 99
        self.seqs = [(i * tp, tp, 0) for i in range(n_p)] + [(n_p * tp, ts, 1)]
        self.rows = n_p * tp + ts
        self.dbg = dbg or []
        self.n_rw = (depth + 1) // 2
        self.n_na = depth // 2


WEIGHTS = [
    ("ada_w", "L", D, 6 * D), ("mlp_w1", "L", D, DFF), ("mlp_w2", "L", DFF, D),
    ("rwkv_w_rkv", "R3", D, D), ("rwkv_w_o", "R", D, D),
    ("rwkv_w1", "R2", D, 64), ("rwkv_w2", "R2", 64, D),
    ("rwkv_a1", "R2", D, 64), ("rwkv_a2", "R2", 64, D),
    ("rwkv_g1", "R", D, 128), ("rwkv_g2", "R", 128, D),
    ("na_w_qkv", "N", D, 3 * D), ("na_w_o", "N", D, D),
]


def build(cfg):
    nc = bass.Bass("TRN2", target_bir_lowering=False)
    es = ExitStack()
    k = K(nc, es)
    L = cfg.depth
    NR, NN = cfg.n_rw, cfg.n_na
    ROWS = cfg.rows
    NP, TP, TS = cfg.n_p, cfg.tp, cfg.ts

    def din(name, shape, dt=F32):
        return nc.dram_tensor(name, list(shape), dt, kind="ExternalInput").ap()

    def dout(name, shape, dt=F32):
        return nc.dram_tensor(name, list(shape), dt, kind="ExternalOutput").ap()

    def dscr(name, shape, dt=F32):
        if name in cfg.dbg:
            return nc.dram_tensor(name, list(shape), dt, kind="ExternalOutput").ap()
        return nc.dram_tensor(name, list(shape), dt, kind="Internal").ap()

    x_in = din("x_in", [ROWS, D])
    conds = din("conds", [2, D])
    norm_g = din("norm_g", [L, 2, D])
    ada_b = din("ada_b", [L, 6 * D])
    nlay = {"L": L, "R": NR, "R2": NR * 2, "R3": NR * 3, "N": NN}
    wf = {}
    wb = {}
    for name, kind, r, c in WEIGHTS:
        n = nlay[kind]
        if n == 0:
            continue
        wf[name] = din(name, [n * r, c])
        wb[name] = dscr(name + "_b", [n * r, c], BF16)
    ident_in = din("ident", [128, 128])
    zeros_in = din("zeros", [1, D])
    tric_in = din("tric", [2, 128, 128])
    allc_in = din("allc", [128, 128])
    mask4_in = din("mask4", [2, 128, 512])
    maskl_in = din("maskl", [2, 128, 128])
    y_out = dout("y_out", [ROWS, D])
    if NR:
        rw_mu = din("rwkv_mu", [NR, 6, D])
        rw_vec = {n: din(n, [NR, 2, D]) for n in ("rwkv_w0", "rwkv_a0")}
        for n in ("rwkv_k_k", "rwkv_k_a", "rwkv_r_k", "rwkv_ln_w", "rwkv_ln_b"):
            rw_vec[n] = din(n, [NR, D])
        st_in = din("state_in", [NR, 2, H, 64, 64])
        st_out = dout("state_out", [NP, NR, 2, H, 64, 64])
        SC = {n: dscr("S_" + n, [ROWS, D]) for n in ("Hh", "LW0", "LW1", "YF", "YB")}
        for n in ("R", "V", "KK", "K0", "K1", "B0", "B1", "G"):
            SC[n] = dscr("S_" + n, [ROWS, D], BF16)
        SC["COEF"] = dscr("S_COEF", [ROWS, H])

    if NN:
        ecol_in = din("ecol", [31, 4096])
        colmask_in = din("colmask", [1, 4096])
        j15_in = din("j15", [15, 15])
        rpb_in = din("na_rpb", [NN, H, 15, 31])
        qg_in = din("na_q_g", [NN, 64])
        kg_in = din("na_k_g", [NN, 64])
        ck_in = din("cache_k", [NN, H, 256, 64])
        cv_in = din("cache_v", [NN, H, 256, 64])
        nk_out = dout("nk_out", [NP, NN, H, TP, 64])
        nv_out = dout("nv_out", [NP, NN, H, TP, 64])
        QT = dscr("S_QT", [H, 64, ROWS], BF16)
        KT = dscr("S_KT", [H, 64, ROWS], BF16)
        VV = dscr("S_VV", [ROWS, D], BF16)
        OO = dscr("S_OO", [ROWS, D], BF16)
        T1R = dscr("S_T1R", [H, 15, 4096])
    X = dscr("X", [ROWS, D])
    MODS = dscr("MODS", [L, 2, 6 * D])

    with es:
        cst = ExitStack()
        es.enter_context(cst)
        ident_f = k.sb(cst, "ident_f", [128, 128], F32)
        ident_b = k.sb(cst, "ident_b", [128, 128], BF16)
        k.dma("sp", lambda e: e.dma_start(out=ident_f[:], in_=ident_in[:, :]), writes=[ident_f])
        k.op("dve", lambda e: e.tensor_copy(out=ident_b[:], in_=ident_f[:]), reads=[ident_f], writes=[ident_b])
        tric = [k.sb(cst, "tric%d" % d, [128, 128], F32) for d in range(2)]
        allc = k.sb(cst, "allc", [128, 128], F32)
        mask4 = [k.sb(cst, "mask4_%d" % d, [128, 512], F32) for d in range(2)]
        maskl = [k.sb(cst, "maskl_%d" % d, [128, 128], F32) for d in range(2)]
        for d in range(2):
            k.dma("sp", lambda e, d=d: e.dma_start(out=tric[d][:], in_=tric_in[d]), writes=[tric[d]])
            k.dma("sp", lambda e, d=d: e.dma_start(out=mask4[d][:], in_=mask4_in[d]), writes=[mask4[d]])
            k.dma("sp", lambda e, d=d: e.dma_start(out=maskl[d][:], in_=maskl_in[d]), writes=[maskl[d]])
        k.dma("sp", lambda e: e.dma_start(out=allc[:], in_=allc_in[:, :]), writes=[allc])

        def cast_list(names):
            out = []
            for name, kind, r, c in WEIGHTS:
                if name not in wf or name not in names:
                    continue
                tot = nlay[kind] * r
                step = max(1, (1 << 19) // c)
                for r0 in range(0, tot, step):
                    r1 = min(tot, r0 + step)
                    out.append(lambda name=name, r0=r0, r1=r1: k.dma(
                        "pool", lambda e: e.dma_start(out=wb[name][r0:r1, :], in_=wf[name][r0:r1, :]), writes=[k.dbuf((name, "b", r0))]))
            return out
        early = [n for n, _, _, _ in WEIGHTS if n == "ada_w" or n.startswith("rwkv")]
        late = [n for n, _, _, _ in WEIGHTS if n not in early]
        if not NR or "rwkv" not in cfg.phases:
            early, late = early + late, []
        for f in cast_list(early):
            f()
        pending_casts = cast_list(late)

        def trickle(n=1):
            for _ in range(n):
                if pending_casts:
                    pending_casts.pop(0)()

        def drain_casts():
            while pending_casts:
                pending_casts.pop(0)()

        def wread(name):
            return [b for key, b in k.dram.items() if key[0] == name]

        with ExitStack() as st:
            cT = k.sb(st, "cT", [128, 2, NCH], F32)
            sg = k.sb(st, "sg", [128, 2, NCH], F32)
            cTb = k.sb(st, "cTb", [128, NCH, 2], BF16)
            k.dma("sp", lambda e: e.dma_start(out=cT[:], in_=conds.rearrange("j (c p) -> p j c", p=128), allow_slow_non_contiguous=True), writes=[cT])
            k.op("act", lambda e: e.activation(out=sg[:], in_=cT[:], func=AF.Sigmoid), reads=[cT], writes=[sg])
            k.op("dve", lambda e: e.tensor_tensor(out=cTb[:].rearrange("p c j -> p j c"), in0=cT[:], in1=sg[:], op=ALU.mult), reads=[cT, sg], writes=[cTb])
            aw = [k.sb(st, "aw%d" % i, [128, NCH, 512], BF16) for i in range(3)]
            adb = k.sb(st, "adb", [2, 6 * D], F32)
            mo = k.sb(st, "mo", [2, 6 * D], F32)
            pa = [k.ps(st, "pa%d" % i, [2, 512]) for i in range(2)]
            it = 0
            for l in range(L):
                k.dma("sp", lambda e, l=l: e.dma_start(out=adb[:], in_=ada_b[l:l + 1, :].broadcast_to([2, 6 * D])), writes=[adb])
                for nb in range(12):
                    a = aw[it % 3]
                    p = pa[it % 2]
                    it += 1
                    k.dma("sp", lambda e, a=a, l=l, nb=nb: e.dma_start(
                        out=a[:], in_=wb["ada_w"][l * D:(l + 1) * D, nb * 512:(nb + 1) * 512].rearrange("(c p) f -> p c f", p=128)),
                        reads=wread("ada_w"), writes=[a])
                    for kc in range(NCH):
                        k.op("pe", lambda e, p=p, a=a, kc=kc: e.matmul(p[:], lhsT=cTb[:, kc, :], rhs=a[:, kc, :], start=(kc == 0), stop=(kc == NCH - 1)),
                             reads=[cTb, a], writes=[p], inc=(kc == NCH - 1))
                    k.op("dve", lambda e, p=p, nb=nb: e.tensor_tensor(out=mo[:, nb * 512:(nb + 1) * 512], in0=p[:], in1=adb[:, nb * 512:(nb + 1) * 512], op=ALU.add),
                         reads=[p, adb], writes=[mo])
                k.dma("sp", lambda e, l=l: e.dma_start(out=MODS[l], in_=mo[:]), reads=[mo], writes=[k.dbuf(("MODS", l))])
        k.barrier()

        NTx = ROWS // 128
        for t0 in range(0, NTx, 4):
            t1 = min(NTx, t0 + 4)
            k.dma("sp", lambda e, t0=t0, t1=t1: e.dma_start(out=X[t0 * 128:t1 * 128, :], in_=x_in[t0 * 128:t1 * 128, :]),
                  writes=[k.dbuf(("X", t)) for t in range(t0, t1)])

        def load_mod(st, l, which, gidx=None):
            out = {}
            for cnd in range(2):
                for idx in which:
                    b = k.sb(st, "mod_%d_%d" % (cnd, idx), [128, D], F32)
                    k.dma("sp", lambda e, b=b, cnd=cnd, idx=idx: e.dma_start(
                        out=b[:], in_=MODS[l, cnd:cnd + 1, idx * D:(idx + 1) * D].broadcast_to([128, D])),
                        reads=[k.dbuf(("MODS", l))], writes=[b])
                    out[(cnd, idx)] = b
            return out

        def make_gain(st, l, sub, mods, sc_idx):
            g = k.sb(st, "ng", [128, D], F32)
            k.dma("sp", lambda e: e.dma_start(out=g[:], in_=norm_g[l, sub:sub + 1, :].broadcast_to([128, D])), writes=[g])
            for cnd in range(2):
                b = mods[(cnd, sc_idx)]
                k.op("dve", lambda e, b=b: e.scalar_tensor_tensor(out=b[:], in0=b[:], scalar=1.0, in1=g[:], op0=ALU.add, op1=ALU.mult),
                     reads=[b, g], writes=[b])

        def tile_cond(t):
            return 0 if t * 128 < NP * TP else 1

        def norm_mod(xt, hb, G, S, ss, junk):
            k.op("act", lambda e: e.activation(out=hb[:], in_=xt[:], func=AF.Square, accum_out=ss[:]), reads=[xt], writes=[hb, ss])
            k.op("act", lambda e: e.activation(out=ss[:], in_=ss[:], func=AF.Ln, scale=1.0 / D, bias=1e-6), reads=[ss], writes=[ss])
            k.op("act", lambda e: e.activation(out=ss[:], in_=ss[:], func=AF.Exp, scale=-0.5), reads=[ss], writes=[ss])
            k.op("dve", lambda e: e.scalar_tensor_tensor(out=hb[:], in0=xt[:], scalar=ss[:, 0:1], in1=G[:], op0=ALU.mult, op1=ALU.mult),
                 reads=[xt, ss, G], writes=[hb])
            k.op("pool", lambda e: e.tensor_tensor(out=hb[:], in0=hb[:], in1=S[:], op=ALU.add), reads=[hb, S], writes=[hb])

        def mlp_phase(l):
            drain_casts()
            with ExitStack() as st:
                mods = load_mod(st, l, [3, 4, 5])
                make_gain(st, l, 1, mods, 4)
                xts = [k.sb(st, "mx%d" % i, [128, D], F32) for i in range(8)]
                hbs = [k.sb(st, "mh%d" % i, [128, D], BF16) for i in range(2)]
                junk = k.sb(st, "mjunk", [128, D], F32)
                junk2 = k.sb(st, "mjunk2", [128, 512], F32)
                sss = [k.sb(st, "mss%d" % i, [128, 1], F32) for i in range(2)]
                hT = [k.sb(st, "mhT%d" % i, [128, NCH, 512], BF16) for i in range(2)]
                hidb = [k.sb(st, "mhid%d" % i, [128, 4, 512], BF16) for i in range(8)]
                rl = [k.sb(st, "mrl%d" % i, [128, 512], F32) for i in range(2)]
                w1s = [k.sb(st, "mw1_%d" % i, [128, NCH, 512], BF16) for i in range(4)]
                w2s = [k.sb(st, "mw2_%d" % i, [128, 4, 512], BF16) for i in range(4)]
                ptr = k.ps(st, "mptr", [128, NCH, 128], BF16)
                phid = [k.ps(st, "mphid%d" % i, [128, 512]) for i in range(2)]
                pacc = [k.ps(st, "mpacc%d" % i, [128, 512]) for i in range(4)]
                ngrp = ROWS // 512
                c1 = [0]
                c2 = [0]
                ch = [0]
                w1r = wread("mlp_w1")
                w2r = wread("mlp_w2")

                hb4 = [k.sb(st, "mhb%d" % i, [128, D], BF16) for i in range(4)]

                def front_norm(g):
                    cnd = tile_cond(g * 4)
                    G, S = mods[(cnd, 4)], mods[(cnd, 3)]
                    for s in range(4):
                        t = g * 4 + s
                        xt = xts[(g % 2) * 4 + s]
                        k.dma("sp", lambda e, xt=xt, t=t: e.dma_start(out=xt[:], in_=X[t * 128:(t + 1) * 128, :]),
                              reads=[k.dbuf(("X", t))], writes=[xt])
                        norm_mod(xt, hb4[s], G, S, sss[s % 2], junk)

                def front_T(g):
                    hTg = hT[g % 2]
                    for s in range(4):
                        hb = hb4[s]
                        for kc in range(NCH):
                            k.op("pe", lambda e, hb=hb, kc=kc: e.transpose(out=ptr[:, kc, :], in_=hb[:, kc * 128:(kc + 1) * 128], identity=ident_b[:]),
                                 reads=[hb, ident_b], writes=[ptr], inc=(kc == NCH - 1))
                        k.op("act", lambda e, hTg=hTg, s=s: e.copy(out=hTg[:, :, s * 128:(s + 1) * 128], in_=ptr[:]), reads=[ptr], writes=[hTg])

                def out_block(g, half, fb, w2):
                    for s in range(4):
                        for fi in range(4):
                            fc = fb * 4 + fi
                            k.op("pe", lambda e, s=s, fi=fi, fc=fc, w2=w2: e.matmul(
                                pacc[s][:], lhsT=hidb[fb][:, fi, s * 128:(s + 1) * 128], rhs=w2[:, fi, :], start=(fc == 0), stop=(fc == 31)),
                                reads=[hidb[fb], w2], writes=[pacc[s]], inc=(fi == 3))

                def load_w2(half, fb):
                    w2 = w2s[c2[0] % 4]
                    c2[0] += 1
                    k.dma("sp", lambda e, w2=w2, fb=fb, half=half: e.dma_start(
                        out=w2[:], in_=wb["mlp_w2"][l * DFF + fb * 512:l * DFF + (fb + 1) * 512, half * 512:(half + 1) * 512].rearrange("(c p) f -> p c f", p=128)),
                        reads=w2r, writes=[w2])
                    return w2

                front_norm(0)
                front_T(0)
                pend_store = []
                for g in range(ngrp):
                    cnd = tile_cond(g * 4)
                    GT = mods[(cnd, 5)]
                    hTg = hT[g % 2]
                    xg = [xts[(g % 2) * 4 + s] for s in range(4)]
                    for half in range(2):
                        prev = None
                        for fb in range(8):
                            if half == 0:
                                w1 = w1s[c1[0] % 4]
                                c1[0] += 1
                                k.dma("sp", lambda e, w1=w1, fb=fb: e.dma_start(
                                    out=w1[:], in_=wb["mlp_w1"][l * D:(l + 1) * D, fb * 512:(fb + 1) * 512].rearrange("(c p) f -> p c f", p=128)),
                                    reads=w1r, writes=[w1])
                                w2 = load_w2(half, fb)
                                if fb == 2:
                                    while pend_store:
                                        pend_store.pop(0)()
                                for fi in range(4):
                                    fc = fb * 4 + fi
                                    ph = phid[ch[0] % 2]
                                    r_ = rl[ch[0] % 2]
                                    ch[0] += 1
                                    for kc in range(NCH):
                                        k.op("pe", lambda e, ph=ph, w1=w1, kc=kc, fi=fi, hTg=hTg: e.matmul(
                                            ph[:], lhsT=w1[:, kc, fi * 128:(fi + 1) * 128], rhs=hTg[:, kc, :], start=(kc == 0), stop=(kc == NCH - 1)),
                                            reads=[w1, hTg], writes=[ph], inc=(kc == NCH - 1))
                                    k.op("act", lambda e, ph=ph, r_=r_: e.activation(out=r_[:], in_=ph[:], func=AF.Relu), reads=[ph], writes=[r_])
                                    k.op("pool", lambda e, r_=r_, fb=fb, fi=fi: e.tensor_tensor(out=hidb[fb][:, fi, :], in0=r_[:], in1=r_[:], op=ALU.mult), reads=[r_], writes=[hidb[fb]])
                                if prev is not None:
                                    out_block(g, half, prev[0], prev[1])
                                prev = (fb, w2)
                            else:
                                w2 = load_w2(half, fb)
                                out_block(g, half, fb, w2)
                                if fb == 0 and g + 1 < ngrp:
                                    front_norm(g + 1)
                                if fb == 5 and g + 1 < ngrp:
                                    front_T(g + 1)
                        if half == 0:
                            out_block(g, half, prev[0], prev[1])
                        for s in range(4):
                            xt = xg[s]
                            k.op("dve", lambda e, s=s, half=half, GT=GT: e.tensor_tensor(out=junk2[:], in0=pacc[s][:], in1=GT[:, half * 512:(half + 1) * 512], op=ALU.mult),
                                 reads=[pacc[s], GT], writes=[junk2])
                            k.op("dve", lambda e, xt=xt, half=half: e.tensor_tensor(out=xt[:, half * 512:(half + 1) * 512], in0=xt[:, half * 512:(half + 1) * 512], in1=junk2[:], op=ALU.add),
                                 reads=[xt, junk2], writes=[xt])
                    def store(g=g, xg=xg):
                        dst = y_out if l == L - 1 else X
                        for s in range(4):
                            t = g * 4 + s
                            k.dma("sp", lambda e, xt=xg[s], t=t, dst=dst: e.dma_start(out=dst[t * 128:(t + 1) * 128, :], in_=xt[:]),
                                  reads=[xg[s]], writes=[k.dbuf(("Y" if l == L - 1 else "X", t))])
                    pend_store.append(store)
                while pend_store:
                    pend_store.pop(0)()
            k.barrier()

        def seq_tiles():
            out = []
            for si, (r0, T, cnd) in enumerate(cfg.seqs):
                n = T // 128
                for j in range(n):
                    out.append((r0 // 128 + j, si, j, n))
            return out

        def bcast_load(st, name, src_row_ap):
            b = k.sb(st, name, [128, D], F32)
            k.dma("sp", lambda e: e.dma_start(out=b[:], in_=src_row_ap.broadcast_to([128, D])), writes=[b])
            return b

        def scan_pass(l, d):
            i = l // 2
            rows = lambda t: slice(t * 128, (t + 1) * 128)
            hsl = lambda h: slice(h * 64, (h + 1) * 64)
            ydst = "YF" if d == 0 else "YB"
            with ExitStack() as st:
                LfB = [{n: k.sb(st, "sl%d_%s" % (j, n), [128, D], BF16) for n in ("r", "v", "kk", "kd", "bd")} for j in range(2)]
                LfW = [k.sb(st, "slw%d" % j, [128, D], F32) for j in range(2)]
                srcn = {"r": "R", "v": "V", "kk": "KK", "kd": "K%d" % d, "bd": "B%d" % d}
                cumS = k.sb(st, "cumS", [128, D], F32)
                EX = [k.sb(st, "EX%d" % j, [128, D], F32) for j in range(2)]
                tmpf = k.sb(st, "tmpf", [128, D], F32)
                yout = k.sb(st, "yout", [128, D], F32)
                RT, BT, KT = [k.sb(st, n, [128, D], BF16) for n in ("RT", "BT", "KT")]
                XA = k.sb(st, "XA", [128, H, 64], BF16)
                ART = k.sb(st, "ART", [64, H, 128], BF16)
                RTT = [k.sb(st, "RTT%d" % j, [64, H, 128], BF16) for j in range(2)]
                BTT = k.sb(st, "BTT", [64, H, 128], BF16)
                KTT = k.sb(st, "KTT", [64, H, 128], BF16)
                BP = [k.sb(st, "BP%d" % j, [128, D], BF16) for j in range(2)]
                KP = [k.sb(st, "KP%d" % j, [128, D], BF16) for j in range(2)]
                vb = [k.sb(st, "vb%d" % j, [128, D], BF16) for j in range(2)]
                PC = [k.sb(st, "PC%d" % j, [64, H], F32) for j in range(2)]
                MM = [[k.sb(st, "MM%d_%d" % (j, h), [128, 512], BF16) for h in range(H)] for j in range(2)]
                Z = [[k.sb(st, "Z%d_%d" % (h, j), [128, 384], BF16) for j in range(2)] for h in range(H)]
                WTg = [k.sb(st, "WTg%d" % j, [64, 8, 128], BF16) for j in range(2)]
                U = [k.sb(st, "U%d" % h, [128, 64], BF16) for h in range(H)]
                A = [k.sb(st, "A%d" % h, [64, 64], F32) for h in range(H)]
                Ab = [k.sb(st, "Ab%d" % h, [64, 64], BF16) for h in range(H)]
                Sio = k.sb(st, "Sio", [64, H, 64], F32)
                T0 = st.enter_context(nc.psum_tensor("scT0_%d_%d" % (l, d), [128, D], F32))
                T1 = st.enter_context(nc.psum_tensor("scT1_%d_%d" % (l, d), [128, D], F32))
                T2 = st.enter_context(nc.psum_tensor("scT2_%d_%d" % (l, d), [128, D], F32))
                T3 = st.enter_context(nc.psum_tensor("scT3_%d_%d" % (l, d), [128, D], F32))
                bks = [Buf(None, "bank%d" % j) for j in range(8)]
                psy = Buf(T0, "psy", banks=(bks[0], bks[1]))
                pPC = Buf(T0[0:64, 0:16], "pPC", banks=(bks[0],))
                PP = [Buf(T1[:, 0:512], "PP0", banks=(bks[2],)), Buf(T1[:, 512:1024], "PP1", banks=(bks[3],))]
                PQ = [Buf(T2[:, 0:512], "PQ0", banks=(bks[4],)), Buf(T2[:, 512:1024], "PQ1", banks=(bks[5],))]
                PRf = [Buf(T3[:, 0:512], "PRf0", banks=(bks[6],)), Buf(T3[:, 512:1024], "PRf1", banks=(bks[7],))]
                PR = [Buf(T3[:, 512 * j:512 * (j + 1)].bitcast(BF16).rearrange("p (a b) -> p a b", a=8), "PR%d" % j, banks=(bks[6 + j],)) for j in range(2)]
                PT0 = [Buf(T0[:, 512 * j:512 * (j + 1)].bitcast(BF16).rearrange("p (a b) -> p a b", a=8), "PT0_%d" % j, banks=(bks[j],)) for j in range(2)]
                ring4 = [PP[0], PP[1], PQ[0], PQ[1], PRf[0], PRf[1]]
                cnt = {"x": 0, "c": 0}
                ec = float(np.exp(-0.5))

                def load(t, sl):
                    for nm, b in LfB[sl].items():
                        k.dma("sp", lambda e, b=b, nm=nm, t=t: e.dma_start(out=b[:], in_=SC[srcn[nm]][rows(t), :]), reads=[k.dbuf((srcn[nm], t))], writes=[b])
                    k.dma("sp", lambda e, t=t, sl=sl: e.dma_start(out=LfW[sl][:], in_=SC["LW%d" % d][rows(t), :]), reads=[k.dbuf(("LW%d" % d, t))], writes=[LfW[sl]])

                def front12(sl, par):
                    L_, lw = LfB[sl], LfW[sl]
                    th = []
                    ad = th.append
                    for half in range(2):
                        ad(lambda half=half: k.op("pe", lambda e: e.matmul(T0[:, half * 512:(half + 1) * 512], lhsT=tric[d][:], rhs=lw[:, half * 512:(half + 1) * 512], start=True, stop=True),
                                                  reads=[tric[d], lw], writes=[psy]))
                    ad(lambda: k.op("dve", lambda e: e.tensor_copy(out=cumS[:], in_=T0[:]), reads=[psy], writes=[cumS]))
                    for half in range(2):
                        ad(lambda half=half: k.op("pe", lambda e: e.matmul(T0[:, half * 512:(half + 1) * 512], lhsT=allc[:], rhs=lw[:, half * 512:(half + 1) * 512], start=True, stop=True),
                                                  reads=[allc, lw], writes=[psy]))
                    ad(lambda: k.op("dve", lambda e: e.tensor_tensor(out=tmpf[:], in0=T0[:], in1=cumS[:], op=ALU.subtract), reads=[psy, cumS], writes=[tmpf]))
                    ad(lambda: k.op("act", lambda e: e.activation(out=EX[1][:], in_=tmpf[:], func=AF.Exp), reads=[tmpf], writes=[EX[1]]))
                    ad(lambda: k.op("dve", lambda e: e.tensor_tensor(out=BP[par][:], in0=L_["bd"][:], in1=EX[1][:], op=ALU.mult), reads=[L_["bd"], EX[1]], writes=[BP[par]]))
                    ad(lambda: k.op("pool", lambda e: e.tensor_tensor(out=KP[par][:], in0=L_["kd"][:], in1=EX[1][:], op=ALU.mult), reads=[L_["kd"], EX[1]], writes=[KP[par]]))

                    def pcs():
                        for h in range(H):
                            k.op("pe", lambda e, h=h: e.matmul(pPC[:, h:h + 1], lhsT=lw[:, hsl(h)], rhs=allc[:, 0:1], start=True, stop=True), reads=[lw, allc], writes=[pPC], inc=(h == H - 1))
                        k.op("act", lambda e: e.activation(out=PC[par][:], in_=pPC[:], func=AF.Exp), reads=[pPC], writes=[PC[par]])
                    ad(pcs)
                    ad(lambda: k.op("act", lambda e: e.activation(out=EX[0][:], in_=cumS[:], func=AF.Exp), reads=[cumS], writes=[EX[0]]))
                    ad(lambda: k.op("dve", lambda e: e.tensor_tensor(out=RT[:], in0=L_["r"][:], in1=EX[0][:], op=ALU.mult), reads=[L_["r"], EX[0]], writes=[RT]))
                    ad(lambda: k.op("act", lambda e: e.activation(out=EX[1][:], in_=cumS[:], func=AF.Exp, scale=-1.0), reads=[cumS], writes=[EX[1]]))
                    ad(lambda: k.op("dve", lambda e: e.tensor_tensor(out=BT[:], in0=L_["bd"][:], in1=EX[1][:], op=ALU.mult), reads=[L_["bd"], EX[1]], writes=[BT]))
                    ad(lambda: k.op("pool", lambda e: e.tensor_tensor(out=KT[:], in0=L_["kd"][:], in1=EX[1][:], op=ALU.mult), reads=[L_["kd"], EX[1]], writes=[KT]))
                    ad(lambda: k.op("dve", lambda e: e.scalar_tensor_tensor(out=tmpf[:], in0=lw[:], scalar=ec, in1=cumS[:], op0=ALU.mult, op1=ALU.add), reads=[lw, cumS], writes=[tmpf]))
                    ad(lambda: k.op("act", lambda e: e.activation(out=EX[0][:], in_=tmpf[:], func=AF.Exp), reads=[tmpf], writes=[EX[0]]))
                    ad(lambda: k.op("dve", lambda e: e.scalar_tensor_tensor(out=XA[:].rearrange("p h c -> p (h c)"), in0=L_["kk"][:], scalar=-1.0, in1=EX[0][:], op0=ALU.mult, op1=ALU.mult),
                                    reads=[L_["kk"], EX[0]], writes=[XA]))
                    ad(lambda: k.op("pool", lambda e: e.tensor_copy(out=vb[par][:], in_=L_["v"][:]), reads=[L_["v"]], writes=[vb[par]]))
                    for (srcb, srcf, dst) in ((RT, lambda h: RT[:, hsl(h)], RTT[par]), (BT, lambda h: BT[:, hsl(h)], BTT), (KT, lambda h: KT[:, hsl(h)], KTT), (XA, lambda h: XA[:, h, :], ART)):
                        for g8 in range(2):
                            def tr(srcb=srcb, srcf=srcf, dst=dst, g8=g8):
                                pr = PT0[cnt["x"] % 2]
                                cnt["x"] += 1
                                for hh in range(8):
                                    h = g8 * 8 + hh
                                    k.op("pe", lambda e, hh=hh, h=h: e.transpose(out=pr[0:64, hh, :], in_=srcf(h), identity=ident_b[:]), reads=[srcb, ident_b], writes=[pr], inc=(hh == 7))
                                k.op("act", lambda e: e.copy(out=dst[:, g8 * 8:(g8 + 1) * 8, :], in_=pr[0:64, :, :]), reads=[pr], writes=[dst])
                            ad(tr)
                    return th

                def AB1(par, h):
                    p = PQ[h % 2]
                    k.op("pe", lambda e: e.matmul(p[:, 0:128], lhsT=BTT[:, h, :], rhs=ART[:, h, :], start=True, stop=True), reads=[BTT, ART], writes=[p], inc=False)
                    k.op("pe", lambda e: e.matmul(p[:, 128:256], lhsT=KTT[:, h, :], rhs=ART[:, h, :], start=True, stop=True), reads=[KTT, ART], writes=[p], inc=False)
                    k.op("pe", lambda e: e.matmul(p[:, 256:384], lhsT=BTT[:, h, :], rhs=RTT[par][:, h, :], start=True, stop=True), reads=[BTT, RTT[par]], writes=[p], inc=False)
                    k.op("pe", lambda e: e.matmul(p[:, 384:512], lhsT=KTT[:, h, :], rhs=RTT[par][:, h, :], start=True, stop=True), reads=[KTT, RTT[par]], writes=[p])
                    k.op("dve", lambda e: e.tensor_tensor(out=MM[par][h][:], in0=p[:], in1=mask4[d][:], op=ALU.mult), reads=[p, mask4[d]], writes=[MM[par][h]])

                def AB2(par, h):
                    q = PRf[h % 2]
                    k.op("pe", lambda e: e.matmul(q[:, 0:128], lhsT=ART[:, h, :], rhs=BTT[:, h, :], start=True, stop=True), reads=[ART, BTT], writes=[q])
                    k.op("dve", lambda e: e.tensor_tensor(out=Z[h][0][:, 128:256], in0=q[:, 0:128], in1=maskl[d][:], op=ALU.mult), reads=[q, maskl[d]], writes=[Z[h][0]])
                    k.op("pool", lambda e: e.tensor_copy(out=Z[h][0][:, 0:128], in_=MM[par][h][:, 0:128]), reads=[MM[par][h]], writes=[Z[h][0]])

                def AB3(par, h):
                    q = PRf[h % 2]
                    k.op("pe", lambda e: e.matmul(q[:, 128:192], lhsT=MM[par][h][:, 128:256], rhs=vb[par][:, hsl(h)], start=True, stop=True), reads=[MM[par][h], vb[par]], writes=[q])
                    k.op("act", lambda e: e.copy(out=Z[h][0][:, 320:384], in_=q[:, 128:192]), reads=[q], writes=[Z[h][0]])
                    k.op("pool", lambda e: e.tensor_copy(out=Z[h][0][:, 256:320], in_=XA[:, h, :]), reads=[XA], writes=[Z[h][0]])

                def stageAB(par, h):
                    AB1(par, h)
                    AB2(par, h)
                    AB3(par, h)

                def stageC_step(jj, th=None):
                    a_, b_ = jj % 2, (jj + 1) % 2
                    for h in range(H):
                        if th and (jj * H + h) % 3 == 2:
                            th.pop(0)()
                        q = ring4[cnt["c"] % len(ring4)]
                        cnt["c"] += 1
                        za, zb = Z[h][a_], Z[h][b_]
                        if jj < 5:
                            k.op("pe", lambda e, q=q, za=za: e.matmul(q[:, 128:384], lhsT=za[:, 0:128], rhs=za[:, 128:384], start=True, stop=True), reads=[za], writes=[q], inc=False)
                            k.op("pe", lambda e, q=q, za=za: e.matmul(q[:, 0:128], lhsT=za[:, 128:256], rhs=za[:, 0:128], start=True, stop=True), reads=[za], writes=[q])
                        elif jj == 5:
                            k.op("pe", lambda e, q=q, za=za: e.matmul(q[:, 256:384], lhsT=za[:, 0:128], rhs=za[:, 256:384], start=True, stop=True), reads=[za], writes=[q], inc=False)
                            k.op("pe", lambda e, q=q, za=za: e.matmul(q[:, 0:128], lhsT=za[:, 128:256], rhs=za[:, 0:128], start=True, stop=True), reads=[za], writes=[q])
                        else:
                            k.op("pe", lambda e, q=q, za=za: e.matmul(q[:, 256:384], lhsT=za[:, 0:128], rhs=za[:, 256:384], start=True, stop=True), reads=[za], writes=[q])
                        k.op("dve", lambda e, q=q, za=za, zb=zb: e.tensor_tensor(out=zb[:, 256:384], in0=q[:, 256:384], in1=za[:, 256:384], op=ALU.add), reads=[q, za], writes=[zb])
                        if jj < 5:
                            k.op("act", lambda e, q=q, zb=zb: e.copy(out=zb[:, 0:256], in_=q[:, 0:256]), reads=[q], writes=[zb])
                        elif jj == 5:
                            k.op("act", lambda e, q=q, zb=zb: e.copy(out=zb[:, 0:128], in_=q[:, 0:128]), reads=[q], writes=[zb])

                def E1(par, h):
                    q = PP[h % 2]
                    k.op("pe", lambda e: e.matmul(q[:, 0:64], lhsT=ident_b[:], rhs=Z[h][1][:, 320:384], start=True, stop=False), reads=[ident_b, Z[h][1]], writes=[q], inc=False)
                    k.op("pe", lambda e: e.matmul(q[:, 0:64], lhsT=WTg[h // 8][:, h % 8, :], rhs=Ab[h][:], start=False, stop=True), reads=[WTg[h // 8], Ab[h]], writes=[q])
                    k.op("act", lambda e: e.copy(out=U[h][:], in_=q[:, 0:64]), reads=[q], writes=[U[h]])

                def E2(par, h):
                    k.op("pe", lambda e: e.matmul(T0[:, hsl(h)], lhsT=RTT[par][:, h, :], rhs=Ab[h][:], start=True, stop=False), reads=[RTT[par], Ab[h]], writes=[psy], inc=False)
                    k.op("pe", lambda e: e.matmul(T0[:, hsl(h)], lhsT=MM[par][h][:, 256:384], rhs=U[h][:], start=False, stop=False), reads=[MM[par][h], U[h]], writes=[psy], inc=False)
                    k.op("pe", lambda e: e.matmul(T0[:, hsl(h)], lhsT=MM[par][h][:, 384:512], rhs=vb[par][:, hsl(h)], start=False, stop=True), reads=[MM[par][h], vb[par]], writes=[psy])
                    pa = PP[h % 2]
                    k.op("pe", lambda e: e.matmul(pa[0:64, 64:128], lhsT=BP[par][:, hsl(h)], rhs=U[h][:], start=True, stop=False), reads=[BP[par], U[h]], writes=[pa], inc=False)
                    k.op("pe", lambda e: e.matmul(pa[0:64, 64:128], lhsT=KP[par][:, hsl(h)], rhs=vb[par][:, hsl(h)], start=False, stop=True), reads=[KP[par], vb[par]], writes=[pa])
                    k.op("dve", lambda e: e.scalar_tensor_tensor(out=A[h][:], in0=A[h][:], scalar=PC[par][:, h:h + 1], in1=pa[0:64, 64:128], op0=ALU.mult, op1=ALU.add), reads=[A[h], PC[par], pa], writes=[A[h]])
                    k.op("act", lambda e: e.copy(out=Ab[h][:], in_=A[h][:]), reads=[A[h]], writes=[Ab[h]])

                def stageD():
                    for g8 in range(2):
                        pr = PR[cnt["x"] % 2]
                        cnt["x"] += 1
                        for hh in range(8):
                            h = g8 * 8 + hh
                            k.op("pe", lambda e, hh=hh, h=h, pr=pr: e.transpose(out=pr[0:64, hh, :], in_=Z[h][1][:, 256:320], identity=ident_b[:]), reads=[Z[h][1], ident_b], writes=[pr], inc=(hh == 7))
                        k.op("act", lambda e, g8=g8, pr=pr: e.copy(out=WTg[g8][:], in_=pr[0:64, :, :]), reads=[pr], writes=[WTg[g8]])

                sched = []
                for si, (r0s, T, cnd) in enumerate(cfg.seqs):
                    n = T // 128
                    order = list(range(n)) if d == 0 else list(range(n - 1, -1, -1))
                    for oi, j in enumerate(order):
                        sched.append((r0s // 128 + j, si, oi == 0, oi == n - 1, cnd))
                load(sched[0][0], 0)
                if len(sched) > 1:
                    load(sched[1][0], 1)
                for f in front12(0, 0):
                    f()
                for h in range(H):
                    stageAB(0, h)
                for ci, (t, si, first, last, cnd) in enumerate(sched):
                    trickle(2)
                    par = ci % 2
                    nxt = sched[ci + 1] if ci + 1 < len(sched) else None
                    if first:
                        if cnd == 0:
                            for h in range(H):
                                k.op("pool", lambda e, h=h: e.memset(A[h][:], 0.0), writes=[A[h]])
                                k.op("pool", lambda e, h=h: e.memset(Ab[h][:], 0.0), writes=[Ab[h]])
                        else:
                            k.dma("sp", lambda e: e.dma_start(out=Sio[:], in_=st_in[i, d].rearrange("h v k -> v h k")), writes=[Sio])
                            for h in range(H):
                                p = PR[h % 2]
                                pf = Buf(p.t.bitcast(F32), "prf", banks=p.banks) if False else None
                                q = PQ[h % 2]
                                k.op("pe", lambda e, h=h, q=q: e.transpose(out=q[0:64, 256:320], in_=Sio[:, h, :], identity=ident_f[0:64, 0:64]), reads=[Sio, ident_f], writes=[q])
                                k.op("dve", lambda e, h=h, q=q: e.tensor_copy(out=A[h][:], in_=q[0:64, 256:320]), reads=[q], writes=[A[h]])
                                k.op("act", lambda e, h=h: e.copy(out=Ab[h][:], in_=A[h][:]), reads=[A[h]], writes=[Ab[h]])
                    th = []
                    if nxt is not None:
                        th = front12((ci + 1) % 2, (ci + 1) % 2)
                    for jj in range(7):
                        stageC_step(jj, th)
                    while th:
                        th.pop(0)()
                    if ci + 2 < len(sched):
                        load(sched[ci + 2][0], ci % 2)
                    stageD()
                    np_ = (ci + 1) % 2
                    E1(par, 0)
                    if nxt is not None:
                        AB1(np_, 0)
                    for h in range(H):
                        if h + 1 < H:
                            E1(par, h + 1)
                        if nxt is not None:
                            AB2(np_, h)
                        E2(par, h)
                        if nxt is not None:
                            if h + 1 < H:
                                AB1(np_, h + 1)
                            AB3(np_, h)
                    k.op("act", lambda e: e.copy(out=yout[:], in_=T0[:]), reads=[psy], writes=[yout])
                    k.dma("sp", lambda e, t=t: e.dma_start(out=SC[ydst][rows(t), :], in_=yout[:]), reads=[yout], writes=[k.dbuf((ydst, t))])
                    if last and cnd == 0:
                        for h in range(H):
                            q = PQ[h % 2]
                            k.op("pe", lambda e, h=h, q=q: e.transpose(out=q[0:64, 256:320], in_=A[h][:], identity=ident_f[0:64, 0:64]), reads=[A[h], ident_f], writes=[q])
                            k.op("dve", lambda e, h=h, q=q: e.tensor_copy(out=Sio[:, h, :], in_=q[0:64, 256:320]), reads=[q], writes=[Sio])
                        k.dma("sp", lambda e, si=si: e.dma_start(out=st_out[si, i, d].rearrange("h v k -> v h k"), in_=Sio[:]), reads=[Sio], writes=[k.dbuf(("st_out", si, i, d))])
            k.barrier()

        def rwkv_epilogue(l):
            i = l // 2
            rows = lambda t: slice(t * 128, (t + 1) * 128)
            with ExitStack() as st:
                wo = k.sb(st, "wo", [128, NCH, D], BF16)
                k.dma("sp", lambda e: e.dma_start(out=wo[:], in_=wb["rwkv_w_o"][i * D:(i + 1) * D, :].rearrange("(c p) f -> p c f", p=128)), reads=wread("rwkv_w_o"), writes=[wo])
                mods = load_mod(st, l, [2])
                lnw = bcast_load(st, "lnw", rw_vec["rwkv_ln_w"][i:i + 1, :])
                lnb = bcast_load(st, "lnb", rw_vec["rwkv_ln_b"][i:i + 1, :])
                NB_ = 4
                yf = [k.sb(st, "e_yf%d" % j, [128, D], F32) for j in range(NB_)]
                yb = [k.sb(st, "e_yb%d" % j, [128, D], F32) for j in range(NB_)]
                gb = [k.sb(st, "e_g%d" % j, [128, D], BF16) for j in range(NB_)]
                xb = [k.sb(st, "e_x%d" % j, [128, D], F32) for j in range(NB_)]
                vv = [k.sb(st, "e_v%d" % j, [128, D], BF16) for j in range(NB_)]
                cen = [k.sb(st, "e_c%d" % j, [128, D], F32) for j in range(NB_)]
                sq = [k.sb(st, "e_s%d" % j, [128, D], F32) for j in range(NB_)]
                ob16 = [k.sb(st, "e_ob%d" % j, [128, D], BF16) for j in range(NB_)]
                oT = [k.sb(st, "e_oT%d" % j, [128, NCH, 128], BF16) for j in range(NB_)]
                cf = [k.sb(st, "e_cf%d" % j, [128, H], F32) for j in range(NB_)]
                st1 = [k.sb(st, "e_st%d" % j, [128, H], F32) for j in range(NB_)]
                ptr = [k.ps(st, "e_ptr%d" % j, [128, NCH, 128], BF16) for j in range(2)]
                po = [k.ps(st, "e_po%d" % j, [128, D]) for j in range(2)]
                v3 = lambda b: b[:].rearrange("p (h c) -> p h c", h=H)
                bc = lambda b: b[:].unsqueeze(2).to_broadcast([128, H, 64])
                def epiA(t):
                    j = t % NB_
                    yf_, yb_, g_, x_, v_, c_, s_, o_, oT_, cf_, s1, pt, pp = yf[j], yb[j], gb[j], xb[j], vv[j], cen[j], sq[j], ob16[j], oT[j], cf[j], st1[j], ptr[t % 2], po[t % 2]
                    for (dst, nm) in ((yf_, "YF"), (yb_, "YB"), (g_, "G"), (v_, "V")):
                        k.dma("sp", lambda e, dst=dst, nm=nm, t=t: e.dma_start(out=dst[:], in_=SC[nm][rows(t), :]), reads=[k.dbuf((nm, t))], writes=[dst])
                    k.dma("sp", lambda e, x_=x_, t=t: e.dma_start(out=x_[:], in_=X[rows(t), :]), reads=[k.dbuf(("X", t))], writes=[x_])
                    k.dma("sp", lambda e, cf_=cf_, t=t: e.dma_start(out=cf_[:], in_=SC["COEF"][rows(t), :]), reads=[k.dbuf(("COEF", t))], writes=[cf_])
                    k.op("dve", lambda e, yf_=yf_, yb_=yb_: e.tensor_tensor(out=yf_[:], in0=yf_[:], in1=yb_[:], op=ALU.add), reads=[yf_, yb_], writes=[yf_])
                    k.op("dve", lambda e, yf_=yf_, s1=s1: e.tensor_reduce(out=s1[:], in_=v3(yf_), axis=AX.X, op=ALU.add), reads=[yf_], writes=[s1])
                    k.op("dve", lambda e, s1=s1: e.tensor_scalar(out=s1[:], in0=s1[:], scalar1=-1.0 / 64, scalar2=0.0, op0=ALU.mult, op1=ALU.add), reads=[s1], writes=[s1])
                    k.op("dve", lambda e, c_=c_, yf_=yf_, s1=s1: e.tensor_tensor(out=v3(c_), in0=v3(yf_), in1=bc(s1), op=ALU.add), reads=[yf_, s1], writes=[c_])
                    k.op("pool", lambda e, c_=c_, s_=s_: e.tensor_tensor(out=s_[:], in0=c_[:], in1=c_[:], op=ALU.mult), reads=[c_], writes=[s_])
                    k.op("dve", lambda e, s_=s_, s1=s1: e.tensor_reduce(out=s1[:], in_=v3(s_), axis=AX.X, op=ALU.add), reads=[s_], writes=[s1])
                    k.op("act", lambda e, s1=s1: e.activation(out=s1[:], in_=s1[:], func=AF.Ln, scale=1.0 / 64, bias=64e-5), reads=[s1], writes=[s1])
                    k.op("act", lambda e, s1=s1: e.activation(out=s1[:], in_=s1[:], func=AF.Exp, scale=-0.5), reads=[s1], writes=[s1])
                    k.op("dve", lambda e, c_=c_, s1=s1: e.tensor_tensor(out=v3(c_), in0=v3(c_), in1=bc(s1), op=ALU.mult), reads=[c_, s1], writes=[c_])
                    k.op("pool", lambda e, c_=c_: e.tensor_tensor(out=c_[:], in0=c_[:], in1=lnw[:], op=ALU.mult), reads=[c_, lnw], writes=[c_])
                    k.op("pool", lambda e, c_=c_: e.tensor_tensor(out=c_[:], in0=c_[:], in1=lnb[:], op=ALU.add), reads=[c_, lnb], writes=[c_])
                    k.op("dve", lambda e, s_=s_, v_=v_, cf_=cf_: e.tensor_tensor(out=v3(s_), in0=v3(v_), in1=bc(cf_), op=ALU.mult), reads=[v_, cf_], writes=[s_])
                    k.op("dve", lambda e, c_=c_, s_=s_: e.tensor_tensor(out=c_[:], in0=c_[:], in1=s_[:], op=ALU.add), reads=[c_, s_], writes=[c_])
                    k.op("dve", lambda e, o_=o_, c_=c_, g_=g_: e.tensor_tensor(out=o_[:], in0=c_[:], in1=g_[:], op=ALU.mult), reads=[c_, g_], writes=[o_])

                def epiB(t):
                    cnd = tile_cond(t)
                    GT = mods[(cnd, 2)]
                    j = t % NB_
                    yf_, yb_, g_, x_, v_, c_, s_, o_, oT_, cf_, s1, pt, pp = yf[j], yb[j], gb[j], xb[j], vv[j], cen[j], sq[j], ob16[j], oT[j], cf[j], st1[j], ptr[t % 2], po[t % 2]
                    for kc in range(NCH):
                        k.op("pe", lambda e, kc=kc, pt=pt, o_=o_: e.transpose(out=pt[:, kc, :], in_=o_[:, kc * 128:(kc + 1) * 128], identity=ident_b[:]), reads=[o_, ident_b], writes=[pt], inc=(kc == NCH - 1))
                    k.op("act", lambda e, pt=pt, oT_=oT_: e.copy(out=oT_[:], in_=pt[:]), reads=[pt], writes=[oT_])
                    for half in range(2):
                        for kc in range(NCH):
                            k.op("pe", lambda e, half=half, kc=kc, pp=pp, oT_=oT_: e.matmul(pp[:, half * 512:(half + 1) * 512], lhsT=oT_[:, kc, :], rhs=wo[:, kc, half * 512:(half + 1) * 512], start=(kc == 0), stop=(kc == NCH - 1)),
                                 reads=[oT_, wo], writes=[pp], inc=(kc == NCH - 1))
                    k.op("dve", lambda e, pp=pp, yb_=yb_, GT=GT: e.tensor_tensor(out=yb_[:], in0=pp[:], in1=GT[:], op=ALU.mult), reads=[pp, GT], writes=[yb_])
                    k.op("pool", lambda e, x_=x_, yb_=yb_: e.tensor_tensor(out=x_[:], in0=x_[:], in1=yb_[:], op=ALU.add), reads=[x_, yb_], writes=[x_])
                    k.dma("sp", lambda e, x_=x_, t=t: e.dma_start(out=X[rows(t), :], in_=x_[:]), reads=[x_], writes=[k.dbuf(("X", t))])

                NTe = ROWS // 128
                pairs = [list(range(t, min(t + 2, NTe))) for t in range(0, NTe, 2)]

                def emitA(pr):
                    k.play_interleaved([k.capture(lambda t=t: epiA(t)) for t in pr])
                emitA(pairs[0])
                for pi, pr in enumerate(pairs):
                    if pi + 1 < len(pairs):
                        emitA(pairs[pi + 1])
                    for t in pr:
                        epiB(t)
            k.barrier()

        def rwkv_layer(l):
            i = l // 2
            rows = lambda t: slice(t * 128, (t + 1) * 128)
            with ExitStack() as st:
                mods = load_mod(st, l, [0, 1])
                make_gain(st, l, 0, mods, 1)
                xts = [k.sb(st, "r1x%d" % j, [128, D], F32) for j in range(3)]
                hs = [k.sb(st, "r1h%d" % j, [128, D], F32) for j in range(3)]
                junk = k.sb(st, "r1junk", [128, D], F32)
                sss = [k.sb(st, "r1ss%d" % j, [128, 1], F32) for j in range(3)]
                NT1 = ROWS // 128

                def r1load(t):
                    xt = xts[t % 3]
                    k.dma("sp", lambda e: e.dma_start(out=xt[:], in_=X[rows(t), :]), reads=[k.dbuf(("X", t))], writes=[xt])
                r1load(0)
                if NT1 > 1:
                    r1load(1)
                for t in range(NT1):
                    cnd = tile_cond(t)
                    xt, hb, ss = xts[t % 3], hs[t % 3], sss[t % 3]
                    norm_mod(xt, hb, mods[(cnd, 1)], mods[(cnd, 0)], ss, junk)
                    if t + 2 < NT1:
                        r1load(t + 2)
                    k.dma("sp", lambda e, hb=hb, t=t: e.dma_start(out=SC["Hh"][rows(t), :], in_=hb[:]), reads=[hb], writes=[k.dbuf(("Hh", t))])
            k.barrier()

            with ExitStack() as st:
                wr = [k.sb(st, "wrkv%d" % j, [128, NCH, D], BF16) for j in range(3)]
                for j in range(3):
                    k.dma("sp", lambda e, j=j: e.dma_start(out=wr[j][:], in_=wb["rwkv_w_rkv"][(i * 3 + j) * D:(i * 3 + j + 1) * D, :].rearrange("(c p) f -> p c f", p=128)),
                          reads=wread("rwkv_w_rkv"), writes=[wr[j]])
                w1 = [k.sb(st, "w1_%d" % d, [128, NCH, 64], BF16) for d in range(2)]
                a1 = [k.sb(st, "a1_%d" % d, [128, NCH, 64], BF16) for d in range(2)]
                w2 = [k.sb(st, "w2_%d" % d, [64, D], BF16) for d in range(2)]
                a2 = [k.sb(st, "a2_%d" % d, [64, D], BF16) for d in range(2)]
                g1 = k.sb(st, "g1", [128, NCH, 128], BF16)
                g2 = k.sb(st, "g2", [128, D], BF16)
                for d in range(2):
                    o = i * 2 + d
                    k.dma("sp", lambda e, d=d, o=o: e.dma_start(out=w1[d][:], in_=wb["rwkv_w1"][o * D:(o + 1) * D, :].rearrange("(c p) f -> p c f", p=128)), reads=wread("rwkv_w1"), writes=[w1[d]])
                    k.dma("sp", lambda e, d=d, o=o: e.dma_start(out=a1[d][:], in_=wb["rwkv_a1"][o * D:(o + 1) * D, :].rearrange("(c p) f -> p c f", p=128)), reads=wread("rwkv_a1"), writes=[a1[d]])
                    k.dma("sp", lambda e, d=d, o=o: e.dma_start(out=w2[d][:], in_=wb["rwkv_w2"][o * 64:(o + 1) * 64, :]), reads=wread("rwkv_w2"), writes=[w2[d]])
                    k.dma("sp", lambda e, d=d, o=o: e.dma_start(out=a2[d][:], in_=wb["rwkv_a2"][o * 64:(o + 1) * 64, :]), reads=wread("rwkv_a2"), writes=[a2[d]])
                k.dma("sp", lambda e: e.dma_start(out=g1[:], in_=wb["rwkv_g1"][i * D:(i + 1) * D, :].rearrange("(c p) f -> p c f", p=128)), reads=wread("rwkv_g1"), writes=[g1])
                k.dma("sp", lambda e: e.dma_start(out=g2[:], in_=wb["rwkv_g2"][i * 128:(i + 1) * 128, :]), reads=wread("rwkv_g2"), writes=[g2])
                mu = k.sb(st, "mu", [128, 6, NCH], F32)
                k.dma("sp", lambda e: e.dma_start(out=mu[:], in_=rw_mu[i].rearrange("m (c p) -> p m c", p=128), allow_slow_non_contiguous=True), writes=[mu])
                w0b = [bcast_load(st, "w0b%d" % d, rw_vec["rwkv_w0"][i, d:d + 1, :]) for d in range(2)]
                a0b = [bcast_load(st, "a0b%d" % d, rw_vec["rwkv_a0"][i, d:d + 1, :]) for d in range(2)]
                kkb = bcast_load(st, "kkb", rw_vec["rwkv_k_k"][i:i + 1, :])
                kab = bcast_load(st, "kab", rw_vec["rwkv_k_a"][i:i + 1, :])
                rkb = bcast_load(st, "rkb", rw_vec["rwkv_r_k"][i:i + 1, :])
                h_, hp, hn, dl = [k.sb(st, n, [128, D], F32) for n in ("h_", "hp", "hn", "dl")]
                hb16, db16 = [k.sb(st, n, [128, D], BF16) for n in ("hb16", "db16")]
                hT, dT = [k.sb(st, n, [128, NCH, 128], BF16) for n in ("hT", "dT")]
                xsT = [k.sb(st, "xsT%d" % m, [128, NCH, 128], BF16) for m in range(6)]
                th = [k.sb(st, "th%d" % d, [64, 128], BF16) for d in range(2)]
                la = [k.sb(st, "la%d" % d, [64, 128], BF16) for d in range(2)]
                sgT = k.sb(st, "sgT", [128, 128], BF16)
                F = {n: k.sb(st, "f_" + n, [128, D], F32) for n in ("rf", "kf", "vf", "kk", "kka", "rr", "zt0", "zt1", "at0", "at1", "t10", "t11", "ksum")}
                Fb = {n: k.sb(st, "fb_" + n, [128, D], BF16) for n in ("kd0", "kd1", "bd0", "bd1", "gf")}
                ssq = k.sb(st, "ssq", [128, H], F32)
                coef = k.sb(st, "coef", [128, H], F32)
                ptr = [k.ps(st, "r2ptr%d" % j, [128, NCH, 128], BF16) for j in range(2)]
                pbig = [k.ps(st, "r2pb%d" % j, [128, D]) for j in range(2)]
                psm = [k.ps(st, "r2ps%d" % j, [128, 128]) for j in range(2)]
                cb = [0]
                cs = [0]
                v3 = lambda b: b[:].rearrange("p (h c) -> p h c", h=H)
                tiles = seq_tiles()

                def A1a(t, si, j, n):
                    r0 = t * 128
                    k.dma("sp", lambda e: e.dma_start(out=h_[:], in_=SC["Hh"][rows(t), :]), reads=[k.dbuf(("Hh", t))], writes=[h_])
                    if j == 0:
                        k.dma("sp", lambda e: e.dma_start(out=hp[0:1, :], in_=zeros_in[0:1, :]), writes=[hp])
                        k.dma("sp", lambda e: e.dma_start(out=hp[1:128, :], in_=SC["Hh"][r0:r0 + 127, :]), reads=[k.dbuf(("Hh", t))], writes=[hp])
                    else:
                        k.dma("sp", lambda e: e.dma_start(out=hp[:], in_=SC["Hh"][r0 - 1:r0 + 127, :]), reads=[k.dbuf(("Hh", t)), k.dbuf(("Hh", t - 1))], writes=[hp])
                    if j == n - 1:
                        k.dma("sp", lambda e: e.dma_start(out=hn[127:128, :], in_=zeros_in[0:1, :]), writes=[hn])
                        k.dma("sp", lambda e: e.dma_start(out=hn[0:127, :], in_=SC["Hh"][r0 + 1:r0 + 128, :]), reads=[k.dbuf(("Hh", t))], writes=[hn])
                    else:
                        k.dma("sp", lambda e: e.dma_start(out=hn[:], in_=SC["Hh"][r0 + 1:r0 + 129, :]), reads=[k.dbuf(("Hh", t)), k.dbuf(("Hh", t + 1))], writes=[hn])
                    k.op("pool", lambda e: e.tensor_tensor(out=hp[:], in0=hp[:], in1=hn[:], op=ALU.add), reads=[hp, hn], writes=[hp])
                    k.op("dve", lambda e: e.scalar_tensor_tensor(out=dl[:], in0=hp[:], scalar=0.5, in1=h_[:], op0=ALU.mult, op1=ALU.subtract), reads=[hp, h_], writes=[dl])
                    k.op("act", lambda e: e.copy(out=hb16[:], in_=h_[:]), reads=[h_], writes=[hb16])
                    k.op("act", lambda e: e.copy(out=db16[:], in_=dl[:]), reads=[dl], writes=[db16])
                    for (src, dst) in ((hb16, hT), (db16, dT)):
                        p = ptr[cb[0] % 2]
                        cb[0] += 1
                        for kc in range(NCH):
                            k.op("pe", lambda e, p=p, src=src, kc=kc: e.transpose(out=p[:, kc, :], in_=src[:, kc * 128:(kc + 1) * 128], identity=ident_b[:]),
                                 reads=[src, ident_b], writes=[p], inc=(kc == NCH - 1))
                        k.op("act", lambda e, p=p, dst=dst: e.copy(out=dst[:], in_=p[:]), reads=[p], writes=[dst])

                def A1b():
                    for m in range(6):
                        for kc in range(NCH):
                            k.op("dve", lambda e, m=m, kc=kc: e.scalar_tensor_tensor(out=xsT[m][:, kc, :], in0=dT[:, kc, :], scalar=mu[:, m, kc:kc + 1], in1=hT[:, kc, :], op0=ALU.mult, op1=ALU.add),
                                 reads=[dT, hT, mu], writes=[xsT[m]])

                def A23(t):
                    def proj(m, w, dstf):
                        p = pbig[cb[0] % 2]
                        cb[0] += 1
                        for half in range(2):
                            for kc in range(NCH):
                                k.op("pe", lambda e, p=p, half=half, kc=kc: e.matmul(p[:, half * 512:(half + 1) * 512], lhsT=xsT[m][:, kc, :], rhs=w[:, kc, half * 512:(half + 1) * 512], start=(kc == 0), stop=(kc == NCH - 1)),
                                     reads=[xsT[m], w], writes=[p], inc=(kc == NCH - 1))
                        k.op("act", lambda e, p=p: e.copy(out=dstf[:], in_=p[:]), reads=[p], writes=[dstf])
                    proj(0, wr[0], F["rf"])
                    proj(1, wr[1], F["kf"])
                    proj(2, wr[2], F["vf"])

                    def lora1(wl, m, dst, func, npart):
                        p = psm[cs[0] % 2]
                        cs[0] += 1
                        for kc in range(NCH):
                            k.op("pe", lambda e, p=p, kc=kc: e.matmul(p[0:npart, :], lhsT=wl[:, kc, :], rhs=xsT[m][:, kc, :], start=(kc == 0), stop=(kc == NCH - 1)),
                                 reads=[wl, xsT[m]], writes=[p], inc=(kc == NCH - 1))
                        k.op("act", lambda e, p=p: e.activation(out=dst[:], in_=p[0:npart, :], func=func), reads=[p], writes=[dst])
                    for d in range(2):
                        lora1(w1[d], 3, th[d], AF.Tanh, 64)
                        lora1(a1[d], 4, la[d], AF.Copy, 64)
                    lora1(g1, 5, sgT, AF.Sigmoid, 128)

                def B(t):
                    k.dma("pool", lambda e: e.dma_start(out=SC["R"][rows(t), :], in_=F["rf"][:]), reads=[F["rf"]], writes=[k.dbuf(("R", t))])
                    k.dma("pool", lambda e: e.dma_start(out=SC["V"][rows(t), :], in_=F["vf"][:]), reads=[F["vf"]], writes=[k.dbuf(("V", t))])

                    def lora2(lhs, w, p):
                        for half in range(2):
                            k.op("pe", lambda e, half=half: e.matmul(p[:, half * 512:(half + 1) * 512], lhsT=lhs[:], rhs=w[:, half * 512:(half + 1) * 512], start=True, stop=True),
                                 reads=[lhs, w], writes=[p])
                    sqb = F["t10"]
                    k.op("dve", lambda e: e.tensor_tensor(out=F["kk"][:], in0=F["kf"][:], in1=kkb[:], op=ALU.mult), reads=[F["kf"], kkb], writes=[F["kk"]])
                    k.op("pool", lambda e: e.tensor_tensor(out=sqb[:], in0=F["kk"][:], in1=F["kk"][:], op=ALU.mult), reads=[F["kk"]], writes=[sqb])
                    k.op("dve", lambda e: e.tensor_reduce(out=ssq[:], in_=v3(sqb), axis=AX.X, op=ALU.add), reads=[sqb], writes=[ssq])
                    k.op("act", lambda e: e.activation(out=ssq[:], in_=ssq[:], func=AF.Ln, bias=1e-12), reads=[ssq], writes=[ssq])
                    k.op("act", lambda e: e.activation(out=ssq[:], in_=ssq[:], func=AF.Exp, scale=-0.5), reads=[ssq], writes=[ssq])
                    k.op("dve", lambda e: e.tensor_tensor(out=v3(F["kk"]), in0=v3(F["kk"]), in1=ssq[:].unsqueeze(2).to_broadcast([128, H, 64]), op=ALU.mult), reads=[F["kk"], ssq], writes=[F["kk"]])
                    k.dma("pool", lambda e: e.dma_start(out=SC["KK"][rows(t), :], in_=F["kk"][:]), reads=[F["kk"]], writes=[k.dbuf(("KK", t))])
                    k.op("pool", lambda e: e.tensor_tensor(out=F["kka"][:], in0=F["kf"][:], in1=kab[:], op=ALU.mult), reads=[F["kf"], kab], writes=[F["kka"]])
                    k.op("pool", lambda e: e.tensor_tensor(out=F["rr"][:], in0=F["rf"][:], in1=rkb[:], op=ALU.mult), reads=[F["rf"], rkb], writes=[F["rr"]])
                    for d in range(2):
                        zt, at, t1, kd, bd = F["zt%d" % d], F["at%d" % d], F["t1%d" % d], Fb["kd%d" % d], Fb["bd%d" % d]
                        p = pbig[cb[0] % 2]
                        cb[0] += 1
                        lora2(th[d], w2[d], p)
                        k.op("dve", lambda e, p=p, d=d, zt=zt: e.tensor_tensor(out=zt[:], in0=p[:], in1=w0b[d][:], op=ALU.add), reads=[p, w0b[d]], writes=[zt])
                        k.op("act", lambda e, zt=zt: e.activation(out=zt[:], in_=zt[:], func=AF.Sigmoid), reads=[zt], writes=[zt])
                        k.dma("sp", lambda e, d=d, zt=zt: e.dma_start(out=SC["LW%d" % d][rows(t), :], in_=zt[:]), reads=[zt], writes=[k.dbuf(("LW%d" % d, t))])
                        p = pbig[cb[0] % 2]
                        cb[0] += 1
                        lora2(la[d], a2[d], p)
                        k.op("dve", lambda e, p=p, d=d, at=at: e.tensor_tensor(out=at[:], in0=p[:], in1=a0b[d][:], op=ALU.add), reads=[p, a0b[d]], writes=[at])
                        k.op("act", lambda e, at=at: e.activation(out=at[:], in_=at[:], func=AF.Sigmoid), reads=[at], writes=[at])
                        k.op("dve", lambda e, at=at, t1=t1: e.scalar_tensor_tensor(out=t1[:], in0=at[:], scalar=-1.0, in1=F["kka"][:], op0=ALU.add, op1=ALU.mult), reads=[at, F["kka"]], writes=[t1])
                        k.op("pool", lambda e, t1=t1, kd=kd: e.tensor_tensor(out=kd[:], in0=t1[:], in1=F["kf"][:], op=ALU.add), reads=[t1, F["kf"]], writes=[kd])
                        k.op("pool", lambda e, at=at, bd=bd: e.tensor_tensor(out=bd[:], in0=F["kk"][:], in1=at[:], op=ALU.mult), reads=[F["kk"], at], writes=[bd])
                        k.dma("sp", lambda e, d=d, kd=kd: e.dma_start(out=SC["K%d" % d][rows(t), :], in_=kd[:]), reads=[kd], writes=[k.dbuf(("K%d" % d, t))])
                        k.dma("sp", lambda e, d=d, bd=bd: e.dma_start(out=SC["B%d" % d][rows(t), :], in_=bd[:]), reads=[bd], writes=[k.dbuf(("B%d" % d, t))])
                    k.op("pool", lambda e: e.tensor_tensor(out=F["ksum"][:], in0=Fb["kd0"][:], in1=Fb["kd1"][:], op=ALU.add), reads=[Fb["kd0"], Fb["kd1"]], writes=[F["ksum"]])
                    k.op("dve", lambda e: e.tensor_tensor(out=F["t10"][:], in0=F["rr"][:], in1=F["ksum"][:], op=ALU.mult), reads=[F["rr"], F["ksum"]], writes=[F["t10"]])
                    k.op("dve", lambda e: e.tensor_reduce(out=coef[:], in_=v3(F["t10"]), axis=AX.X, op=ALU.add), reads=[F["t10"]], writes=[coef])
                    k.dma("sp", lambda e: e.dma_start(out=SC["COEF"][rows(t), :], in_=coef[:]), reads=[coef], writes=[k.dbuf(("COEF", t))])
                    p = pbig[cb[0] % 2]
                    cb[0] += 1
                    lora2(sgT, g2, p)
                    k.op("act", lambda e, p=p: e.copy(out=Fb["gf"][:], in_=p[:]), reads=[p], writes=[Fb["gf"]])
                    k.dma("sp", lambda e: e.dma_start(out=SC["G"][rows(t), :], in_=Fb["gf"][:]), reads=[Fb["gf"]], writes=[k.dbuf(("G", t))])

                A1a(*tiles[0])
                A1b()
                for ti, (t, si, j, n) in enumerate(tiles):
                    trickle(1)
                    A23(t)
                    if ti + 1 < len(tiles):
                        A1a(*tiles[ti + 1])
                    B(t)
                    if ti + 1 < len(tiles):
                        A1b()
            k.barrier()
            if "scan" in cfg.phases:
                for d in range(2):
                    scan_pass(l, d)
                rwkv_epilogue(l)

        def na_layer(l):
            drain_casts()
            i = l // 2
            rows = lambda t: slice(t * 128, (t + 1) * 128)
            hsl = lambda h: slice(h * 64, (h + 1) * 64)
            NT = ROWS // 128
            S0 = NP * TP
            with ExitStack() as st:
                Ecol = k.sb(st, "Ecol", [31, 4096], F32)
                CM = k.sb(st, "CM", [15, 4096], F32)
                J15 = k.sb(st, "J15", [15, 15], F32)
                k.dma("sp", lambda e: e.dma_start(out=Ecol[:], in_=ecol_in[:, :]), writes=[Ecol])
                k.dma("sp", lambda e: e.dma_start(out=CM[:], in_=colmask_in[0:1, :].broadcast_to([15, 4096])), writes=[CM])
                k.dma("sp", lambda e: e.dma_start(out=J15[:], in_=j15_in[:, :]), writes=[J15])
                rp = [k.sb(st, "rp%d" % j, [15, 31], F32) for j in range(2)]
                rrT = [k.sb(st, "rrT%d" % j, [31, 15], BF16) for j in range(2)]
                Ecb = k.sb(st, "Ecb", [31, 4096], BF16)
                k.op("act", lambda e: e.copy(out=Ecb[:], in_=Ecol[:]), reads=[Ecol], writes=[Ecb])
                t1 = [k.sb(st, "t1r%d" % j, [15, 4096], F32) for j in range(2)]
                pr_ = [k.ps(st, "n0p%d" % j, [31, 15]) for j in range(2)]
                pt_ = [k.ps(st, "n0t%d" % j, [15, 512]) for j in range(4)]
                c = 0
                for h in range(H):
                    a, b_, t_ = rp[h % 2], rrT[h % 2], t1[h % 2]
                    k.dma("sp", lambda e, a=a, h=h: e.dma_start(out=a[:], in_=rpb_in[i, h]), writes=[a])
                    p = pr_[h % 2]
                    k.op("pe", lambda e, p=p, a=a: e.matmul(p[:], lhsT=a[:], rhs=J15[:], start=True, stop=True), reads=[a, J15], writes=[p])
                    k.op("act", lambda e, p=p, b_=b_: e.copy(out=b_[:], in_=p[:]), reads=[p], writes=[b_])
                    for cb in range(8):
                        q = pt_[c % 4]
                        c += 1
                        k.op("pe", lambda e, q=q, b_=b_, cb=cb: e.matmul(q[:], lhsT=b_[:], rhs=Ecb[:, cb * 512:(cb + 1) * 512], start=True, stop=True), reads=[b_, Ecb], writes=[q])
                        k.op("dve", lambda e, q=q, t_=t_, cb=cb: e.tensor_tensor(out=t_[:, cb * 512:(cb + 1) * 512], in0=q[:], in1=CM[:, cb * 512:(cb + 1) * 512], op=ALU.add), reads=[q, CM], writes=[t_])
                    k.dma("sp", lambda e, t_=t_, h=h: e.dma_start(out=T1R[h], in_=t_[:]), reads=[t_], writes=[k.dbuf(("T1R", h))])
            k.barrier()

            if cfg.na_stop < 1:
                return
            with ExitStack() as st:
                mods = load_mod(st, l, [0, 1])
                make_gain(st, l, 0, mods, 1)
                wq = k.sb(st, "wqkv", [128, NCH, 3 * D], BF16)
                for j in range(3):
                    k.dma("sp", lambda e, j=j: e.dma_start(out=wq[:, :, j * D:(j + 1) * D], in_=wb["na_w_qkv"][i * D:(i + 1) * D, j * D:(j + 1) * D].rearrange("(c p) f -> p c f", p=128)),
                          reads=wread("na_w_qkv"), writes=[wq])
                qg = k.sb(st, "qg", [128, 64], F32)
                kg = k.sb(st, "kg", [128, 64], F32)
                k.dma("sp", lambda e: e.dma_start(out=qg[:], in_=qg_in[i:i + 1, :].broadcast_to([128, 64])), writes=[qg])
                k.dma("sp", lambda e: e.dma_start(out=kg[:], in_=kg_in[i:i + 1, :].broadcast_to([128, 64])), writes=[kg])
                k.op("dve", lambda e: e.tensor_scalar(out=qg[:], in0=qg[:], scalar1=0.125, scalar2=0.0, op0=ALU.mult, op1=ALU.add), reads=[qg], writes=[qg])
                xt = [k.sb(st, "n1x%d" % j, [128, D], F32) for j in range(2)]
                hb = [k.sb(st, "n1h%d" % j, [128, D], BF16) for j in range(2)]
                junk = k.sb(st, "n1junk", [128, D], F32)
                ss = [k.sb(st, "n1ss%d" % j, [128, 1], F32) for j in range(2)]
                hT = [k.sb(st, "n1hT%d" % j, [128, NCH, 128], BF16) for j in range(2)]
                qf2 = [k.sb(st, "n1q%d" % j, [128, D], F32) for j in range(2)]
                kf2 = [k.sb(st, "n1k%d" % j, [128, D], F32) for j in range(2)]
                vf2 = [k.sb(st, "n1v%d" % j, [128, D], F32) for j in range(2)]
                sq = k.sb(st, "n1sq", [128, D], F32)
                qb, kb_, vb_ = [k.sb(st, n, [128, D], BF16) for n in ("n1qb", "n1kb", "n1vb")]
                qTs = k.sb(st, "n1qT", [64, H, 128], BF16)
                kTs = k.sb(st, "n1kT", [64, H, 128], BF16)
                st16 = [k.sb(st, "n1st%d" % j, [128, H], F32) for j in range(2)]
                ptr = k.ps(st, "n1ptr", [128, NCH, 128], BF16)
                pq = [k.ps(st, "n1pq%d" % j, [128, D]) for j in range(3)]
                ptr2 = k.ps(st, "n1ptr2", [64, 8, 128], BF16)
                v3 = lambda b: b[:].rearrange("p (h c) -> p h c", h=H)
                def n1L(t):
                    x_ = xt[t % 2]
                    k.dma("sp", lambda e, x_=x_, t=t: e.dma_start(out=x_[:], in_=X[rows(t), :]), reads=[k.dbuf(("X", t))], writes=[x_])

                def n1A(t):
                    cnd = tile_cond(t)
                    x_, h_, s_, hT_ = xt[t % 2], hb[t % 2], ss[t % 2], hT[t % 2]
                    qf, kf, vf = qf2[t % 2], kf2[t % 2], vf2[t % 2]
                    norm_mod(x_, h_, mods[(cnd, 1)], mods[(cnd, 0)], s_, junk)

                def n1Ape(t):
                    x_, h_, s_, hT_ = xt[t % 2], hb[t % 2], ss[t % 2], hT[t % 2]
                    qf, kf, vf = qf2[t % 2], kf2[t % 2], vf2[t % 2]
                    for kc in range(NCH):
                        k.op("pe", lambda e, h_=h_, kc=kc: e.transpose(out=ptr[:, kc, :], in_=h_[:, kc * 128:(kc + 1) * 128], identity=ident_b[:]), reads=[h_, ident_b], writes=[ptr], inc=(kc == NCH - 1))
                    k.op("act", lambda e, hT_=hT_: e.copy(out=hT_[:], in_=ptr[:]), reads=[ptr], writes=[hT_])
                    for j, dstf in enumerate((qf, kf, vf)):
                        p = pq[j]
                        for half in range(2):
                            for kc in range(NCH):
                                k.op("pe", lambda e, p=p, half=half, kc=kc, j=j, hT_=hT_: e.matmul(p[:, half * 512:(half + 1) * 512], lhsT=hT_[:, kc, :], rhs=wq[:, kc, j * D + half * 512:j * D + (half + 1) * 512], start=(kc == 0), stop=(kc == NCH - 1)),
                                     reads=[hT_, wq], writes=[p], inc=(kc == NCH - 1))

                def n1A3(t):
                    for j, dstf in enumerate((qf2[t % 2], kf2[t % 2], vf2[t % 2])):
                        p = pq[j]
                        k.op("act", lambda e, p=p, dstf=dstf: e.copy(out=dstf[:], in_=p[:]), reads=[p], writes=[dstf])

                def n1B(t):
                    cnd = tile_cond(t)
                    qf, kf, vf = qf2[t % 2], kf2[t % 2], vf2[t % 2]
                    for (src, gn, dstb, sti) in ((qf, qg, qb, 0), (kf, kg, kb_, 1)):
                        s16 = st16[sti]
                        k.op("pool", lambda e, src=src: e.tensor_tensor(out=sq[:], in0=src[:], in1=src[:], op=ALU.mult), reads=[src], writes=[sq])
                        k.op("dve", lambda e, s16=s16: e.tensor_reduce(out=s16[:], in_=v3(sq), axis=AX.X, op=ALU.add), reads=[sq], writes=[s16])
                        k.op("act", lambda e, s16=s16: e.activation(out=s16[:], in_=s16[:], func=AF.Ln, scale=1.0 / 64, bias=1e-6), reads=[s16], writes=[s16])
                        k.op("act", lambda e, s16=s16: e.activation(out=s16[:], in_=s16[:], func=AF.Exp, scale=-0.5), reads=[s16], writes=[s16])
                        k.op("dve", lambda e, src=src, s16=s16: e.tensor_tensor(out=v3(src), in0=v3(src), in1=s16[:].unsqueeze(2).to_broadcast([128, H, 64]), op=ALU.mult), reads=[src, s16], writes=[src])
                        k.op("dve", lambda e, src=src, gn=gn: e.tensor_tensor(out=v3(src), in0=v3(src), in1=gn[:].unsqueeze(1).to_broadcast([128, H, 64]), op=ALU.mult), reads=[src, gn], writes=[src])
                        k.op("act", lambda e, src=src, dstb=dstb: e.copy(out=dstb[:], in_=src[:]), reads=[src], writes=[dstb])
                    k.op("act", lambda e, vf=vf: e.copy(out=vb_[:], in_=vf[:]), reads=[vf], writes=[vb_])
                    k.dma("sp", lambda e, t=t: e.dma_start(out=VV[rows(t), :], in_=vb_[:]), reads=[vb_], writes=[k.dbuf(("VV", t))])

                def n1BT(t):
                    cnd = tile_cond(t)
                    qf, kf, vf = qf2[t % 2], kf2[t % 2], vf2[t % 2]
                    for (srcb, dsts, dname, DR) in ((qb, qTs, "QT", QT), (kb_, kTs, "KT", KT)):
                        for g8 in range(2):
                            for hh in range(8):
                                h = g8 * 8 + hh
                                k.op("pe", lambda e, hh=hh, h=h, srcb=srcb: e.transpose(out=ptr2[:, hh, :], in_=srcb[:, hsl(h)], identity=ident_b[:]), reads=[srcb, ident_b], writes=[ptr2], inc=(hh == 7))
                            k.op("act", lambda e, g8=g8, dsts=dsts: e.copy(out=dsts[:, g8 * 8:(g8 + 1) * 8, :], in_=ptr2[:]), reads=[ptr2], writes=[dsts])
                        k.dma("sp", lambda e, t=t, dsts=dsts, DR=DR: e.dma_start(out=DR[:, :, rows(t)].rearrange("h d t -> d h t"), in_=dsts[:]), reads=[dsts], writes=[k.dbuf((dname, t))])
                    if cnd == 0:
                        bi, t0 = (t * 128) // TP, (t * 128) % TP
                        k.dma("sp", lambda e, bi=bi, t0=t0, kf=kf: e.dma_start(out=nk_out[bi, i, :, t0:t0 + 128, :].rearrange("h t d -> t h d"), in_=v3(kf)), reads=[kf], writes=[k.dbuf(("nk", bi, i, t0))])
                        k.dma("sp", lambda e, bi=bi, t0=t0, vf=vf: e.dma_start(out=nv_out[bi, i, :, t0:t0 + 128, :].rearrange("h t d -> t h d"), in_=v3(vf)), reads=[vf], writes=[k.dbuf(("nv", bi, i, t0))])

                n1L(0)
                if NT > 1:
                    n1L(1)
                n1A(0)
                n1Ape(0)
                n1A3(0)
                for t in range(NT):
                    if t + 2 < NT:
                        n1L(t + 2)
                    if t + 1 < NT:
                        n1A(t + 1)
                        n1Ape(t + 1)
                    n1B(t)
                    n1BT(t)
                    if t + 1 < NT:
                        n1A3(t + 1)
            k.barrier()

            if cfg.na_stop < 2:
                return
            with ExitStack() as st:
                NB = TS // 128
                HS = []
                for j2 in range(2):
                    d_ = {}
                    d_["qTh"] = k.sb(st, "qTh%d" % j2, [64, TS], BF16)
                    d_["kTh"] = k.sb(st, "kTh%d" % j2, [64, TS], BF16)
                    d_["Vx"] = k.sb(st, "Vx%d" % j2, [128, NB, 65], BF16)
                    d_["qTp"] = k.sb(st, "qTp%d" % j2, [64, NP * TP], BF16)
                    d_["kTp"] = k.sb(st, "kTp%d" % j2, [64, NP * TP], BF16)
                    d_["Vxp"] = k.sb(st, "Vxp%d" % j2, [128, NP * TP // 128, 65], BF16)
                    d_["ck"] = k.sb(st, "ck%d" % j2, [128, 2, 64], F32)
                    d_["ckb"] = k.sb(st, "ckb%d" % j2, [128, 2, 64], BF16)
                    d_["cv"] = k.sb(st, "cv%d" % j2, [128, 2, 64], F32)
                    d_["KTc"] = k.sb(st, "KTc%d" % j2, [64, 256], BF16)
                    d_["Vcx"] = k.sb(st, "Vcx%d" % j2, [128, 2, 65], BF16)
                    d_["BB"] = [k.sb(st, "BB%d_%d" % (j2, v), [128, 16, 64], F32) for v in range(2)]
                    d_["Oh"] = k.sb(st, "Oh%d" % j2, [128, NB, 64], BF16)
                    d_["Ohp"] = k.sb(st, "Ohp%d" % j2, [128, NP * TP // 128, 64], BF16)
                    HS.append(d_)
                ND = 3
                PT = [k.sb(st, "PT%d" % j, [128, 1024], BF16) for j in range(ND)]
                tmpb = [k.sb(st, "tmpb%d" % j, [128, 640], F32) for j in range(ND)]
                rc = [k.sb(st, "rc%d" % j, [128, 1], F32) for j in range(ND)]
                SA = [k.ps(st, "nSA%d" % j, [128, 1024]) for j in range(ND)]
                PO = [k.ps(st, "nPO%d" % j, [128, 65]) for j in range(1)]
                ptc = k.ps(st, "nptc", [64, 2, 128], BF16)
                for d_ in HS:
                    for nm in ("Vx", "Vxp", "Vcx"):
                        k.op("pool", lambda e, b=d_[nm]: e.memset(b[:], 1.0), writes=[d_[nm]])
                uc = [0]

                pend = []

                def unit(qT_ap, kts, vxs, nbias, bias_ap, o_ap, rd, bbr=()):
                    bbr = list(bbr)
                    u = uc[0]
                    uc[0] += 1
                    S, P, tb, po, r_ = SA[u % ND], PT[u % ND], tmpb[u % ND], PO[0], rc[u % ND]
                    nb = len(kts)
                    for b in range(nb):
                        k.op("pe", lambda e, b=b: e.matmul(S[:, b * 128:(b + 1) * 128], lhsT=kts[b], rhs=qT_ap, start=True, stop=True), reads=rd[:-1], writes=[S], inc=(b == nb - 1))
                    if nbias:
                        k.op("dve", lambda e: e.tensor_tensor(out=tb[:, 0:nbias * 128], in0=S[:, 0:nbias * 128], in1=bias_ap, op=ALU.add), reads=[S] + bbr, writes=[tb])
                        k.op("act", lambda e: e.activation(out=P[:, 0:nbias * 128], in_=tb[:, 0:nbias * 128], func=AF.Exp), reads=[tb], writes=[P])
                    if nb > nbias:
                        k.op("act", lambda e: e.activation(out=P[:, nbias * 128:nb * 128], in_=S[:, nbias * 128:nb * 128], func=AF.Exp), reads=[S], writes=[P])

                    def partB():
                        for b in range(nb):
                            k.op("pe", lambda e, b=b: e.matmul(po[:], lhsT=P[:, b * 128:(b + 1) * 128], rhs=vxs[b], start=(b == 0), stop=(b == nb - 1)), reads=[P] + rd[:-1], writes=[po], inc=(b == nb - 1))
                        k.op("dve", lambda e: e.reciprocal(out=r_[:], in_=po[:, 64:65]), reads=[po], writes=[r_])
                        k.op("dve", lambda e: e.tensor_scalar(out=o_ap, in0=po[:, 0:64], scalar1=r_[:, 0:1], scalar2=0.0, op0=ALU.mult, op1=ALU.add), reads=[po, r_], writes=rd[-1:])
                    pend.append(partB)
                    if len(pend) > ND - 1:
                        pend.pop(0)()

                def flush():
                    while pend:
                        pend.pop(0)()

                def load_head(h):
                    d_ = HS[h % 2]
                    qTp, kTp, Vxp, qTh, kTh, Vx, ck, ckb, cv, KTc, Vcx, BB = (d_[n_] for n_ in ("qTp", "kTp", "Vxp", "qTh", "kTh", "Vx", "ck", "ckb", "cv", "KTc", "Vcx", "BB"))
                    k.dma("sp", lambda e: e.dma_start(out=qTp[:], in_=QT[h, :, 0:S0]), reads=[k.dbuf(("QT", t)) for t in range(S0 // 128)], writes=[qTp])
                    k.dma("sp", lambda e: e.dma_start(out=kTp[:], in_=KT[h, :, 0:S0]), reads=[k.dbuf(("KT", t)) for t in range(S0 // 128)], writes=[kTp])
                    k.dma("sp", lambda e: e.dma_start(out=Vxp[:, :, 0:64], in_=VV[0:S0, hsl(h)].rearrange("(b t) c -> t b c", t=128)), reads=[k.dbuf(("VV", t)) for t in range(S0 // 128)], writes=[Vxp])
                    k.dma("sp", lambda e: e.dma_start(out=qTh[:], in_=QT[h, :, S0:ROWS]), reads=[k.dbuf(("QT", t)) for t in range(S0 // 128, NT)], writes=[qTh])
                    k.dma("sp", lambda e: e.dma_start(out=kTh[:], in_=KT[h, :, S0:ROWS]), reads=[k.dbuf(("KT", t)) for t in range(S0 // 128, NT)], writes=[kTh])
                    k.dma("sp", lambda e: e.dma_start(out=Vx[:, :, 0:64], in_=VV[S0:ROWS, hsl(h)].rearrange("(b t) c -> t b c", t=128)), reads=[k.dbuf(("VV", t)) for t in range(S0 // 128, NT)], writes=[Vx])
                    k.dma("sp", lambda e: e.dma_start(out=ck[:], in_=ck_in[i, h].rearrange("(b t) d -> t b d", t=128)), writes=[ck])
                    k.dma("sp", lambda e: e.dma_start(out=cv[:], in_=cv_in[i, h].rearrange("(b t) d -> t b d", t=128)), writes=[cv])
                    k.op("act", lambda e: e.copy(out=ckb[:], in_=ck[:]), reads=[ck], writes=[ckb])
                    k.op("act", lambda e: e.copy(out=Vcx[:, :, 0:64], in_=cv[:]), reads=[cv], writes=[Vcx])
                    for b in range(2):
                        k.op("pe", lambda e, b=b: e.transpose(out=ptc[:, b, :], in_=ckb[:, b, :], identity=ident_b[:]), reads=[ckb, ident_b], writes=[ptc], inc=(b == 1))
                    k.op("act", lambda e: e.copy(out=KTc[:].rearrange("p (b t) -> p b t", b=2), in_=ptc[:]), reads=[ptc], writes=[KTc])
                    for v, (lo, hi) in enumerate(((4, 11), (0, 14))):
                        k.op("pool", lambda e, v=v: e.memset(BB[v][:], NEG), writes=[BB[v]])
                        for kr2 in range(2):
                            hi2 = min(hi, 14 - kr2)
                            k.dma("sp", lambda e, v=v, kr2=kr2, lo=lo, hi2=hi2: e.dma_start(
                                out=BB[v][kr2 * 64:(kr2 + 1) * 64, lo + 1 + kr2:hi2 + 2 + kr2, :],
                                in_=T1R[h, lo:hi2 + 1, :].rearrange("r (k q) -> k r q", k=64)), reads=[k.dbuf(("T1R", h))], writes=[BB[v]])

                def run_head(h):
                    d_ = HS[h % 2]
                    qTp, kTp, Vxp, qTh, kTh, Vx, KTc, Vcx, BB, Oh, Ohp = (d_[n_] for n_ in ("qTp", "kTp", "Vxp", "qTh", "kTh", "Vx", "KTc", "Vcx", "BB", "Oh", "Ohp"))
                    nbp = TP // 128
                    for sq_ in range(NP):
                        for qt in range(nbp):
                            tq = sq_ * nbp + qt
                            unit(qTp[:, tq * 128:(tq + 1) * 128],
                                 [kTp[:, (sq_ * nbp + b) * 128:(sq_ * nbp + b + 1) * 128] for b in range(nbp)],
                                 [Vxp[:, sq_ * nbp + b, :] for b in range(nbp)], 0, None, Ohp[:, tq, :], [qTp, kTp, Vxp, Ohp])
                    for p in range(NB):
                        if NB >= 5 and 2 <= p <= NB - 3:
                            kbs = [p + 2, p + 1, p, p - 1, p - 2]
                            v = 0
                        else:
                            base = 0 if p < 2 else NB - 4
                            kbs = [base + 3, base + 2, base + 1, base]
                            v = 1
                        j0 = 8 - 2 * (kbs[0] - p)
                        nbz = len(kbs)
                        bias_ap = BB[v][:, j0:j0 + 2 * nbz, :].rearrange("p j q -> p (j q)")
                        kts = [kTh[:, b * 128:(b + 1) * 128] for b in kbs] + [KTc[:, 0:128], KTc[:, 128:256]]
                        vxs = [Vx[:, b, :] for b in kbs] + [Vcx[:, 0, :], Vcx[:, 1, :]]
                        unit(qTh[:, p * 128:(p + 1) * 128], kts, vxs, nbz, bias_ap, Oh[:, p, :], [qTh, kTh, Vx, KTc, Vcx, Oh], bbr=BB)
                    flush()
                    k.dma("sp", lambda e: e.dma_start(out=OO[0:S0, hsl(h)].rearrange("(b t) c -> t b c", t=128), in_=Ohp[:]), reads=[Ohp], writes=[k.dbuf(("OO", "p", h))])
                    k.dma("sp", lambda e: e.dma_start(out=OO[S0:ROWS, hsl(h)].rearrange("(b t) c -> t b c", t=128), in_=Oh[:]), reads=[Oh], writes=[k.dbuf(("OO", "s", h))])

                load_head(0)
                for h in range(H):
                    if h + 1 < H:
                        load_head(h + 1)
                    run_head(h)
            k.barrier()

            if cfg.na_stop < 3:
                return
            with ExitStack() as st:
                mods = load_mod(st, l, [2])
                wo = k.sb(st, "nwo", [128, NCH, D], BF16)
                k.dma("sp", lambda e: e.dma_start(out=wo[:], in_=wb["na_w_o"][i * D:(i + 1) * D, :].rearrange("(c p) f -> p c f", p=128)), reads=wread("na_w_o"), writes=[wo])
                ob = [k.sb(st, "n3o%d" % j, [128, D], BF16) for j in range(2)]
                oT = [k.sb(st, "n3oT%d" % j, [128, NCH, 128], BF16) for j in range(2)]
                xb = [k.sb(st, "n3x%d" % j, [128, D], F32) for j in range(2)]
                yb = [k.sb(st, "n3y%d" % j, [128, D], F32) for j in range(2)]
                ptr3 = [k.ps(st, "n3ptr%d" % j, [128, NCH, 128], BF16) for j in range(2)]
                po = [k.ps(st, "n3po%d" % j, [128, D]) for j in range(2)]
                oread = [b for key, b in k.dram.items() if key[0] == "OO"]
                def n3A(t):
                    o_, x_ = ob[t % 2], xb[t % 2]
                    k.dma("sp", lambda e, o_=o_, t=t: e.dma_start(out=o_[:], in_=OO[rows(t), :]), reads=oread, writes=[o_])
                    k.dma("sp", lambda e, x_=x_, t=t: e.dma_start(out=x_[:], in_=X[rows(t), :]), reads=[k.dbuf(("X", t))], writes=[x_])

                def n3B(t):
                    cnd = tile_cond(t)
                    GT = mods[(cnd, 2)]
                    o_, oT_, x_, y_, pt, pp = ob[t % 2], oT[t % 2], xb[t % 2], yb[t % 2], ptr3[t % 2], po[t % 2]
                    for kc in range(NCH):
                        k.op("pe", lambda e, pt=pt, o_=o_, kc=kc: e.transpose(out=pt[:, kc, :], in_=o_[:, kc * 128:(kc + 1) * 128], identity=ident_b[:]), reads=[o_, ident_b], writes=[pt], inc=(kc == NCH - 1))
                    k.op("act", lambda e, pt=pt, oT_=oT_: e.copy(out=oT_[:], in_=pt[:]), reads=[pt], writes=[oT_])
                    for half in range(2):
                        for kc in range(NCH):
                            k.op("pe", lambda e, pp=pp, oT_=oT_, half=half, kc=kc: e.matmul(pp[:, half * 512:(half + 1) * 512], lhsT=oT_[:, kc, :], rhs=wo[:, kc, half * 512:(half + 1) * 512], start=(kc == 0), stop=(kc == NCH - 1)),
                                 reads=[oT_, wo], writes=[pp], inc=(kc == NCH - 1))
                    k.op("dve", lambda e, pp=pp, y_=y_, GT=GT: e.tensor_tensor(out=y_[:], in0=pp[:], in1=GT[:], op=ALU.mult), reads=[pp, GT], writes=[y_])
                    k.op("pool", lambda e, x_=x_, y_=y_: e.tensor_tensor(out=x_[:], in0=x_[:], in1=y_[:], op=ALU.add), reads=[x_, y_], writes=[x_])
                    k.dma("sp", lambda e, x_=x_, t=t: e.dma_start(out=X[rows(t), :], in_=x_[:]), reads=[x_], writes=[k.dbuf(("X", t))])

                n3A(0)
                for t in range(NT):
                    if t + 1 < NT:
                        n3A(t + 1)
                    n3B(t)
            k.barrier()

        for l in range(L):
            if l % 2 == 0 and "rwkv" in cfg.phases:
                rwkv_layer(l)
            if l % 2 == 1 and "na" in cfg.phases:
                na_layer(l)
            if "mlp" in cfg.phases:
                mlp_phase(l)

        with ExitStack() as st:
          if "mlp" not in cfg.phases:
              xb_ = [k.sb(st, "ycp%d" % i, [128, D], F32) for i in range(3)]
              for t in range(ROWS // 128):
                  b = xb_[t % 3]
                  k.dma("sp", lambda e, b=b, t=t: e.dma_start(out=b[:], in_=X[t * 128:(t + 1) * 128, :]), reads=[k.dbuf(("X", t))], writes=[b])
                  k.dma("sp", lambda e, b=b, t=t: e.dma_start(out=y_out[t * 128:(t + 1) * 128, :], in_=b[:]), reads=[b], writes=[k.dbuf(("Y", t))])
        k.finish()
        k.emit()
    return nc, k


def make_consts():
    c = -float(np.exp(-0.5))
    s_ = np.arange(128)[:, None]
    t_ = np.arange(128)[None, :]
    prec = [s_ < t_, s_ > t_]
    tric = np.stack([np.where(s_ <= t_, c, 0.0), np.where(s_ >= t_, c, 0.0)]).astype(np.float32)
    mask4 = np.stack([np.concatenate([p, p, p | (s_ == t_), p | (s_ == t_)], axis=1) for p in prec]).astype(np.float32)
    maskl = np.stack([p.T for p in prec]).astype(np.float32)
    kc = np.arange(64)[:, None]
    qc = np.arange(64)[None, :]
    ecol = np.stack([(kc - qc + 15 == co) for co in range(31)]).reshape(31, 4096).astype(np.float32)
    ws = np.clip(qc - 8, 0, 48)
    colmask = np.where((kc >= ws) & (kc < ws + 16), 0.0, NEG).reshape(1, 4096).astype(np.float32)
    j15 = np.eye(15, dtype=np.float32)[::-1].copy()
    return {"ident": np.eye(128, dtype=np.float32), "zeros": np.zeros((1, D), np.float32), "tric": tric,
            "allc": np.full((128, 128), c, np.float32), "mask4": mask4, "maskl": maskl,
            "ecol": ecol, "colmask": colmask, "j15": j15}


N_CORES = 8
_PHASES = ('rwkv', 'scan', 'na', 'mlp')


def kernel(x_prompt, x_sample, state_rwkv, cache_na_k, cache_na_v, c, c_ctx,
           norm_g, ada_w, ada_b, mlp_w1, mlp_w2,
           rwkv_mu, rwkv_w_rkv, rwkv_w_o, rwkv_w0, rwkv_w1, rwkv_w2, rwkv_a0, rwkv_a1, rwkv_a2,
           rwkv_g1, rwkv_g2, rwkv_k_k, rwkv_k_a, rwkv_r_k, rwkv_ln_w, rwkv_ln_b,
           na_w_qkv, na_w_o, na_q_g, na_k_g, na_rpb):
    f = lambda a: np.ascontiguousarray(np.asarray(a, dtype=np.float32))
    x_prompt, x_sample, state_rwkv, c, c_ctx = f(x_prompt), f(x_sample), f(state_rwkv), f(c), f(c_ctx)
    B, T, _ = x_prompt.shape
    BS, TS, _ = x_sample.shape
    L = norm_g.shape[0]
    n_p = B // N_CORES
    cfg = Cfg(tp=T, n_p=n_p, ts=TS, depth=L, phases=_PHASES)
    nc, k = build(cfg)
    NR, NN = cfg.n_rw, cfg.n_na
    shared = {
        "norm_g": f(norm_g), "ada_b": f(ada_b),
        "ada_w": f(ada_w).reshape(L * D, 6 * D), "mlp_w1": f(mlp_w1).reshape(L * D, DFF), "mlp_w2": f(mlp_w2).reshape(L * DFF, D),
        "rwkv_mu": f(rwkv_mu), "rwkv_w_rkv": f(rwkv_w_rkv).reshape(NR * 3 * D, D), "rwkv_w_o": f(rwkv_w_o).reshape(NR * D, D),
        "rwkv_w0": f(rwkv_w0), "rwkv_a0": f(rwkv_a0),
        "rwkv_w1": f(rwkv_w1).reshape(NR * 2 * D, 64), "rwkv_w2": f(rwkv_w2).reshape(NR * 2 * 64, D),
        "rwkv_a1": f(rwkv_a1).reshape(NR * 2 * D, 64), "rwkv_a2": f(rwkv_a2).reshape(NR * 2 * 64, D),
        "rwkv_g1": f(rwkv_g1).reshape(NR * D, 128), "rwkv_g2": f(rwkv_g2).reshape(NR * 128, D),
        "rwkv_k_k": f(rwkv_k_k), "rwkv_k_a": f(rwkv_k_a), "rwkv_r_k": f(rwkv_r_k).reshape(NR, D),
        "rwkv_ln_w": f(rwkv_ln_w), "rwkv_ln_b": f(rwkv_ln_b),
        "na_w_qkv": f(na_w_qkv).reshape(NN * D, 3 * D), "na_w_o": f(na_w_o).reshape(NN * D, D),
        "na_q_g": f(na_q_g), "na_k_g": f(na_k_g), "na_rpb": f(na_rpb),
    }
    cache_na_k, cache_na_v = f(cache_na_k), f(cache_na_v)
    shared.update(make_consts())
    in_maps = []
    for i in range(N_CORES):
        m = dict(shared)
        m["x_in"] = np.concatenate([x_prompt[i * n_p + j] for j in range(n_p)] + [x_sample[i]], axis=0)
        m["conds"] = np.stack([c_ctx, c[i]])
        m["state_in"] = state_rwkv[i]
        m["cache_k"] = cache_na_k[i]
        m["cache_v"] = cache_na_v[i]
        in_maps.append(m)
    res = run_bass_kernel_spmd(nc, in_maps, core_ids=list(range(N_CORES)))
    y_prompt = np.zeros((B, T, D), np.float32)
    y_sample = np.zeros((BS, TS, D), np.float32)
    new_state = np.zeros((B, NR, 2, H, 64, 64), np.float32)
    new_k = np.zeros((B, NN, H, T, 64), np.float32)
    new_v = np.zeros((B, NN, H, T, 64), np.float32)
    for i in range(N_CORES):
        r = res.results[i]
        y = r["y_out"]
        for j in range(n_p):
            y_prompt[i * n_p + j] = y[j * T:(j + 1) * T]
            new_state[i * n_p + j] = r["state_out"][j]
            new_k[i * n_p + j] = r["nk_out"][j]
            new_v[i * n_p + j] = r["nv_out"][j]
        y_sample[i] = y[n_p * T:]
    return (y_prompt, y_sample, new_state, new_k, new_v)
```
